# Optimizing a Trainium2 kernel written in Bass

```python
import math
import jax, jax.numpy as jnp
from jax import lax
import numpy as np

D_MODEL = 1024
BATCH = 4
SEQ = 8192
DEPTH = 2

HEAD_DIM = 64
ROPE_THETA = 10000.0
BLOCK = 128
EPS = 1e-6

A_HEADS = 4
A_VDIM = 2 * HEAD_DIM
A_WIDTH = A_HEADS * A_VDIM
A_QK = A_HEADS * 2 * HEAD_DIM
B_HEADS = 8
B_KV_HEADS = 2
B_WIDTH = B_HEADS * HEAD_DIM
B_KV = B_KV_HEADS * HEAD_DIM
WINDOW = 128
C_HEADS = 16
C_WIDTH = C_HEADS * HEAD_DIM

EVEN_SIZES = (A_QK, A_QK, A_WIDTH, A_WIDTH, B_WIDTH, B_KV, B_KV, B_WIDTH)
EVEN_IN = sum(EVEN_SIZES)
EVEN_MIX = A_WIDTH + B_WIDTH
ODD_IN = 4 * C_WIDTH
N_EVEN = (DEPTH + 1) // 2
N_ODD = DEPTH // 2

kernel_name = "hybrid_diffattn_swa_sink_stickbreaking_adaln"


def _split_points(sizes):
    pts, acc = [], 0
    for s in sizes[:-1]:
        acc += s
        pts.append(acc)
    return pts


def rms_norm(x, g):
    xf = x.astype(jnp.float32)
    y = xf * lax.rsqrt(jnp.mean(xf * xf, axis=-1, keepdims=True) + EPS)
    return (y * g.astype(jnp.float32)).astype(x.dtype)


def rope(x, pos):
    half = x.shape[-1] // 2
    inv = ROPE_THETA ** (-jnp.arange(half, dtype=jnp.float32) / half)
    ang = pos.astype(jnp.float32)[..., None] * inv
    cos = jnp.cos(ang)[:, :, None, :]
    sin = jnp.sin(ang)[:, :, None, :]
    xf = x.astype(jnp.float32)
    x1, x2 = xf[..., :half], xf[..., half:]
    out = jnp.concatenate([x1 * cos - x2 * sin, x2 * cos + x1 * sin], axis=-1)
    return out.astype(x.dtype)


def modulate(x, c, norm_g, w_mod, b_mod):
    mod = jax.nn.silu(c) @ w_mod + b_mod
    shift, scale, gate = jnp.split(mod, 3, axis=-1)
    h = rms_norm(x, norm_g) * (1 + scale[:, None, :]) + shift[:, None, :]
    return h, gate[:, None, :]


def diff_attention(q, k, v, lam):
    B, S, H, _, d = q.shape
    nb = S // BLOCK
    scale = d ** -0.5
    qb = q.astype(jnp.float32).reshape(B, nb, BLOCK, H, 2, d).transpose(1, 0, 2, 3, 4, 5)
    kf = k.astype(jnp.float32)
    vf = v.astype(jnp.float32)
    key_idx = jnp.arange(S)

    def block(args):
        qi, i = args
        s = jnp.einsum('bthcd,bshcd->bhcts', qi, kf) * scale
        q_idx = i * BLOCK + jnp.arange(BLOCK)
        mask = key_idx[None, :] <= q_idx[:, None]
        p = jax.nn.softmax(jnp.where(mask, s, -jnp.inf), axis=-1)
        w = p[:, :, 0] - lam * p[:, :, 1]
        return jnp.einsum('bhts,bshe->bthe', w, vf)

    out = lax.map(block, (qb, jnp.arange(nb)))
    return out.transpose(1, 0, 2, 3, 4).reshape(B, S, H, 2 * d).astype(v.dtype)


def sliding_window_sink_attention(q, k, v, sinks):
    B, S, Hq, d = q.shape
    Hkv = k.shape[2]
    G = Hq // Hkv
    nb = S // BLOCK
    scale = d ** -0.5
    qb = q.astype(jnp.float32).reshape(B, nb, BLOCK, Hkv, G, d)
    kb = k.astype(jnp.float32).reshape(B, nb, BLOCK, Hkv, d)
    vb = v.astype(jnp.float32).reshape(B, nb, BLOCK, Hkv, d)
    prev = lambda t: jnp.concatenate([jnp.zeros_like(t[:, :1]), t[:, :-1]], axis=1)
    kk = jnp.concatenate([prev(kb), kb], axis=2)
    vv = jnp.concatenate([prev(vb), vb], axis=2)
    s = jnp.einsum('bntkgd,bnskd->bnkgts', qb, kk) * scale
    t_rel = jnp.arange(BLOCK)[:, None] + BLOCK
    s_rel = jnp.arange(2 * BLOCK)[None, :]
    blk = jnp.arange(nb)[:, None, None]
    mask = (s_rel <= t_rel) & (t_rel - s_rel < WINDOW) & ((blk > 0) | (s_rel >= BLOCK))
    s = jnp.where(mask[None, :, None, None], s, -jnp.inf)
    sink = sinks.astype(jnp.float32).reshape(Hkv, G)[None, None, :, :, None, None]
    m = jnp.maximum(jnp.max(s, axis=-1, keepdims=True), sink)
    e = jnp.exp(s - m)
    p = e / (jnp.sum(e, axis=-1, keepdims=True) + jnp.exp(sink - m))
    out = jnp.einsum('bnkgts,bnskd->bntkgd', p, vv)
    return out.reshape(B, S, Hq, d).astype(v.dtype)


def stick_breaking_attention(q, k, v):
    B, S, H, d = q.shape
    nb = S // BLOCK
    scale = d ** -0.5
    qb = q.astype(jnp.float32).reshape(B, nb, BLOCK, H, d).transpose(1, 0, 2, 3, 4)
    kf = k.astype(jnp.float32)
    vf = v.astype(jnp.float32)
    key_idx = jnp.arange(S)

    def block(args):
        qi, i = args
        z = jnp.einsum('bthd,bshd->bhts', qi, kf) * scale
        q_idx = i * BLOCK + jnp.arange(BLOCK)
        strict = key_idx[None, :] < q_idx[:, None]
        log_1m = jnp.where(strict, jax.nn.log_sigmoid(-z), 0.0)
        suffix = lax.cumsum(log_1m, axis=3, reverse=True) - log_1m
        w = jnp.where(strict, jnp.exp(jax.nn.log_sigmoid(z) + suffix), 0.0)
        return jnp.einsum('bhts,bshd->bthd', w, vf)

    out = lax.map(block, (qb, jnp.arange(nb)))
    return out.transpose(1, 0, 2, 3, 4).reshape(B, S, H, d).astype(v.dtype)


def even_layer(x, c, positions, layer, norm_g, w_mod, b_mod, w_in, a_q_gain, a_k_gain,
               lq1, lk1, lq2, lk2, a_subln_g, b_q_gain, b_k_gain, b_sinks, w_out):
    B, S, _ = x.shape
    h, gate = modulate(x, c, norm_g, w_mod, b_mod)
    u = h @ w_in
    qa, ka, va, ga, qb, kb, vb, gb = jnp.split(u, _split_points(EVEN_SIZES), axis=-1)
    qa = rope(rms_norm(qa.reshape(B, S, 2 * A_HEADS, HEAD_DIM), a_q_gain), positions)
    ka = rope(rms_norm(ka.reshape(B, S, 2 * A_HEADS, HEAD_DIM), a_k_gain), positions)
    lambda_init = 0.8 - 0.6 * math.exp(-0.3 * layer)
    f32 = jnp.float32
    lam = (jnp.exp(jnp.sum(lq1.astype(f32) * lk1.astype(f32)))
           - jnp.exp(jnp.sum(lq2.astype(f32) * lk2.astype(f32))) + lambda_init)
    ya = diff_attention(qa.reshape(B, S, A_HEADS, 2, HEAD_DIM),
                        ka.reshape(B, S, A_HEADS, 2, HEAD_DIM),
                        va.reshape(B, S, A_HEADS, A_VDIM), lam)
    ya = (rms_norm(ya, a_subln_g) * (1 - lambda_init)).reshape(B, S, A_WIDTH)
    qb = rope(rms_norm(qb.reshape(B, S, B_HEADS, HEAD_DIM), b_q_gain), positions)
    kb = rope(rms_norm(kb.reshape(B, S, B_KV_HEADS, HEAD_DIM), b_k_gain), positions)
    yb = sliding_window_sink_attention(qb, kb, vb.reshape(B, S, B_KV_HEADS, HEAD_DIM), b_sinks)
    yb = yb.reshape(B, S, B_WIDTH)
    y = jnp.concatenate([ya * jax.nn.silu(ga), yb * jax.nn.silu(gb)], axis=-1) @ w_out
    return x + gate * y


def odd_layer(x, c, norm_g, w_mod, b_mod, w_in, w_out):
    B, S, _ = x.shape
    h, gate = modulate(x, c, norm_g, w_mod, b_mod)
    q, k, v, g = jnp.split(h @ w_in, 4, axis=-1)
    shp = (B, S, C_HEADS, HEAD_DIM)
    y = stick_breaking_attention(q.reshape(shp), k.reshape(shp), v.reshape(shp)).reshape(B, S, C_WIDTH)
    y = (y * jax.nn.silu(g)) @ w_out
    return x + gate * y


def setup_inputs(seed: int = 0) -> dict:
    key = jax.random.key(seed)
    ks = iter(jax.random.split(key, 32))
    nrm = lambda shape, s: jax.random.normal(next(ks), shape, jnp.float32) * s
    D = D_MODEL
    return {
        "x": nrm((BATCH, SEQ, D), 1.0),
        "c": nrm((BATCH, D), 1.0),
        "positions": jnp.broadcast_to(jnp.arange(SEQ, dtype=jnp.int32), (BATCH, SEQ)),
        "even_norm_g": 1.0 + nrm((N_EVEN, D), 0.05),
        "even_w_mod": nrm((N_EVEN, D, 3 * D), 0.2 * D ** -0.5),
        "even_b_mod": nrm((N_EVEN, 3 * D), 0.02),
        "even_w_in": nrm((N_EVEN, D, EVEN_IN), D ** -0.5),
        "a_q_gain": 1.0 + nrm((N_EVEN, HEAD_DIM), 0.05),
        "a_k_gain": 1.0 + nrm((N_EVEN, HEAD_DIM), 0.05),
        "a_lambda_q1": nrm((N_EVEN, HEAD_DIM), 0.1),
        "a_lambda_k1": nrm((N_EVEN, HEAD_DIM), 0.1),
        "a_lambda_q2": nrm((N_EVEN, HEAD_DIM), 0.1),
        "a_lambda_k2": nrm((N_EVEN, HEAD_DIM), 0.1),
        "a_subln_g": 1.0 + nrm((N_EVEN, A_VDIM), 0.05),
        "b_q_gain": 1.0 + nrm((N_EVEN, HEAD_DIM), 0.05),
        "b_k_gain": 1.0 + nrm((N_EVEN, HEAD_DIM), 0.05),
        "b_sinks": nrm((N_EVEN, B_HEADS), 0.5),
        "even_w_out": nrm((N_EVEN, EVEN_MIX, D), EVEN_MIX ** -0.5),
        "odd_norm_g": 1.0 + nrm((N_ODD, D), 0.05),
        "odd_w_mod": nrm((N_ODD, D, 3 * D), 0.2 * D ** -0.5),
        "odd_b_mod": nrm((N_ODD, 3 * D), 0.02),
        "odd_w_in": nrm((N_ODD, D, ODD_IN), D ** -0.5),
        "odd_w_out": nrm((N_ODD, C_WIDTH, D), C_WIDTH ** -0.5),
    }


def reference(x, c, positions, even_norm_g, even_w_mod, even_b_mod, even_w_in, a_q_gain, a_k_gain,
              a_lambda_q1, a_lambda_k1, a_lambda_q2, a_lambda_k2, a_subln_g, b_q_gain, b_k_gain,
              b_sinks, even_w_out, odd_norm_g, odd_w_mod, odd_b_mod, odd_w_in, odd_w_out):
    for layer in range(DEPTH):
        j = layer // 2
        if layer % 2 == 0:
            x = even_layer(x, c, positions, layer, even_norm_g[j], even_w_mod[j], even_b_mod[j],
                           even_w_in[j], a_q_gain[j], a_k_gain[j], a_lambda_q1[j], a_lambda_k1[j],
                           a_lambda_q2[j], a_lambda_k2[j], a_subln_g[j], b_q_gain[j], b_k_gain[j],
                           b_sinks[j], even_w_out[j])
        else:
            x = odd_layer(x, c, odd_norm_g[j], odd_w_mod[j], odd_b_mod[j], odd_w_in[j], odd_w_out[j])
    return x
```

```python
import math
from contextlib import ExitStack
import numpy as np
import concourse.bass as bass
import concourse.mybir as mybir
from concourse.bass_utils import run_bass_kernel_spmd

F32 = mybir.dt.float32
BF16 = mybir.dt.bfloat16
I32 = mybir.dt.int32
AF = mybir.ActivationFunctionType
ALU = mybir.AluOpType
AX = mybir.AxisListType

D = 1024
EPS = 1e-6
PI = math.pi


class Res:
    __slots__ = ("name", "lw", "rd")

    def __init__(self, name=""):
        self.name = name
        self.lw = None
        self.rd = {}


class SemObj:
    __slots__ = ("sem", "count", "name")

    def __init__(self, sem, name):
        self.sem = sem
        self.count = 0
        self.name = name


class KB:
    def __init__(self, nc, stack):
        self.nc = nc
        self.stack = stack
        self.engs = {"pe": nc.tensor, "act": nc.scalar, "dve": nc.vector, "pool": nc.gpsimd, "sp": nc.sync}
        self.so = {}
        for k in self.engs:
            s = stack.enter_context(nc.semaphore("prog_" + k))
            self.so[k] = SemObj(s, k)
        self.waited = {k: {} for k in self.engs}
        self.n_inst = 0
        self.all_so = list(self.so.values())
        self.free_dma = {}

    def new_dma_sem(self, name, q="sp"):
        if self.free_dma.setdefault(q, []):
            return self.free_dma[q].pop()
        s = self.stack.enter_context(self.nc.semaphore("dma_" + name))
        so = SemObj(s, name)
        self.all_so.append(so)
        return so

    def barrier(self):
        for e in self.engs:
            self._wait(e, [(so, so.count) for so in self.all_so if so is not self.so[e]])

    def _wait(self, e, deps):
        w = self.waited[e]
        for so, val in deps:
            if val <= 0 or w.get(so, 0) >= val:
                continue
            self.engs[e].wait_ge(so.sem, val)
            w[so] = val

    def _deps(self, e, reads, writes, same_engine):
        me = self.so[e]
        deps = []
        for r in reads:
            if r.lw is not None and (same_engine or r.lw[0] is not me):
                deps.append(r.lw)
        for r in writes:
            if r.lw is not None and (same_engine or r.lw[0] is not me):
                deps.append(r.lw)
            for so, v in r.rd.items():
                if same_engine or so is not me:
                    deps.append((so, v))
        return deps

    def op(self, e, fn, reads=(), writes=(), signal=True, same_engine=True):
        me = self.so[e]
        self._wait(e, self._deps(e, reads, writes, same_engine))
        ins = fn()
        self.n_inst += 1
        if signal:
            me.count += 1
            ins.then_inc(me.sem, 1)
            ev = (me, me.count)
        else:
            ev = (me, me.count + 1)
        for r in reads:
            if r.rd.get(me, 0) < ev[1]:
                r.rd[me] = ev[1]
        for r in writes:
            r.lw = ev
            r.rd = {}
        return ins

    def dma(self, q, out_ap, in_ap, dsem, reads=(), writes=(), **kw):
        if isinstance(dsem, Buf):
            dsem = dsem.get_ds(q)
        self._wait(q, self._deps(q, reads, writes, True))
        ins = self.engs[q].dma_start(out=out_ap, in_=in_ap, **kw)
        self.n_inst += 1
        dsem.count += 16
        ins.then_inc(dsem.sem, 16)
        ev = (dsem, dsem.count)
        for r in reads:
            if r.rd.get(dsem, 0) < ev[1]:
                r.rd[dsem] = ev[1]
        for r in writes:
            r.lw = ev
            r.rd = {}
        return ins


class Buf:
    def __init__(self, t, name, k):
        self.t = t
        self.r = Res(name)
        self.name = name
        self._k = k
        self._ds = {}

    @property
    def ds(self):
        return self

    def get_ds(self, q):
        if q not in self._ds:
            self._ds[q] = self._k.new_dma_sem(self.name + q, q)
        return self._ds[q]


def build(S=8192, layers=(0, 1), dbg=False):
    NB = S // 128
    NT = S // 512
    SO = S // 2
    NTO = SO // 512
    nc = bass.Bass("TRN2", target_bir_lowering=False)
    dram_in = lambda n, sh, dt=F32: nc.dram_tensor(n, sh, dt, kind="ExternalInput").ap()
    dram_out = lambda n, sh, dt=F32: nc.dram_tensor(n, sh, dt, kind="ExternalOutput").ap()
    dram_scr = lambda n, sh, dt: nc.dram_tensor(n, sh, dt, kind=("ExternalOutput" if dbg else "Internal")).ap()

    L0 = 0 in layers
    L1 = 1 in layers
    cT = dram_in("cT", [128, 8])
    consts = dram_in("consts", [128, 128 * 4 + 32])
    if L0:
        x_in = dram_in("x", [S, D])
        posT = dram_in("posT", [128, NB], I32)
        e_norm_g = dram_in("e_norm_g", [1, D]); e_w_mod = dram_in("e_w_mod", [D, 3 * D]); e_b_mod = dram_in("e_b_mod", [1, 3 * D])
        e_w_in = dram_in("e_w_in", [D, 3328]); e_w_out = dram_in("e_w_out", [D, D])
        gains = dram_in("gains", [1, 4 * 64])
        lamv = dram_in("lamv", [1, 4 * 64])
        subg = dram_in("subg", [1, 128])
        sinks = dram_in("sinks", [1, 8])
    if L1:
        o_norm_g = dram_in("o_norm_g", [1, D]); o_w_mod = dram_in("o_w_mod", [D, 3 * D]); o_b_mod = dram_in("o_b_mod", [1, 3 * D])
        o_w_in = dram_in("o_w_in", [D, 4 * D]); o_w_out = dram_in("o_w_out", [D, D])
        ownm = dram_in("ownm", [128, 2])
        dmask = dram_in("dmask", [128, 8 * 512])
        out_o = dram_out("out", [SO, D])
    if L0 and L1:
        x1 = dram_scr("x1", [S, D], F32)
    elif L0:
        x1 = dram_out("x1", [S, D])
    else:
        x1 = dram_in("x1", [S, D])

    with ExitStack() as st:
        k = KB(nc, st)

        cur = [st]
        bufs_of = {id(st): []}

        uid = [0]

        def sb(name, shape, dt):
            uid[0] += 1
            name = f"{name}_{uid[0]}"
            bf_ = Buf(cur[0].enter_context(nc.sbuf_tensor(name, shape, dt)), name, k)
            bufs_of[id(cur[0])].append(bf_)
            return bf_

        def slots(name, shape, dt, n):
            return [sb(f"{name}{i}", shape, dt) for i in range(n)]

        class Phase:
            def __enter__(self_p):
                self_p.prev = cur[0]
                self_p.stk = ExitStack()
                cur[0] = self_p.stk
                bufs_of[id(self_p.stk)] = []
                return self_p

            def __exit__(self_p, *a):
                k.barrier()
                for bf_ in bufs_of.pop(id(self_p.stk)):
                    for q_, so_ in bf_._ds.items():
                        k.free_dma.setdefault(q_, []).append(so_)
                    bf_._ds = {}
                self_p.stk.close()
                cur[0] = self_p.prev
                return False

        PSF = st.enter_context(nc.psum_tensor("psf", [128, 7, 512], F32))
        PST = st.enter_context(nc.psum_tensor("pst", [128, 1024], BF16))
        bank = [Res(f"bank{i}") for i in range(7)]
        bankT = Res("bankT")

        cst32 = sb("cst32", [128, 128 * 4 + 32], F32)
        ident = sb("ident", [128, 128], BF16)
        tri = sb("tri", [128, 128], BF16)
        upp = sb("upp", [128, 128], BF16)
        negtri = sb("negtri", [128, 128], BF16)
        negrest = sb("negrest", [128, 128], BF16)
        k.dma("sp", cst32.t[:], consts[:, :], cst32.ds, writes=[cst32.r])
        for i, tdst in enumerate((ident, tri, upp, negtri)):
            k.op("dve", lambda tdst=tdst, i=i: nc.vector.tensor_copy(out=tdst.t[:], in_=cst32.t[:, i * 128:(i + 1) * 128]),
                 reads=[cst32.r], writes=[tdst.r])
        k.op("dve", lambda: nc.vector.tensor_scalar(out=negrest.t[:], in0=cst32.t[:, 384:512], scalar1=-1.0, scalar2=-1.0,
                                                   op0=ALU.mult, op1=ALU.add), reads=[cst32.r], writes=[negrest.r])
        invf = cst32.t[:, 512:544]

        sc = sb("sc", [128, 8], F32)
        sce = sb("sce", [128, 8], F32)
        k.dma("sp", sc.t[:], cT[:, :], sc.ds, writes=[sc.r])
        k.op("act", lambda: nc.scalar.activation(out=sce.t[:], in_=sc.t[:], func=AF.Exp, scale=-1.0), reads=[sc.r], writes=[sce.r])
        k.op("dve", lambda: nc.vector.tensor_scalar_add(out=sce.t[:], in0=sce.t[:], scalar1=1.0), reads=[sce.r], writes=[sce.r])
        k.op("dve", lambda: nc.vector.reciprocal(out=sce.t[:], in_=sce.t[:]), reads=[sce.r], writes=[sce.r])
        k.op("dve", lambda: nc.vector.tensor_tensor(out=sc.t[:], in0=sc.t[:], in1=sce.t[:], op=ALU.mult), reads=[sc.r, sce.r], writes=[sc.r])
        screp = sb("screp", [128, 8, 128], F32)
        k.op("dve", lambda: nc.vector.tensor_copy(out=screp.t[:], in_=sc.t[:].unsqueeze(2).to_broadcast([128, 8, 128])),
             reads=[sc.r], writes=[screp.r])

        def mod_vectors(tag, w_mod, b_mod, norm_g):
            shift = sb("shift" + tag, [128, D], F32)
            gs = sb("gs" + tag, [128, D], F32)
            gate = sb("gate" + tag, [128, D], F32)
            with Phase():
                _mod_vectors(w_mod, b_mod, norm_g, shift, gs, gate)
            return shift, gs, gate

        def _mod_vectors(w_mod, b_mod, norm_g, shift, gs, gate):
            wm = slots("wm", [128, 8, 512], F32, 2)
            bmod = sb("bmod", [128, 3 * D], F32)
            gbc = sb("gbc", [128, D], F32)
            k.dma("sp", bmod.t[:], b_mod.partition_broadcast(128), bmod.ds, writes=[bmod.r])
            k.dma("sp", gbc.t[:], norm_g.partition_broadcast(128), gbc.ds, writes=[gbc.r])
            wv = w_mod.rearrange("(kc p) n -> p kc n", p=128)
            for cc in range(6):
                w = wm[cc % 2]
                k.dma("sp", w.t[:], wv[:, :, cc * 512:(cc + 1) * 512], w.ds, writes=[w.r])
                b = cc % 2
                for kc in range(8):
                    k.op("pe", lambda kc=kc, b=b, w=w: nc.tensor.matmul(PSF[:, b, :], lhsT=screp.t[:, kc, :], rhs=w.t[:, kc, :],
                                                                        start=(kc == 0), stop=(kc == 7)),
                         reads=[screp.r, w.r], writes=[bank[b]], signal=(kc == 7), same_engine=False)
                seg, off = cc // 2, (cc % 2) * 512
                bsl = bmod.t[:, cc * 512:(cc + 1) * 512]
                if seg == 0:
                    k.op("dve", lambda b=b, off=off, bsl=bsl: nc.vector.tensor_tensor(out=shift.t[:, off:off + 512], in0=PSF[:, b, :], in1=bsl, op=ALU.add),
                         reads=[bank[b], bmod.r], writes=[shift.r])
                elif seg == 1:
                    k.op("dve", lambda b=b, off=off, bsl=bsl: nc.vector.scalar_tensor_tensor(out=gs.t[:, off:off + 512], in0=PSF[:, b, :], scalar=1.0, in1=bsl,
                                                                                            op0=ALU.add, op1=ALU.add),
                         reads=[bank[b], bmod.r], writes=[gs.r])
                    k.op("dve", lambda off=off: nc.vector.tensor_tensor(out=gs.t[:, off:off + 512], in0=gs.t[:, off:off + 512], in1=gbc.t[:, off:off + 512], op=ALU.mult),
                         reads=[gs.r, gbc.r], writes=[gs.r])
                else:
                    k.op("dve", lambda b=b, off=off, bsl=bsl: nc.vector.tensor_tensor(out=gate.t[:, off:off + 512], in0=PSF[:, b, :], in1=bsl, op=ALU.add),
                         reads=[bank[b], bmod.r], writes=[gate.r])
            return shift, gs, gate

        xt = slots("xt", [128, D], F32, 3)
        junk = sb("junk", [128, D], F32)
        ssq = sb("ssq", [128, 1], F32)
        rstd = sb("rstd", [128, 1], F32)
        htm = slots("htm", [128, D], BF16, 2)

        def norm_mod(xb, hb, gs, shift):
            k.op("act", lambda: nc.scalar.activation(out=junk.t[:], in_=xb.t[:], func=AF.Square, accum_out=ssq.t[:]),
                 reads=[xb.r], writes=[junk.r, ssq.r])
            k.op("act", lambda: nc.scalar.activation(out=rstd.t[:], in_=ssq.t[:], func=AF.Ln, scale=1.0 / D, bias=EPS), reads=[ssq.r], writes=[rstd.r])
            k.op("act", lambda: nc.scalar.activation(out=rstd.t[:], in_=rstd.t[:], func=AF.Exp, scale=-0.5), reads=[rstd.r], writes=[rstd.r])
            k.op("dve", lambda: nc.vector.scalar_tensor_tensor(out=junk.t[:], in0=xb.t[:], scalar=rstd.t[:, 0:1], in1=gs.t[:], op0=ALU.mult, op1=ALU.mult),
                 reads=[xb.r, rstd.r, gs.r, junk.r], writes=[junk.r])
            k.op("dve", lambda: nc.vector.tensor_tensor(out=hb.t[:], in0=junk.t[:], in1=shift.t[:], op=ALU.add),
                 reads=[junk.r, shift.r], writes=[hb.r])

        def transpose8(src_ap_fn, src_r, dst_ap, dst_r, eng="act"):
            for kc in range(8):
                k.op("pe", lambda kc=kc: nc.tensor.transpose(out=PST[:, kc * 128:(kc + 1) * 128], in_=src_ap_fn(kc), identity=ident.t[:]),
                     reads=[src_r, ident.r], writes=[bankT], signal=(kc == 7), same_engine=False)
            src3 = PST[:, :].rearrange("p (a b) -> p a b", a=8)
            if eng == "act":
                k.op("act", lambda: nc.scalar.copy(out=dst_ap, in_=src3), reads=[bankT], writes=[dst_r])
            else:
                k.op("dve", lambda: nc.vector.tensor_copy(out=dst_ap, in_=src3), reads=[bankT], writes=[dst_r])

        if L0:
            sh0, gs0, gate0 = mod_vectors("0", e_w_mod, e_b_mod, e_norm_g)
            QAT = dram_scr("QAT", [4, 128, S], BF16)
            KAT = dram_scr("KAT", [4, 128, S], BF16)
            VA = dram_scr("VA", [S, 512], BF16)
            SGA = dram_scr("SGA", [S, 512], BF16)
            QBT = dram_scr("QBT", [4, 128, S], BF16)
            KBT = dram_scr("KBT", [128, S], BF16)
            VB = dram_scr("VB", [S, 128], BF16)
            SGB = dram_scr("SGB", [S, 512], BF16)
            Y0 = dram_scr("Y0", [S, D], BF16)

            lvl = int(str(dbg)[3:]) if str(dbg).startswith("b0x") else 99
            with (Phase() if lvl == 99 else ExitStack()):
              if lvl == 99:
                  posi = sb("posi", [128, NB], I32)
                  posf = sb("posf", [128, NB], F32)
                  ang = sb("ang", [128, NB, 32], F32)
                  cosT = sb("cosT", [128, NB, 32], F32)
                  sinT = sb("sinT", [128, NB, 32], F32)
                  k.dma("sp", posi.t[:], posT[:, :], posi.ds, writes=[posi.r])
                  k.op("dve", lambda: nc.vector.tensor_copy(out=posf.t[:], in_=posi.t[:]), reads=[posi.r], writes=[posf.r])
                  k.op("dve", lambda: nc.vector.tensor_tensor(out=ang.t[:], in0=posf.t[:].unsqueeze(2).to_broadcast([128, NB, 32]),
                                                             in1=invf.unsqueeze(1).to_broadcast([128, NB, 32]), op=ALU.mult),
                       reads=[posf.r, cst32.r], writes=[ang.r])
                  negpi = sb("negpi", [128, 1], F32)
                  k.op("dve", lambda: nc.vector.memset(negpi.t[:], -PI), writes=[negpi.r])
                  ui = sb("ui", [128, NB, 32], I32)
                  uf = sb("uf", [128, NB, 32], F32)
                  for tbl, c0 in ((sinT, 0.5), (cosT, 0.75)):
                      k.op("dve", lambda tbl=tbl, c0=c0: nc.vector.tensor_scalar(out=tbl.t[:], in0=ang.t[:], scalar1=1.0 / (2 * PI), scalar2=c0, op0=ALU.mult, op1=ALU.add),
                           reads=[ang.r], writes=[tbl.r])
                      k.op("dve", lambda tbl=tbl: nc.vector.tensor_copy(out=ui.t[:], in_=tbl.t[:]), reads=[tbl.r, ui.r], writes=[ui.r])
                      k.op("dve", lambda: nc.vector.tensor_copy(out=uf.t[:], in_=ui.t[:]), reads=[ui.r, uf.r], writes=[uf.r])
                      k.op("dve", lambda tbl=tbl: nc.vector.tensor_tensor(out=tbl.t[:], in0=tbl.t[:], in1=uf.t[:], op=ALU.subtract), reads=[tbl.r, uf.r], writes=[tbl.r])
                      k.op("dve", lambda tbl=tbl: nc.vector.tensor_single_scalar(out=uf.t[:], in_=tbl.t[:], scalar=0.0, op=ALU.is_lt), reads=[tbl.r, uf.r], writes=[uf.r])
                      k.op("dve", lambda tbl=tbl: nc.vector.tensor_tensor(out=tbl.t[:], in0=tbl.t[:], in1=uf.t[:], op=ALU.add), reads=[tbl.r, uf.r], writes=[tbl.r])
                      k.op("act", lambda tbl=tbl: nc.scalar.activation(out=tbl.t[:], in_=tbl.t[:], func=AF.Sin, bias=negpi.t[:, 0:1], scale=2 * PI),
                           reads=[tbl.r, negpi.r], writes=[tbl.r])
                  gn = sb("gn", [128, 4 * 64], F32)
                  k.dma("sp", gn.t[:], gains.partition_broadcast(128), gn.ds, writes=[gn.r])
                  w0 = sb("w0", [128, 8, 3328], BF16)
                  w0v = e_w_in.rearrange("(kc p) n -> p kc n", p=128)
                  for kc in range(8):
                      k.dma("pool", w0.t[:, kc, :], w0v[:, kc, :], w0.ds, writes=[w0.r])

                  hT = slots("hT", [128, 8, 128], BF16, 2)
                  st_qa = slots("st_qa", [128, 4, 512], BF16, 2)
                  st_ka = slots("st_ka", [128, 4, 512], BF16, 2)
                  st_qb = slots("st_qb", [128, 4, 512], BF16, 2)
                  st_kb = slots("st_kb", [128, 512], BF16, 2)
                  st_va = slots("st_va", [128, 4, 512], BF16, 2)
                  st_vb = slots("st_vb", [128, 4, 128], BF16, 2)
                  st_ga = slots("st_ga", [128, 4, 512], BF16, 2)
                  st_gb = slots("st_gb", [128, 4, 512], BF16, 2)
                  sq = sb("sq", [128, 512], F32)
                  ss8 = sb("ss8", [128, 8], F32)
                  rs8 = sb("rs8", [128, 8], F32)
                  tq = sb("tq", [128, 512], F32)
                  ra = sb("ra", [128, 8, 32], F32); rb = sb("rb", [128, 8, 32], F32)
                  qn = slots("qn", [128, 512], BF16, 2)
                  ge = sb("ge", [128, 512], F32)

                  def normrope(b, nh, gain_ap, blk, dst, dst_r):
                      W = nh * 64
                      k.op("act", lambda: nc.scalar.activation(out=sq.t[:, 0:W], in_=PSF[:, b, 0:W], func=AF.Square),
                           reads=[bank[b]], writes=[sq.r])
                      k.op("dve", lambda: nc.vector.tensor_reduce(out=ss8.t[:, 0:nh], in_=sq.t[:, 0:W].rearrange("p (h d) -> p h d", d=64), axis=AX.X, op=ALU.add),
                           reads=[sq.r], writes=[ss8.r])
                      k.op("act", lambda: nc.scalar.activation(out=rs8.t[:, 0:nh], in_=ss8.t[:, 0:nh], func=AF.Ln, scale=1.0 / 64, bias=EPS), reads=[ss8.r], writes=[rs8.r])
                      k.op("act", lambda: nc.scalar.activation(out=rs8.t[:, 0:nh], in_=rs8.t[:, 0:nh], func=AF.Exp, scale=-0.5), reads=[rs8.r], writes=[rs8.r])
                      t3 = tq.t[:, 0:W].rearrange("p (h d) -> p h d", d=64)
                      k.op("dve", lambda: nc.vector.tensor_tensor(out=t3, in0=PSF[:, b, 0:W].rearrange("p (h d) -> p h d", d=64),
                                                                 in1=rs8.t[:, 0:nh].unsqueeze(2).to_broadcast([128, nh, 64]), op=ALU.mult),
                           reads=[bank[b], rs8.r], writes=[tq.r])
                      k.op("dve", lambda: nc.vector.tensor_tensor(out=t3, in0=t3, in1=gain_ap.unsqueeze(1).to_broadcast([128, nh, 64]), op=ALU.mult),
                           reads=[tq.r, gn.r], writes=[tq.r])
                      cb = cosT.t[:, blk, :].unsqueeze(1).to_broadcast([128, nh, 32])
                      sbn = sinT.t[:, blk, :].unsqueeze(1).to_broadcast([128, nh, 32])
                      x1v, x2v = t3[:, :, 0:32], t3[:, :, 32:64]
                      d3 = dst.rearrange("p (h d) -> p h d", d=64)
                      k.op("dve", lambda: nc.vector.tensor_tensor(out=ra.t[:, 0:nh, :], in0=x1v, in1=cb, op=ALU.mult), reads=[tq.r, cosT.r], writes=[ra.r])
                      k.op("dve", lambda: nc.vector.tensor_tensor(out=rb.t[:, 0:nh, :], in0=x2v, in1=sbn, op=ALU.mult), reads=[tq.r, sinT.r], writes=[rb.r])
                      k.op("dve", lambda: nc.vector.tensor_tensor(out=d3[:, :, 0:32], in0=ra.t[:, 0:nh, :], in1=rb.t[:, 0:nh, :], op=ALU.subtract),
                           reads=[ra.r, rb.r], writes=[dst_r])
                      k.op("dve", lambda: nc.vector.tensor_tensor(out=ra.t[:, 0:nh, :], in0=x2v, in1=cb, op=ALU.mult), reads=[tq.r, cosT.r, ra.r], writes=[ra.r])
                      k.op("dve", lambda: nc.vector.tensor_tensor(out=rb.t[:, 0:nh, :], in0=x1v, in1=sbn, op=ALU.mult), reads=[tq.r, sinT.r, rb.r], writes=[rb.r])
                      k.op("dve", lambda: nc.vector.tensor_tensor(out=d3[:, :, 32:64], in0=ra.t[:, 0:nh, :], in1=rb.t[:, 0:nh, :], op=ALU.add),
                           reads=[ra.r, rb.r], writes=[dst_r])

                  def silu_to(b, dst_ap, dst_r):
                      k.op("act", lambda: nc.scalar.activation(out=ge.t[:], in_=PSF[:, b, :], func=AF.Exp, scale=-1.0), reads=[bank[b]], writes=[ge.r])
                      k.op("dve", lambda: nc.vector.tensor_scalar_add(out=ge.t[:], in0=ge.t[:], scalar1=1.0), reads=[ge.r], writes=[ge.r])
                      k.op("dve", lambda: nc.vector.reciprocal(out=ge.t[:], in_=ge.t[:]), reads=[ge.r], writes=[ge.r])
                      k.op("dve", lambda: nc.vector.tensor_tensor(out=dst_ap, in0=ge.t[:], in1=PSF[:, b, :], op=ALU.mult), reads=[ge.r, bank[b]], writes=[dst_r])

                  xv = x_in.rearrange("(n p) d -> n p d", p=128)
                  k.dma("sp", xt[0].t[:], xv[0], xt[0].ds, writes=[xt[0].r])
                  bsel = 0
                  for T in range(NT):
                      s_ = T % 2
                      for j in range(4):
                          blk = T * 4 + j
                          if blk + 1 < NB:
                              nx = xt[(blk + 1) % 3]
                              k.dma("sp", nx.t[:], xv[blk + 1], nx.ds, writes=[nx.r])
                          xb = xt[blk % 3]
                          hb = htm[blk % 2]
                          norm_mod(xb, hb, gs0, sh0)
                          hTb = hT[blk % 2]
                          transpose8(lambda kc, hb=hb: hb.t[:, kc * 128:(kc + 1) * 128], hb.r, hTb.t[:], hTb.r, eng="act")
                          for ch in range(7):
                              b = bsel
                              bsel = (bsel + 1) % 3
                              c0 = ch * 512
                              W = 512 if ch < 6 else 256
                              for kc in range(8):
                                  k.op("pe", lambda kc=kc, b=b, c0=c0, W=W, hTb=hTb: nc.tensor.matmul(PSF[:, b, 0:W], lhsT=hTb.t[:, kc, :], rhs=w0.t[:, kc, c0:c0 + W],
                                                                                                      start=(kc == 0), stop=(kc == 7)),
                                       reads=[hTb.r, w0.r], writes=[bank[b]], signal=(kc == 7), same_engine=False)
                              if ch in (0, 1, 4):
                                  q = qn[ch % 2]
                                  gain_ap = {0: gn.t[:, 0:64], 1: gn.t[:, 64:128], 4: gn.t[:, 128:192]}[ch]
                                  normrope(b, 8, gain_ap, blk, q.t[:], q.r)
                                  stg = {0: st_qa, 1: st_ka, 4: st_qb}[ch][s_]
                                  for cc in range(4):
                                      k.op("pe", lambda cc=cc, q=q: nc.tensor.transpose(out=PST[:, cc * 128:(cc + 1) * 128], in_=q.t[:, cc * 128:(cc + 1) * 128], identity=ident.t[:]),
                                           reads=[q.r, ident.r], writes=[bankT], signal=(cc == 3), same_engine=False)
                                  k.op("act", lambda stg=stg, j=j: nc.scalar.copy(out=stg.t[:, :, j * 128:(j + 1) * 128], in_=PST[:, 0:512].rearrange("p (c t) -> p c t", c=4)),
                                       reads=[bankT], writes=[stg.r])
                              elif ch == 2:
                                  stg = st_va[s_]
                                  k.op("act", lambda stg=stg, j=j, b=b: nc.scalar.copy(out=stg.t[:, j, :], in_=PSF[:, b, :]), reads=[bank[b]], writes=[stg.r])
                              elif ch in (3, 5):
                                  stg = (st_ga if ch == 3 else st_gb)[s_]
                                  silu_to(b, stg.t[:, j, :], stg.r)
                              else:
                                  q = qn[0]
                                  normrope(b, 2, gn.t[:, 192:256], blk, q.t[:, 0:128], q.r)
                                  k.op("pe", lambda q=q: nc.tensor.transpose(out=PST[:, 0:128], in_=q.t[:, 0:128], identity=ident.t[:]),
                                       reads=[q.r, ident.r], writes=[bankT], same_engine=False)
                                  stg = st_kb[s_]
                                  k.op("act", lambda stg=stg, j=j: nc.scalar.copy(out=stg.t[:, j * 128:(j + 1) * 128], in_=PST[:, 0:128]), reads=[bankT], writes=[stg.r])
                                  stg2 = st_vb[s_]
                                  k.op("act", lambda stg2=stg2, j=j, b=b: nc.scalar.copy(out=stg2.t[:, j, :], in_=PSF[:, b, 128:256]), reads=[bank[b]], writes=[stg2.r])
                      t0 = T * 512
                      for hh_ in range(4):
                          k.dma("pool", QAT[hh_, :, t0:t0 + 512], st_qa[s_].t[:, hh_, :], st_qa[s_].ds, reads=[st_qa[s_].r])
                          k.dma("pool", KAT[hh_, :, t0:t0 + 512], st_ka[s_].t[:, hh_, :], st_ka[s_].ds, reads=[st_ka[s_].r])
                          k.dma("pool", QBT[hh_, :, t0:t0 + 512], st_qb[s_].t[:, hh_, :], st_qb[s_].ds, reads=[st_qb[s_].r])
                      k.dma("pool", KBT[:, t0:t0 + 512], st_kb[s_].t[:], st_kb[s_].ds, reads=[st_kb[s_].r])
                      k.dma("pool", VA[t0:t0 + 512, :].rearrange("(j p) c -> p j c", p=128), st_va[s_].t[:], st_va[s_].ds, reads=[st_va[s_].r])
                      k.dma("pool", VB[t0:t0 + 512, :].rearrange("(j p) c -> p j c", p=128), st_vb[s_].t[:], st_vb[s_].ds, reads=[st_vb[s_].r])
                      k.dma("pool", SGA[t0:t0 + 512, :].rearrange("(j p) c -> p j c", p=128), st_ga[s_].t[:], st_ga[s_].ds, reads=[st_ga[s_].r])
                      k.dma("pool", SGB[t0:t0 + 512, :].rearrange("(j p) c -> p j c", p=128), st_gb[s_].t[:], st_gb[s_].ds, reads=[st_gb[s_].r])

        if dbg == "p0":
            k.barrier()
            return nc, k

        if L0:
            lvl = int(str(dbg)[3:]) if str(dbg).startswith("b0x") else 99
            with (Phase() if lvl == 99 else ExitStack()):
              if lvl == 99:
                  lv = sb("lv", [128, 256], F32)
                  lp = sb("lp", [128, 128], F32)
                  l2 = sb("l2", [128, 2], F32)
                  nlam = sb("nlam", [128, 1], F32)
                  k.dma("sp", lv.t[:], lamv.partition_broadcast(128), lv.ds, writes=[lv.r])
                  lv4 = lv.t[:].rearrange("p (a d) -> p a d", d=64)
                  k.op("dve", lambda: nc.vector.tensor_tensor(out=lp.t[:, 0:64], in0=lv4[:, 0, :], in1=lv4[:, 1, :], op=ALU.mult), reads=[lv.r], writes=[lp.r])
                  k.op("dve", lambda: nc.vector.tensor_tensor(out=lp.t[:, 64:128], in0=lv4[:, 2, :], in1=lv4[:, 3, :], op=ALU.mult), reads=[lv.r, lp.r], writes=[lp.r])
                  k.op("dve", lambda: nc.vector.tensor_reduce(out=l2.t[:], in_=lp.t[:].rearrange("p (a d) -> p a d", d=64), axis=AX.X, op=ALU.add), reads=[lp.r], writes=[l2.r])
                  k.op("act", lambda: nc.scalar.activation(out=l2.t[:], in_=l2.t[:], func=AF.Exp), reads=[l2.r], writes=[l2.r])
                  k.op("dve", lambda: nc.vector.tensor_tensor(out=nlam.t[:], in0=l2.t[:, 1:2], in1=l2.t[:, 0:1], op=ALU.subtract), reads=[l2.r], writes=[nlam.r])
                  k.op("dve", lambda: nc.vector.tensor_scalar_add(out=nlam.t[:], in0=nlam.t[:], scalar1=-0.2), reads=[nlam.r], writes=[nlam.r])
                  gsub = sb("gsub", [128, 128], F32)
                  k.dma("sp", gsub.t[:], subg.partition_broadcast(128), gsub.ds, writes=[gsub.r])
                  k.op("dve", lambda: nc.vector.tensor_scalar_mul(out=gsub.t[:], in0=gsub.t[:], scalar1=0.8), reads=[gsub.r], writes=[gsub.r])
                  KT = sb("KT", [128, S], BF16)
                  QT = sb("QT", [128, S], BF16)
                  V1 = sb("V1", [128, NB, 132], BF16)
                  SGh = sb("SGh", [128, NB, 128], BF16)
                  et = slots("et", [128, 2, 512], BF16, 2)
                  ya = sb("ya", [128, 128], F32)
                  yn = sb("yn", [128, 128], F32)
                  z2 = sb("z2", [128, 2], F32)
                  yst = slots("yst", [128, 4, 128], BF16, 2)
                  k.op("dve", lambda: nc.vector.memset(V1.t[:, :, 128:129], 1.0), writes=[V1.r])
                  accR = [bank[4 + a_ // 3] for a_ in range(8)]

                  def acc_ap(c_, j_, lo, hi):
                      a_ = c_ * 4 + j_
                      return PSF[:, 4 + a_ // 3, (a_ % 3) * 132 + lo:(a_ % 3) * 132 + hi]

                  it = 0
                  for h in range(4):
                      k.dma("sp", KT.t[:], KAT[h], KT.ds, writes=[KT.r])
                      k.dma("sp", QT.t[:], QAT[h], QT.ds, writes=[QT.r])
                      k.dma("sp", V1.t[:, :, 0:128], VA[:, h * 128:(h + 1) * 128].rearrange("(n p) c -> p n c", p=128), V1.ds, writes=[V1.r])
                      k.dma("sp", SGh.t[:], SGA[:, h * 128:(h + 1) * 128].rearrange("(n p) c -> p n c", p=128), SGh.ds, writes=[SGh.r])
                      for qt in range(NT):
                          ys = yst[qt % 2]
                          started = set()
                          for kb in range(4 * qt + 4):
                              pr = it % 2
                              it += 1
                              e_ = et[pr]
                              for c_ in range(2):
                                  k.op("pe", lambda c_=c_, pr=pr, kb=kb, qt=qt: nc.tensor.matmul(PSF[:, 2 * pr + c_, :], lhsT=KT.t[c_ * 64:(c_ + 1) * 64, kb * 128:(kb + 1) * 128],
                                                                                             rhs=QT.t[c_ * 64:(c_ + 1) * 64, qt * 512:(qt + 1) * 512], start=True, stop=True),
                                       reads=[KT.r, QT.r], writes=[bank[2 * pr + c_]], signal=(c_ == 1), same_engine=False)
                              k.op("act", lambda pr=pr, e_=e_: nc.scalar.activation(out=e_.t[:], in_=PSF[:, 2 * pr:2 * pr + 2, :], func=AF.Exp, scale=0.125),
                                   reads=[bank[2 * pr], bank[2 * pr + 1]], writes=[e_.r])
                              r_ = kb - 4 * qt
                              if r_ >= 0:
                                  k.op("dve", lambda e_=e_, r_=r_: nc.vector.tensor_tensor(out=e_.t[:, :, r_ * 128:(r_ + 1) * 128], in0=e_.t[:, :, r_ * 128:(r_ + 1) * 128],
                                                                                         in1=tri.t[:].unsqueeze(1).to_broadcast([128, 2, 128]), op=ALU.mult),
                                       reads=[e_.r, tri.r], writes=[e_.r])
                              for j_ in range(4):
                                  if r_ > j_:
                                      continue
                                  last = (kb == 4 * qt + j_)
                                  for c_ in range(2):
                                      bk_ = 4 + (c_ * 4 + j_) // 3
                                      st_ = (kb == 0 and bk_ not in started)
                                      started.add(bk_)
                                      k.op("pe", lambda c_=c_, j_=j_, e_=e_, kb=kb, last=last, st_=st_: nc.tensor.matmul(acc_ap(c_, j_, 0, 129), lhsT=e_.t[:, c_, j_ * 128:(j_ + 1) * 128],
                                                                                                           rhs=V1.t[:, kb, 0:129], start=st_, stop=last, skip_group_check=True),
                                           reads=[e_.r, V1.r], writes=[accR[c_ * 4 + j_]], signal=(last or (j_ == 3 and c_ == 1)), same_engine=False)
                                  if last:
                                      blk = 4 * qt + j_
                                      a0, a1 = accR[j_], accR[4 + j_]
                                      k.op("dve", lambda j_=j_: nc.vector.tensor_copy(out=z2.t[:, 0:1], in_=acc_ap(0, j_, 128, 129)), reads=[a0], writes=[z2.r])
                                      k.op("dve", lambda j_=j_: nc.vector.tensor_copy(out=z2.t[:, 1:2], in_=acc_ap(1, j_, 128, 129)), reads=[a1, z2.r], writes=[z2.r])
                                      k.op("dve", lambda: nc.vector.reciprocal(out=z2.t[:], in_=z2.t[:]), reads=[z2.r], writes=[z2.r])
                                      k.op("dve", lambda: nc.vector.tensor_tensor(out=z2.t[:, 1:2], in0=z2.t[:, 1:2], in1=nlam.t[:], op=ALU.mult), reads=[z2.r, nlam.r], writes=[z2.r])
                                      k.op("dve", lambda j_=j_: nc.vector.tensor_scalar(out=ya.t[:], in0=acc_ap(0, j_, 0, 128), scalar1=z2.t[:, 0:1], scalar2=None, op0=ALU.mult),
                                           reads=[a0, z2.r], writes=[ya.r])
                                      k.op("dve", lambda j_=j_: nc.vector.scalar_tensor_tensor(out=ya.t[:], in0=acc_ap(1, j_, 0, 128), scalar=z2.t[:, 1:2], in1=ya.t[:], op0=ALU.mult, op1=ALU.add),
                                           reads=[a1, z2.r, ya.r], writes=[ya.r])
                                      k.op("act", lambda: nc.scalar.activation(out=yn.t[:], in_=ya.t[:], func=AF.Square, accum_out=ssq.t[:]), reads=[ya.r], writes=[yn.r, ssq.r])
                                      k.op("act", lambda: nc.scalar.activation(out=rstd.t[:], in_=ssq.t[:], func=AF.Ln, scale=1.0 / 128, bias=EPS), reads=[ssq.r], writes=[rstd.r])
                                      k.op("act", lambda: nc.scalar.activation(out=rstd.t[:], in_=rstd.t[:], func=AF.Exp, scale=-0.5), reads=[rstd.r], writes=[rstd.r])
                                      k.op("dve", lambda: nc.vector.scalar_tensor_tensor(out=yn.t[:], in0=ya.t[:], scalar=rstd.t[:, 0:1], in1=gsub.t[:], op0=ALU.mult, op1=ALU.mult),
                                           reads=[ya.r, rstd.r, gsub.r, yn.r], writes=[yn.r])
                                      k.op("dve", lambda j_=j_, blk=blk, ys=ys: nc.vector.tensor_tensor(out=ys.t[:, j_, :], in0=yn.t[:], in1=SGh.t[:, blk, :], op=ALU.mult),
                                           reads=[yn.r, SGh.r], writes=[ys.r])
                          k.dma("pool", Y0[qt * 512:(qt + 1) * 512, h * 128:(h + 1) * 128].rearrange("(j p) c -> p j c", p=128), ys.t[:], ys.ds, reads=[ys.r])

            if dbg == "a0":
                k.barrier()
                return nc, k
            with Phase():
                esk = sb("esk", [128, 8], F32)
                k.dma("sp", esk.t[:], sinks.partition_broadcast(128), esk.ds, writes=[esk.r])
                k.op("act", lambda: nc.scalar.activation(out=esk.t[:], in_=esk.t[:], func=AF.Exp), reads=[esk.r], writes=[esk.r])
                qbt = slots("qbt", [128, 4, 512], BF16, 2)
                kbt = slots("kbt", [128, 640], BF16, 2)
                vb1 = slots("vb1", [128, 5, 2, 72], BF16, 2)
                sgb = slots("sgb", [128, 4, 512], BF16, 2)
                eb = slots("eb", [128, 16, 128], BF16, 2)
                zz = sb("zz", [128, 8], F32)
                ybt = sb("ybt", [128, 8, 64], F32)
                ysb = slots("ysb", [128, 4, 512], BF16, 2)
                for v_ in vb1:
                    k.op("dve", lambda v_=v_: nc.vector.memset(v_.t[:, :, :, 64:65], 1.0), writes=[v_.r])

                def b0_load(T):
                    s_ = T % 2
                    t0 = T * 512
                    for p_ in range(4):
                        k.dma("sp", qbt[s_].t[:, p_, :], QBT[p_, :, t0:t0 + 512], qbt[s_].ds, writes=[qbt[s_].r])
                    if T > 0:
                        k.dma("sp", kbt[s_].t[:, :], KBT[:, t0 - 128:t0 + 512], kbt[s_].ds, writes=[kbt[s_].r])
                        for g_ in range(2):
                            k.dma("sp", vb1[s_].t[:, :, g_, 0:64], VB[t0 - 128:t0 + 512, g_ * 64:(g_ + 1) * 64].rearrange("(n p) d -> p n d", p=128), vb1[s_].ds, writes=[vb1[s_].r])
                    else:
                        k.dma("sp", kbt[s_].t[:, 128:640], KBT[:, 0:512], kbt[s_].ds, writes=[kbt[s_].r])
                        for g_ in range(2):
                            k.dma("sp", vb1[s_].t[:, 1:5, g_, 0:64], VB[0:512, g_ * 64:(g_ + 1) * 64].rearrange("(n p) d -> p n d", p=128), vb1[s_].ds, writes=[vb1[s_].r])
                    k.dma("sp", sgb[s_].t[:], SGB[t0:t0 + 512, :].rearrange("(j p) c -> p j c", p=128), sgb[s_].ds, writes=[sgb[s_].r])

                PSflat = PSF[:, 0:4, :].rearrange("p a b -> p (a b)")
                b0_load(0)
                for T in range(NT):
                    s_ = T % 2
                    if T + 1 < NT:
                        b0_load(T + 1)
                    for j_ in range(4):
                        if lvl < 2:
                            break
                        blk = 4 * T + j_
                        e_ = eb[blk % 2]
                        kks = (0, 1) if blk > 0 else (1,)
                        for kk in kks:
                            for p_ in range(4):
                                for hf in range(2):
                                    off = (kk * 8 + hf * 4 + p_) * 128
                                    lastmm = (kk == 1 and p_ == 3 and hf == 1)
                                    k.op("pe", lambda off=off, hf=hf, kk=kk, p_=p_, j_=j_, s_=s_: nc.tensor.matmul(
                                        PSflat[:, off:off + 128], lhsT=kbt[s_].t[hf * 64:(hf + 1) * 64, (j_ + kk) * 128:(j_ + kk + 1) * 128],
                                        rhs=qbt[s_].t[hf * 64:(hf + 1) * 64, p_, j_ * 128:(j_ + 1) * 128], start=True, stop=True),
                                         reads=[kbt[s_].r, qbt[s_].r], writes=[bank[0], bank[1], bank[2], bank[3]], signal=lastmm, same_engine=False)
                        for kk in kks:
                            lo = kk * 8
                            k.op("act", lambda lo=lo, e_=e_: nc.scalar.activation(out=e_.t[:, lo:lo + 8, :], in_=PSflat[:, lo * 128:(lo + 8) * 128].rearrange("p (a t) -> p a t", t=128), func=AF.Exp, scale=0.125),
                                 reads=[bank[0], bank[1], bank[2], bank[3]], writes=[e_.r])
                        if lvl < 3:
                            continue
                        if blk > 0:
                            k.op("dve", lambda e_=e_: nc.vector.tensor_tensor(out=e_.t[:, 0:8, :], in0=e_.t[:, 0:8, :], in1=upp.t[:].unsqueeze(1).to_broadcast([128, 8, 128]), op=ALU.mult),
                                 reads=[e_.r, upp.r], writes=[e_.r])
                        k.op("dve", lambda e_=e_: nc.vector.tensor_tensor(out=e_.t[:, 8:16, :], in0=e_.t[:, 8:16, :], in1=tri.t[:].unsqueeze(1).to_broadcast([128, 8, 128]), op=ALU.mult),
                             reads=[e_.r, tri.r], writes=[e_.r])
                        if lvl < 4:
                            continue
                        for p_ in range(4):
                            for hf in range(2):
                                hd = hf * 4 + p_
                                for kk in kks:
                                    k.op("pe", lambda hd=hd, hf=hf, kk=kk, p_=p_, j_=j_, s_=s_, e_=e_, kks=kks: nc.tensor.matmul(
                                        PSF[:, 4 + hd // 4, (hd % 4) * 80:(hd % 4) * 80 + 65], lhsT=e_.t[:, kk * 8 + hf * 4 + p_, :],
                                        rhs=vb1[s_].t[:, j_ + kk, hf, 0:65], start=(kk == kks[0]), stop=(kk == 1)),
                                         reads=[e_.r, vb1[s_].r], writes=[bank[4], bank[5]], signal=(kk == 1 and p_ == 3 and hf == 1), same_engine=False)
                        if lvl < 5:
                            continue
                        for bb in range(2):
                            k.op("dve", lambda bb=bb: nc.vector.tensor_copy(out=zz.t[:, bb * 4:(bb + 1) * 4], in_=PSF[:, 4 + bb, 0:320].rearrange("p (h d) -> p h d", d=80)[:, :, 64]),
                                 reads=[bank[4], bank[5], zz.r], writes=[zz.r])
                        k.op("dve", lambda: nc.vector.tensor_tensor(out=zz.t[:], in0=zz.t[:], in1=esk.t[:], op=ALU.add), reads=[zz.r, esk.r], writes=[zz.r])
                        k.op("dve", lambda: nc.vector.reciprocal(out=zz.t[:], in_=zz.t[:]), reads=[zz.r], writes=[zz.r])
                        for bb in range(2):
                            k.op("dve", lambda bb=bb: nc.vector.tensor_tensor(out=ybt.t[:, bb * 4:(bb + 1) * 4, :], in0=PSF[:, 4 + bb, 0:320].rearrange("p (h d) -> p h d", d=80)[:, :, 0:64],
                                                                            in1=zz.t[:, bb * 4:(bb + 1) * 4].unsqueeze(2).to_broadcast([128, 4, 64]), op=ALU.mult),
                                 reads=[bank[4], bank[5], zz.r, ybt.r], writes=[ybt.r])
                        k.op("dve", lambda j_=j_, s_=s_: nc.vector.tensor_tensor(out=ysb[s_].t[:, j_, :], in0=ybt.t[:].rearrange("p h d -> p (h d)"), in1=sgb[s_].t[:, j_, :], op=ALU.mult),
                             reads=[ybt.r, sgb[s_].r], writes=[ysb[s_].r])
                    if lvl >= 6:
                        k.dma("pool", Y0[T * 512:(T + 1) * 512, 512:1024].rearrange("(j p) c -> p j c", p=128), ysb[s_].t[:], ysb[s_].ds, reads=[ysb[s_].r])

            if dbg == "b0" or lvl != 99:
                k.barrier()
                return nc, k
            with Phase():
                wo = sb("wo", [128, 8, D], BF16)
                k.dma("pool", wo.t[:], e_w_out.rearrange("(kc p) n -> p kc n", p=128), wo.ds, writes=[wo.r])
                yb_ = slots("yb_", [128, D], BF16, 2)
                yT = slots("yT", [128, 8, 128], BF16, 2)
                x1t = slots("x1t", [128, D], F32, 2)
                xv = x_in.rearrange("(n p) d -> n p d", p=128)
                x1v = x1.rearrange("(n p) d -> n p d", p=128)
                y0v = Y0.rearrange("(n p) d -> n p d", p=128)
                k.dma("sp", xt[0].t[:], xv[0], xt[0].ds, writes=[xt[0].r])
                k.dma("sp", yb_[0].t[:], y0v[0], yb_[0].ds, writes=[yb_[0].r])
                for blk in range(NB):
                    if blk + 1 < NB:
                        k.dma("sp", xt[(blk + 1) % 3].t[:], xv[blk + 1], xt[(blk + 1) % 3].ds, writes=[xt[(blk + 1) % 3].r])
                        k.dma("sp", yb_[(blk + 1) % 2].t[:], y0v[blk + 1], yb_[(blk + 1) % 2].ds, writes=[yb_[(blk + 1) % 2].r])
                    xb, yb, yTb, xo = xt[blk % 3], yb_[blk % 2], yT[blk % 2], x1t[blk % 2]
                    transpose8(lambda kc, yb=yb: yb.t[:, kc * 128:(kc + 1) * 128], yb.r, yTb.t[:], yTb.r, eng="act")
                    for cc in range(2):
                        b = cc
                        for kc in range(8):
                            k.op("pe", lambda kc=kc, b=b, cc=cc, yTb=yTb: nc.tensor.matmul(PSF[:, b, :], lhsT=yTb.t[:, kc, :], rhs=wo.t[:, kc, cc * 512:(cc + 1) * 512],
                                                                                       start=(kc == 0), stop=(kc == 7)),
                                 reads=[yTb.r, wo.r], writes=[bank[b]], signal=(kc == 7), same_engine=False)
                        k.op("dve", lambda b=b, cc=cc: nc.vector.tensor_tensor(out=junk.t[:, cc * 512:(cc + 1) * 512], in0=PSF[:, b, :], in1=gate0.t[:, cc * 512:(cc + 1) * 512], op=ALU.mult),
                             reads=[bank[b], gate0.r, junk.r], writes=[junk.r])
                        k.op("dve", lambda cc=cc, xo=xo, xb=xb: nc.vector.tensor_tensor(out=xo.t[:, cc * 512:(cc + 1) * 512], in0=junk.t[:, cc * 512:(cc + 1) * 512],
                                                                                   in1=xb.t[:, cc * 512:(cc + 1) * 512], op=ALU.add),
                             reads=[junk.r, xb.r, xo.r], writes=[xo.r])
                    k.dma("pool", x1v[blk], xo.t[:], xo.ds, reads=[xo.r])

        if dbg == "l0" or not L1:
            k.barrier()
            return nc, k

        sh1, gs1, gate1 = mod_vectors("1", o_w_mod, o_b_mod, o_norm_g)
        H1T = dram_scr("H1T", [8, 128, S], BF16)
        H1OT = dram_scr("H1OT", [8, 128, SO], BF16)
        X1O = dram_scr("X1O", [SO, D], F32)
        Y1T = dram_scr("Y1T", [16, 64, SO], BF16)
        om = sb("om", [128, 2], F32)
        k.dma("sp", om.t[:], ownm[:, :], om.ds, writes=[om.r])
        x1v = x1.rearrange("(n p) d -> n p d", p=128)
        x1ov = X1O.rearrange("(n p) d -> n p d", p=128)

        with Phase():
            st_h = slots("st_h", [128, 8, 512], BF16, 2)
            st_ho = slots("st_ho", [128, 8, 256], BF16, 2)
            hown = sb("hown", [128, D], BF16)
            hof = sb("hof", [128, D], F32)
            xo = slots("xo", [128, D], F32, 2)
            k.dma("sp", xt[0].t[:], x1v[0], xt[0].ds, writes=[xt[0].r])
            for T in range(NT):
                s_ = T % 2
                for j in range(4):
                    blk = T * 4 + j
                    if blk + 1 < NB:
                        nx = xt[(blk + 1) % 3]
                        k.dma("sp", nx.t[:], x1v[blk + 1], nx.ds, writes=[nx.r])
                    xb, hb = xt[blk % 3], htm[blk % 2]
                    norm_mod(xb, hb, gs1, sh1)
                    transpose8(lambda kc, hb=hb: hb.t[:, kc * 128:(kc + 1) * 128], hb.r, st_h[s_].t[:, :, j * 128:(j + 1) * 128], st_h[s_].r, eng="act")
                    if j % 2 == 1:
                        he, xe = htm[(blk - 1) % 2], xt[(blk - 1) % 3]
                        ob = blk // 2
                        k.op("dve", lambda he=he: nc.vector.tensor_scalar(out=hof.t[:], in0=he.t[:], scalar1=om.t[:, 0:1], scalar2=None, op0=ALU.mult),
                             reads=[he.r, om.r, hof.r], writes=[hof.r])
                        k.op("dve", lambda hb=hb: nc.vector.scalar_tensor_tensor(out=hown.t[:], in0=hb.t[:], scalar=om.t[:, 1:2], in1=hof.t[:], op0=ALU.mult, op1=ALU.add),
                             reads=[hb.r, om.r, hof.r, hown.r], writes=[hown.r])
                        transpose8(lambda kc: hown.t[:, kc * 128:(kc + 1) * 128], hown.r, st_ho[s_].t[:, :, (j // 2) * 128:(j // 2 + 1) * 128], st_ho[s_].r, eng="act")
                        xo_ = xo[ob % 2]
                        k.op("dve", lambda xe=xe: nc.vector.tensor_scalar(out=junk.t[:], in0=xe.t[:], scalar1=om.t[:, 0:1], scalar2=None, op0=ALU.mult),
                             reads=[xe.r, om.r, junk.r], writes=[junk.r])
                        k.op("dve", lambda xb=xb, xo_=xo_: nc.vector.scalar_tensor_tensor(out=xo_.t[:], in0=xb.t[:], scalar=om.t[:, 1:2], in1=junk.t[:], op0=ALU.mult, op1=ALU.add),
                             reads=[xb.r, om.r, junk.r, xo_.r], writes=[xo_.r])
                        k.dma("pool", x1ov[ob], xo_.t[:], xo_.ds, reads=[xo_.r])
                k.dma("pool", H1T[:, :, T * 512:(T + 1) * 512].rearrange("k p t -> p k t"), st_h[s_].t[:], st_h[s_].ds, reads=[st_h[s_].r])
                k.dma("pool", H1OT[:, :, T * 256:(T + 1) * 256].rearrange("k p t -> p k t"), st_ho[s_].t[:], st_ho[s_].ds, reads=[st_ho[s_].r])

        w1v = o_w_in.rearrange("(kc p) n -> p kc n", p=128)
        for g in range(4):
            with Phase():
                KTg = sb("KTg", [128, 2, S], BF16)
                Vg = sb("Vg", [128, NB, 256], BF16)
                QTg = sb("QTg", [128, 2, SO], BF16)
                SGT = sb("SGT", [64, 4, SO], BF16)
                with Phase():
                    wq = sb("wq", [128, 8, 256], BF16); wk = sb("wk", [128, 8, 256], BF16)
                    wv = sb("wv", [128, 8, 256], BF16); wg = sb("wg", [128, 8, 256], BF16)
                    for wt_, off in ((wq, 0), (wk, 1024), (wv, 2048), (wg, 3072)):
                        k.dma("pool", wt_.t[:], w1v[:, :, off + g * 256:off + (g + 1) * 256], wt_.ds, writes=[wt_.r])
                    h1t = slots("h1t", [128, 8, 512], BF16, 2)
                    ge1 = sb("ge1", [64, 512], F32)
                    k.dma("sp", h1t[0].t[:], H1T[:, :, 0:512].rearrange("k p t -> p k t"), h1t[0].ds, writes=[h1t[0].r])
                    bs = 0
                    for T in range(NT):
                        ht = h1t[T % 2]
                        if T + 1 < NT:
                            k.dma("sp", h1t[(T + 1) % 2].t[:], H1T[:, :, (T + 1) * 512:(T + 2) * 512].rearrange("k p t -> p k t"), h1t[(T + 1) % 2].ds, writes=[h1t[(T + 1) % 2].r])
                        for p_ in range(2):
                            b = bs; bs = (bs + 1) % 4
                            for kc in range(8):
                                k.op("pe", lambda kc=kc, b=b, p_=p_, ht=ht: nc.tensor.matmul(PSF[:, b, :], lhsT=wk.t[:, kc, p_ * 128:(p_ + 1) * 128], rhs=ht.t[:, kc, :], start=(kc == 0), stop=(kc == 7)),
                                     reads=[wk.r, ht.r], writes=[bank[b]], signal=(kc == 7), same_engine=False)
                            k.op("act", lambda b=b, p_=p_, T=T: nc.scalar.copy(out=KTg.t[:, p_, T * 512:(T + 1) * 512], in_=PSF[:, b, :]), reads=[bank[b]], writes=[KTg.r])
                        for j in range(4):
                            b = bs; bs = (bs + 1) % 4
                            for kc in range(8):
                                k.op("pe", lambda kc=kc, b=b, j=j, ht=ht: nc.tensor.matmul(PSF[:, b, 0:256], lhsT=ht.t[:, kc, j * 128:(j + 1) * 128], rhs=wv.t[:, kc, :], start=(kc == 0), stop=(kc == 7)),
                                     reads=[wv.r, ht.r], writes=[bank[b]], signal=(kc == 7), same_engine=False)
                            k.op("dve", lambda b=b, j=j, T=T: nc.vector.tensor_copy(out=Vg.t[:, T * 4 + j, :], in_=PSF[:, b, 0:256]), reads=[bank[b]], writes=[Vg.r])
                    k.dma("sp", h1t[0].t[:], H1OT[:, :, 0:512].rearrange("k p t -> p k t"), h1t[0].ds, writes=[h1t[0].r])
                    for TO in range(NTO):
                        ht = h1t[TO % 2]
                        if TO + 1 < NTO:
                            k.dma("sp", h1t[(TO + 1) % 2].t[:], H1OT[:, :, (TO + 1) * 512:(TO + 2) * 512].rearrange("k p t -> p k t"), h1t[(TO + 1) % 2].ds, writes=[h1t[(TO + 1) % 2].r])
                        for p_ in range(2):
                            b = bs; bs = (bs + 1) % 4
                            for kc in range(8):
                                k.op("pe", lambda kc=kc, b=b, p_=p_, ht=ht: nc.tensor.matmul(PSF[:, b, :], lhsT=wq.t[:, kc, p_ * 128:(p_ + 1) * 128], rhs=ht.t[:, kc, :], start=(kc == 0), stop=(kc == 7)),
                                     reads=[wq.r, ht.r], writes=[bank[b]], signal=(kc == 7), same_engine=False)
                            k.op("act", lambda b=b, p_=p_, TO=TO: nc.scalar.mul(out=QTg.t[:, p_, TO * 512:(TO + 1) * 512], in_=PSF[:, b, :], mul=0.125), reads=[bank[b]], writes=[QTg.r])
                        for hl in range(4):
                            b = bs; bs = (bs + 1) % 4
                            for kc in range(8):
                                k.op("pe", lambda kc=kc, b=b, hl=hl, ht=ht: nc.tensor.matmul(PSF[0:64, b, :], lhsT=wg.t[:, kc, hl * 64:(hl + 1) * 64], rhs=ht.t[:, kc, :], start=(kc == 0), stop=(kc == 7)),
                                     reads=[wg.r, ht.r], writes=[bank[b]], signal=(kc == 7), same_engine=False)
                            k.op("act", lambda b=b: nc.scalar.activation(out=ge1.t[:], in_=PSF[0:64, b, :], func=AF.Exp, scale=-1.0), reads=[bank[b]], writes=[ge1.r])
                            k.op("dve", lambda: nc.vector.tensor_scalar_add(out=ge1.t[:], in0=ge1.t[:], scalar1=1.0), reads=[ge1.r], writes=[ge1.r])
                            k.op("dve", lambda: nc.vector.reciprocal(out=ge1.t[:], in_=ge1.t[:]), reads=[ge1.r], writes=[ge1.r])
                            k.op("dve", lambda b=b, hl=hl, TO=TO: nc.vector.tensor_tensor(out=SGT.t[:, hl, TO * 512:(TO + 1) * 512], in0=ge1.t[:], in1=PSF[0:64, b, :], op=ALU.mult),
                                 reads=[ge1.r, bank[b]], writes=[SGT.r])
                with Phase():
                    dm = sb("dm", [128, 8, 512], BF16)
                    k.dma("pool", dm.t[:], dmask.rearrange("p (r q) -> p r q", r=8), dm.ds, writes=[dm.r])
                    e32 = slots("e32", [128, 2, 512], F32, 2)
                    sp_ = slots("sp_", [128, 2, 512], BF16, 2)
                    ex = slots("ex", [128, 2, 512], F32, 2)
                    ww = slots("ww", [128, 2, 512], BF16, 2)
                    yst1 = slots("yst1", [64, 2, 512], BF16, 2)
                    fin = 0
                    for m in range(NTO):
                        for p_ in range(2):
                            kbs = list(range(8 * m + 7, -1, -1))

                            def stage1(kb, i):
                                sl = i % 2
                                for hf in range(2):
                                    k.op("pe", lambda hf=hf, kb=kb: nc.tensor.matmul(PSF[:, hf, :], lhsT=KTg.t[hf * 64:(hf + 1) * 64, p_, kb * 128:(kb + 1) * 128],
                                                                                   rhs=QTg.t[hf * 64:(hf + 1) * 64, p_, m * 512:(m + 1) * 512], start=True, stop=True),
                                         reads=[KTg.r, QTg.r], writes=[bank[hf]], signal=(hf == 1), same_engine=False)
                                k.op("act", lambda sl=sl: nc.scalar.activation(out=e32[sl].t[:], in_=PSF[:, 0:2, :], func=AF.Exp), reads=[bank[0], bank[1]], writes=[e32[sl].r])
                                k.op("act", lambda sl=sl: nc.scalar.activation(out=sp_[sl].t[:], in_=e32[sl].t[:], func=AF.Ln, bias=1.0), reads=[e32[sl].r], writes=[sp_[sl].r])
                                r_ = kb - 8 * m
                                if r_ >= 0:
                                    mk = dm.t[:, r_, :].unsqueeze(1).to_broadcast([128, 2, 512])
                                    k.op("dve", lambda sl=sl, mk=mk: nc.vector.tensor_tensor(out=sp_[sl].t[:], in0=sp_[sl].t[:], in1=mk, op=ALU.mult), reads=[sp_[sl].r, dm.r], writes=[sp_[sl].r])
                                    k.op("dve", lambda sl=sl, mk=mk: nc.vector.tensor_tensor(out=e32[sl].t[:], in0=e32[sl].t[:], in1=mk, op=ALU.mult), reads=[e32[sl].r, dm.r], writes=[e32[sl].r])

                            def stage2(kb, i, first, last):
                                sl = i % 2
                                for hf in range(2):
                                    k.op("pe", lambda hf=hf, sl=sl: nc.tensor.matmul(PSF[:, 2 + hf, :], lhsT=negtri.t[:], rhs=sp_[sl].t[:, hf, :], start=first, stop=True, skip_group_check=True),
                                         reads=[negtri.r, sp_[sl].r], writes=[bank[2 + hf]], signal=(hf == 1), same_engine=False)
                                k.op("act", lambda sl=sl: nc.scalar.activation(out=ex[sl].t[:], in_=PSF[:, 2:4, :], func=AF.Exp), reads=[bank[2], bank[3]], writes=[ex[sl].r])
                                k.op("dve", lambda sl=sl: nc.vector.tensor_tensor(out=ww[sl].t[:], in0=e32[sl].t[:], in1=ex[sl].t[:], op=ALU.mult), reads=[e32[sl].r, ex[sl].r], writes=[ww[sl].r])
                                if not last:
                                    for hf in range(2):
                                        k.op("pe", lambda hf=hf, sl=sl: nc.tensor.matmul(PSF[:, 2 + hf, :], lhsT=negrest.t[:], rhs=sp_[sl].t[:, hf, :], start=False, stop=True, skip_group_check=True),
                                             reads=[negrest.r, sp_[sl].r], writes=[bank[2 + hf]], signal=(hf == 1), same_engine=False)
                                for hf in range(2):
                                    k.op("pe", lambda hf=hf, sl=sl, kb=kb: nc.tensor.matmul(PSF[0:64, 4 + hf, :], lhsT=Vg.t[:, kb, (p_ * 2 + hf) * 64:(p_ * 2 + hf + 1) * 64], rhs=ww[sl].t[:, hf, :],
                                                                                          start=first, stop=last),
                                         reads=[Vg.r, ww[sl].r], writes=[bank[4 + hf]], signal=(hf == 1), same_engine=False)

                            stage1(kbs[0], 0)
                            for i, kb in enumerate(kbs):
                                if i + 1 < len(kbs):
                                    stage1(kbs[i + 1], i + 1)
                                stage2(kb, i, i == 0, kb == 0)
                            ys = yst1[fin % 2]
                            fin += 1
                            for hf in range(2):
                                hl = p_ * 2 + hf
                                k.op("dve", lambda hf=hf, hl=hl, ys=ys: nc.vector.tensor_tensor(out=ys.t[:, hf, :], in0=PSF[0:64, 4 + hf, :], in1=SGT.t[:, hl, m * 512:(m + 1) * 512], op=ALU.mult),
                                     reads=[bank[4 + hf], SGT.r, ys.r], writes=[ys.r])
                            for hf in range(2):
                                k.dma("pool", Y1T[g * 4 + p_ * 2 + hf, :, m * 512:(m + 1) * 512], ys.t[:, hf, :], ys.ds, reads=[ys.r])

        with Phase():
            wo1 = sb("wo1", [64, 16, D], BF16)
            k.dma("pool", wo1.t[:], o_w_out.rearrange("(h p) n -> p h n", p=64), wo1.ds, writes=[wo1.r])
            yt1 = slots("yt1", [64, 16, 128], BF16, 2)
            outt = slots("outt", [128, D], F32, 2)
            outv = out_o.rearrange("(n p) d -> n p d", p=128)
            NOB = SO // 128
            k.dma("sp", xt[0].t[:], x1ov[0], xt[0].ds, writes=[xt[0].r])
            k.dma("sp", yt1[0].t[:], Y1T[:, :, 0:128].rearrange("h p t -> p h t"), yt1[0].ds, writes=[yt1[0].r])
            for ob in range(NOB):
                if ob + 1 < NOB:
                    k.dma("sp", xt[(ob + 1) % 3].t[:], x1ov[ob + 1], xt[(ob + 1) % 3].ds, writes=[xt[(ob + 1) % 3].r])
                    k.dma("sp", yt1[(ob + 1) % 2].t[:], Y1T[:, :, (ob + 1) * 128:(ob + 2) * 128].rearrange("h p t -> p h t"), yt1[(ob + 1) % 2].ds, writes=[yt1[(ob + 1) % 2].r])
                xb, yt, oo = xt[ob % 3], yt1[ob % 2], outt[ob % 2]
                for cc in range(2):
                    b = cc
                    for h in range(16):
                        k.op("pe", lambda h=h, b=b, cc=cc, yt=yt: nc.tensor.matmul(PSF[:, b, :], lhsT=yt.t[:, h, :], rhs=wo1.t[:, h, cc * 512:(cc + 1) * 512], start=(h == 0), stop=(h == 15)),
                             reads=[yt.r, wo1.r], writes=[bank[b]], signal=(h == 15), same_engine=False)
                    k.op("dve", lambda b=b, cc=cc: nc.vector.tensor_tensor(out=junk.t[:, cc * 512:(cc + 1) * 512], in0=PSF[:, b, :], in1=gate1.t[:, cc * 512:(cc + 1) * 512], op=ALU.mult),
                         reads=[bank[b], gate1.r, junk.r], writes=[junk.r])
                    k.op("dve", lambda cc=cc, oo=oo, xb=xb: nc.vector.tensor_tensor(out=oo.t[:, cc * 512:(cc + 1) * 512], in0=junk.t[:, cc * 512:(cc + 1) * 512], in1=xb.t[:, cc * 512:(cc + 1) * 512], op=ALU.add),
                         reads=[junk.r, xb.r, oo.r], writes=[oo.r])
                k.dma("pool", outv[ob], oo.t[:], oo.ds, reads=[oo.r])
        k.barrier()
    return nc, k


def _consts():
    s = np.arange(128)[:, None]
    t = np.arange(128)[None, :]
    ident = (s == t).astype(np.float32)
    tri = (s <= t).astype(np.float32)
    upp = (s > t).astype(np.float32)
    negtri = -(s >= t).astype(np.float32)
    inv = (10000.0 ** (-np.arange(32, dtype=np.float32) / np.float32(32))).astype(np.float32)
    return np.concatenate([ident, tri, upp, negtri, np.broadcast_to(inv[None, :], (128, 32))], axis=1).astype(np.float32)


def host_inputs(inp, b, hh, S, layers=(0, 1)):
    f = lambda a: np.ascontiguousarray(np.asarray(a), dtype=np.float32)
    m = {"cT": f(np.asarray(inp["c"])[b].reshape(8, 128).T), "consts": _consts()}
    if 0 in layers:
        m["x"] = f(np.asarray(inp["x"])[b, :S])
        m["posT"] = np.ascontiguousarray(np.asarray(inp["positions"])[b, :S].reshape(S // 128, 128).T.astype(np.int32))
        m["e_norm_g"] = f(inp["even_norm_g"]).reshape(1, D)
        m["e_w_mod"] = f(inp["even_w_mod"])[0]
        m["e_b_mod"] = f(inp["even_b_mod"]).reshape(1, 3 * D)
        w = f(inp["even_w_in"])[0]
        qa, ka, va, ga, qb, kb, vb, gb = np.split(w, np.cumsum([512, 512, 512, 512, 512, 128, 128])[:], axis=1)
        qbp = qb.reshape(D, 2, 4, 64).transpose(0, 2, 1, 3).reshape(D, 512)
        m["e_w_in"] = np.ascontiguousarray(np.concatenate([qa, ka, va, ga, qbp, gb, kb, vb], axis=1))
        m["e_w_out"] = f(inp["even_w_out"])[0]
        m["gains"] = np.concatenate([f(inp["a_q_gain"])[0], f(inp["a_k_gain"])[0], f(inp["b_q_gain"])[0], f(inp["b_k_gain"])[0]]).reshape(1, 256)
        m["lamv"] = np.concatenate([f(inp["a_lambda_q1"])[0], f(inp["a_lambda_k1"])[0], f(inp["a_lambda_q2"])[0], f(inp["a_lambda_k2"])[0]]).reshape(1, 256)
        m["subg"] = f(inp["a_subln_g"]).reshape(1, 128)
        m["sinks"] = f(inp["b_sinks"]).reshape(1, 8)
    if 1 in layers:
        m["o_norm_g"] = f(inp["odd_norm_g"]).reshape(1, D)
        m["o_w_mod"] = f(inp["odd_w_mod"])[0]
        m["o_b_mod"] = f(inp["odd_b_mod"]).reshape(1, 3 * D)
        m["o_w_in"] = f(inp["odd_w_in"])[0]
        m["o_w_out"] = f(inp["odd_w_out"])[0]
        m["ownm"] = np.ascontiguousarray(np.broadcast_to(np.array([[1.0 - hh, float(hh)]], np.float32), (128, 2)))
        s = np.arange(128)[:, None, None, None]
        r = np.arange(8)[None, :, None, None]
        jj = np.arange(4)[None, None, :, None]
        tq = np.arange(128)[None, None, None, :]
        g = 2 * jj + hh
        msk = ((r < g) | ((r == g) & (s < tq))).astype(np.float32)
        m["dmask"] = np.ascontiguousarray(msk.reshape(128, 8 * 512))
    return m


_CACHE = {}


def kernel(**inputs):
    S = 8192
    if "nc" not in _CACHE:
        _CACHE["nc"] = build(S=S, layers=(0, 1))[0]
    nc = _CACHE["nc"]
    in_maps = [host_inputs(inputs, c // 2, c % 2, S) for c in range(8)]
    res = run_bass_kernel_spmd(nc, in_maps, core_ids=list(range(8)))
    out = np.empty((4, S, D), np.float32)
    for c in range(8):
        b, hh = c // 2, c % 2
        o = np.asarray(res.results[c]["out"]).reshape(S // 256, 128, D)
        out[b].reshape(S // 256, 2, 128, D)[:, hh] = o
    return out
```

```python
import math
from contextlib import ExitStack
import numpy as np
import concourse.bass as bass
import concourse.mybir as mybir
from concourse.bass_utils import run_bass_kernel_spmd

F32 = mybir.dt.float32
BF16 = mybir.dt.bfloat16
I32 = mybir.dt.int32
AF = mybir.ActivationFunctionType
ALU = mybir.AluOpType
AX = mybir.AxisListType

D = 1024
EPS = 1e-6
PI = math.pi


class Res:
    __slots__ = ("name", "lw", "rd")

    def __init__(self, name=""):
        self.name = name
        self.lw = None
        self.rd = {}


class SemObj:
    __slots__ = ("sem", "count", "name")

    def __init__(self, sem, name):
        self.sem = sem
        self.count = 0
        self.name = name


class KB:
    def __init__(self, nc, stack):
        self.nc = nc
        self.stack = stack
        self.engs = {"pe": nc.tensor, "act": nc.scalar, "dve": nc.vector, "pool": nc.gpsimd, "sp": nc.sync}
        self.so = {}
        for k in self.engs:
            s = stack.enter_context(nc.semaphore("prog_" + k))
            self.so[k] = SemObj(s, k)
        self.waited = {k: {} for k in self.engs}
        self.n_inst = 0
        self.all_so = list(self.so.values())
        self.free_dma = {}

    def new_dma_sem(self, name, q="sp"):
        if self.free_dma.setdefault(q, []):
            return self.free_dma[q].pop()
        s = self.stack.enter_context(self.nc.semaphore("dma_" + name))
        so = SemObj(s, name)
        self.all_so.append(so)
        return so

    def barrier(self):
        for e in self.engs:
            self._wait(e, [(so, so.count) for so in self.all_so if so is not self.so[e]])

    def _wait(self, e, deps):
        w = self.waited[e]
        for so, val in deps:
            if val <= 0 or w.get(so, 0) >= val:
                continue
            self.engs[e].wait_ge(so.sem, val)
            w[so] = val

    def _deps(self, e, reads, writes, same_engine):
        me = self.so[e]
        deps = []
        for r in reads:
            if r.lw is not None and (same_engine or r.lw[0] is not me):
                deps.append(r.lw)
        for r in writes:
            if r.lw is not None and (same_engine or r.lw[0] is not me):
                deps.append(r.lw)
            for so, v in r.rd.items():
                if same_engine or so is not me:
                    deps.append((so, v))
        return deps

    def op(self, e, fn, reads=(), writes=(), signal=True, same_engine=True):
        me = self.so[e]
        self._wait(e, self._deps(e, reads, writes, same_engine))
        ins = fn()
        self.n_inst += 1
        if signal:
            me.count += 1
            ins.then_inc(me.sem, 1)
            ev = (me, me.count)
        else:
            ev = (me, me.count + 1)
        for r in reads:
            if r.rd.get(me, 0) < ev[1]:
                r.rd[me] = ev[1]
        for r in writes:
            r.lw = ev
            r.rd = {}
        return ins

    def dma(self, q, out_ap, in_ap, dsem, reads=(), writes=(), **kw):
        if isinstance(dsem, Buf):
            dsem = dsem.get_ds(q)
        self._wait(q, self._deps(q, reads, writes, True))
        ins = self.engs[q].dma_start(out=out_ap, in_=in_ap, **kw)
        self.n_inst += 1
        dsem.count += 16
        ins.then_inc(dsem.sem, 16)
        ev = (dsem, dsem.count)
        for r in reads:
            if r.rd.get(dsem, 0) < ev[1]:
                r.rd[dsem] = ev[1]
        for r in writes:
            r.lw = ev
            r.rd = {}
        return ins


class Buf:
    def __init__(self, t, name, k):
        self.t = t
        self.r = Res(name)
        self.name = name
        self._k = k
        self._ds = {}

    @property
    def ds(self):
        return self

    def get_ds(self, q):
        if q not in self._ds:
            self._ds[q] = self._k.new_dma_sem(self.name + q, q)
        return self._ds[q]


def build(S=8192, layers=(0, 1), dbg=False):
    NB = S // 128
    NT = S // 512
    SO = S // 2
    NTO = SO // 512
    nc = bass.Bass("TRN2", target_bir_lowering=False)
    dram_in = lambda n, sh, dt=F32: nc.dram_tensor(n, sh, dt, kind="ExternalInput").ap()
    dram_out = lambda n, sh, dt=F32: nc.dram_tensor(n, sh, dt, kind="ExternalOutput").ap()
    dram_scr = lambda n, sh, dt: nc.dram_tensor(n, sh, dt, kind=("ExternalOutput" if dbg else "Internal")).ap()

    L0 = 0 in layers
    L1 = 1 in layers
    cT = dram_in("cT", [128, 8])
    consts = dram_in("consts", [128, 128 * 4 + 32])
    if L0:
        x_in = dram_in("x", [S, D])
        posT = dram_in("posT", [128, NB], I32)
        e_norm_g = dram_in("e_norm_g", [1, D]); e_w_mod = dram_in("e_w_mod", [D, 3 * D]); e_b_mod = dram_in("e_b_mod", [1, 3 * D])
        e_w_in = dram_in("e_w_in", [D, 3328]); e_w_out = dram_in("e_w_out", [D, D])
        gains = dram_in("gains", [1, 4 * 64])
        lamv = dram_in("lamv", [1, 4 * 64])
        subg = dram_in("subg", [1, 128])
        sinks = dram_in("sinks", [1, 8])
    if L1:
        o_norm_g = dram_in("o_norm_g", [1, D]); o_w_mod = dram_in("o_w_mod", [D, 3 * D]); o_b_mod = dram_in("o_b_mod", [1, 3 * D])
        o_w_in = dram_in("o_w_in", [D, 4 * D]); o_w_out = dram_in("o_w_out", [D, D])
        ownm = dram_in("ownm", [128, 2])
        dmask = dram_in("dmask", [128, 8 * 512])
        out_o = dram_out("out", [SO, D])
    if L0 and L1:
        x1 = dram_scr("x1", [S, D], F32)
    elif L0:
        x1 = dram_out("x1", [S, D])
    else:
        x1 = dram_in("x1", [S, D])

    with ExitStack() as st:
        k = KB(nc, st)

        cur = [st]
        bufs_of = {id(st): []}

        uid = [0]

        def sb(name, shape, dt):
            uid[0] += 1
            name = f"{name}_{uid[0]}"
            bf_ = Buf(cur[0].enter_context(nc.sbuf_tensor(name, shape, dt)), name, k)
            bufs_of[id(cur[0])].append(bf_)
            return bf_

        def slots(name, shape, dt, n):
            return [sb(f"{name}{i}", shape, dt) for i in range(n)]

        class Phase:
            def __enter__(self_p):
                self_p.prev = cur[0]
                self_p.stk = ExitStack()
                cur[0] = self_p.stk
                bufs_of[id(self_p.stk)] = []
                return self_p

            def __exit__(self_p, *a):
                k.barrier()
                for bf_ in bufs_of.pop(id(self_p.stk)):
                    for q_, so_ in bf_._ds.items():
                        k.free_dma.setdefault(q_, []).append(so_)
                    bf_._ds = {}
                self_p.stk.close()
                cur[0] = self_p.prev
                return False

        PSF = st.enter_context(nc.psum_tensor("psf", [128, 7, 512], F32))
        PST = st.enter_context(nc.psum_tensor("pst", [128, 1024], BF16))
        bank = [Res(f"bank{i}") for i in range(7)]
        bankT = Res("bankT")

        cst32 = sb("cst32", [128, 128 * 4 + 32], F32)
        ident = sb("ident", [128, 128], BF16)
        tri = sb("tri", [128, 128], BF16)
        upp = sb("upp", [128, 128], BF16)
        negtri = sb("negtri", [128, 128], BF16)
        negrest = sb("negrest", [128, 128], BF16)
        k.dma("sp", cst32.t[:], consts[:, :], cst32.ds, writes=[cst32.r])
        for i, tdst in enumerate((ident, tri, upp, negtri)):
            k.op("dve", lambda tdst=tdst, i=i: nc.vector.tensor_copy(out=tdst.t[:], in_=cst32.t[:, i * 128:(i + 1) * 128]),
                 reads=[cst32.r], writes=[tdst.r])
        k.op("dve", lambda: nc.vector.tensor_scalar(out=negrest.t[:], in0=cst32.t[:, 384:512], scalar1=-1.0, scalar2=-1.0,
                                                   op0=ALU.mult, op1=ALU.add), reads=[cst32.r], writes=[negrest.r])
        invf = cst32.t[:, 512:544]

        sc = sb("sc", [128, 8], F32)
        sce = sb("sce", [128, 8], F32)
        k.dma("sp", sc.t[:], cT[:, :], sc.ds, writes=[sc.r])
        k.op("act", lambda: nc.scalar.activation(out=sce.t[:], in_=sc.t[:], func=AF.Exp, scale=-1.0), reads=[sc.r], writes=[sce.r])
        k.op("dve", lambda: nc.vector.tensor_scalar_add(out=sce.t[:], in0=sce.t[:], scalar1=1.0), reads=[sce.r], writes=[sce.r])
        k.op("dve", lambda: nc.vector.reciprocal(out=sce.t[:], in_=sce.t[:]), reads=[sce.r], writes=[sce.r])
        k.op("dve", lambda: nc.vector.tensor_tensor(out=sc.t[:], in0=sc.t[:], in1=sce.t[:], op=ALU.mult), reads=[sc.r, sce.r], writes=[sc.r])
        screp = sb("screp", [128, 8, 128], F32)
        k.op("dve", lambda: nc.vector.tensor_copy(out=screp.t[:], in_=sc.t[:].unsqueeze(2).to_broadcast([128, 8, 128])),
             reads=[sc.r], writes=[screp.r])

        def mod_vectors(tag, w_mod, b_mod, norm_g):
            shift = sb("shift" + tag, [128, D], F32)
            gs = sb("gs" + tag, [128, D], F32)
            gate = sb("gate" + tag, [128, D], F32)
            with Phase():
                _mod_vectors(w_mod, b_mod, norm_g, shift, gs, gate)
            return shift, gs, gate

        def _mod_vectors(w_mod, b_mod, norm_g, shift, gs, gate):
            wm = slots("wm", [128, 8, 512], F32, 2)
            bmod = sb("bmod", [128, 3 * D], F32)
            gbc = sb("gbc", [128, D], F32)
            k.dma("sp", bmod.t[:], b_mod.partition_broadcast(128), bmod.ds, writes=[bmod.r])
            k.dma("sp", gbc.t[:], norm_g.partition_broadcast(128), gbc.ds, writes=[gbc.r])
            wv = w_mod.rearrange("(kc p) n -> p kc n", p=128)
            for cc in range(6):
                w = wm[cc % 2]
                k.dma("sp", w.t[:], wv[:, :, cc * 512:(cc + 1) * 512], w.ds, writes=[w.r])
                b = cc % 2
                for kc in range(8):
                    k.op("pe", lambda kc=kc, b=b, w=w: nc.tensor.matmul(PSF[:, b, :], lhsT=screp.t[:, kc, :], rhs=w.t[:, kc, :],
                                                                        start=(kc == 0), stop=(kc == 7)),
                         reads=[screp.r, w.r], writes=[bank[b]], signal=(kc == 7), same_engine=False)
                seg, off = cc // 2, (cc % 2) * 512
                bsl = bmod.t[:, cc * 512:(cc + 1) * 512]
                if seg == 0:
                    k.op("dve", lambda b=b, off=off, bsl=bsl: nc.vector.tensor_tensor(out=shift.t[:, off:off + 512], in0=PSF[:, b, :], in1=bsl, op=ALU.add),
                         reads=[bank[b], bmod.r], writes=[shift.r])
                elif seg == 1:
                    k.op("dve", lambda b=b, off=off, bsl=bsl: nc.vector.scalar_tensor_tensor(out=gs.t[:, off:off + 512], in0=PSF[:, b, :], scalar=1.0, in1=bsl,
                                                                                            op0=ALU.add, op1=ALU.add),
                         reads=[bank[b], bmod.r], writes=[gs.r])
                    k.op("dve", lambda off=off: nc.vector.tensor_tensor(out=gs.t[:, off:off + 512], in0=gs.t[:, off:off + 512], in1=gbc.t[:, off:off + 512], op=ALU.mult),
                         reads=[gs.r, gbc.r], writes=[gs.r])
                else:
                    k.op("dve", lambda b=b, off=off, bsl=bsl: nc.vector.tensor_tensor(out=gate.t[:, off:off + 512], in0=PSF[:, b, :], in1=bsl, op=ALU.add),
                         reads=[bank[b], bmod.r], writes=[gate.r])
            return shift, gs, gate

        xt = slots("xt", [128, D], F32, 3)
        junk = sb("junk", [128, D], F32)
        ssq = sb("ssq", [128, 1], F32)
        rstd = sb("rstd", [128, 1], F32)
        htm = slots("htm", [128, D], BF16, 2)

        def norm_mod(xb, hb, gs, shift):
            k.op("act", lambda: nc.scalar.activation(out=junk.t[:], in_=xb.t[:], func=AF.Square, accum_out=ssq.t[:]),
                 reads=[xb.r], writes=[junk.r, ssq.r])
            k.op("act", lambda: nc.scalar.activation(out=rstd.t[:], in_=ssq.t[:], func=AF.Ln, scale=1.0 / D, bias=EPS), reads=[ssq.r], writes=[rstd.r])
            k.op("act", lambda: nc.scalar.activation(out=rstd.t[:], in_=rstd.t[:], func=AF.Exp, scale=-0.5), reads=[rstd.r], writes=[rstd.r])
            k.op("dve", lambda: nc.vector.scalar_tensor_tensor(out=junk.t[:], in0=xb.t[:], scalar=rstd.t[:, 0:1], in1=gs.t[:], op0=ALU.mult, op1=ALU.mult),
                 reads=[xb.r, rstd.r, gs.r, junk.r], writes=[junk.r])
            k.op("dve", lambda: nc.vector.tensor_tensor(out=hb.t[:], in0=junk.t[:], in1=shift.t[:], op=ALU.add),
                 reads=[junk.r, shift.r], writes=[hb.r])

        def transpose8(src_ap_fn, src_r, dst_ap, dst_r, eng="act"):
            for kc in range(8):
                k.op("pe", lambda kc=kc: nc.tensor.transpose(out=PST[:, kc * 128:(kc + 1) * 128], in_=src_ap_fn(kc), identity=ident.t[:]),
                     reads=[src_r, ident.r], writes=[bankT], signal=(kc == 7), same_engine=False)
            src3 = PST[:, :].rearrange("p (a b) -> p a b", a=8)
            if eng == "act":
                k.op("act", lambda: nc.scalar.copy(out=dst_ap, in_=src3), reads=[bankT], writes=[dst_r])
            else:
                k.op("dve", lambda: nc.vector.tensor_copy(out=dst_ap, in_=src3), reads=[bankT], writes=[dst_r])

        if L0:
            sh0, gs0, gate0 = mod_vectors("0", e_w_mod, e_b_mod, e_norm_g)
            QAT = dram_scr("QAT", [4, 128, S], BF16)
            KAT = dram_scr("KAT", [4, 128, S], BF16)
            VA = dram_scr("VA", [S, 512], BF16)
            SGA = dram_scr("SGA", [S, 512], BF16)
            QBT = dram_scr("QBT", [4, 128, S], BF16)
            KBT = dram_scr("KBT", [128, S], BF16)
            VB = dram_scr("VB", [S, 128], BF16)
            SGB = dram_scr("SGB", [S, 512], BF16)
            Y0 = dram_scr("Y0", [S, D], BF16)

            lvl = int(str(dbg)[3:]) if str(dbg).startswith("b0x") else 99
            with (Phase() if lvl == 99 else ExitStack()):
              if lvl == 99:
                  posi = sb("posi", [128, NB], I32)
                  posf = sb("posf", [128, NB], F32)
                  ang = sb("ang", [128, NB, 32], F32)
                  cosT = sb("cosT", [128, NB, 32], F32)
                  sinT = sb("sinT", [128, NB, 32], F32)
                  k.dma("sp", posi.t[:], posT[:, :], posi.ds, writes=[posi.r])
                  k.op("dve", lambda: nc.vector.tensor_copy(out=posf.t[:], in_=posi.t[:]), reads=[posi.r], writes=[posf.r])
                  k.op("dve", lambda: nc.vector.tensor_tensor(out=ang.t[:], in0=posf.t[:].unsqueeze(2).to_broadcast([128, NB, 32]),
                                                             in1=invf.unsqueeze(1).to_broadcast([128, NB, 32]), op=ALU.mult),
                       reads=[posf.r, cst32.r], writes=[ang.r])
                  negpi = sb("negpi", [128, 1], F32)
                  k.op("dve", lambda: nc.vector.memset(negpi.t[:], -PI), writes=[negpi.r])
                  ui = sb("ui", [128, NB, 32], I32)
                  uf = sb("uf", [128, NB, 32], F32)
                  for tbl, c0 in ((sinT, 0.5), (cosT, 0.75)):
                      k.op("dve", lambda tbl=tbl, c0=c0: nc.vector.tensor_scalar(out=tbl.t[:], in0=ang.t[:], scalar1=1.0 / (2 * PI), scalar2=c0, op0=ALU.mult, op1=ALU.add),
                           reads=[ang.r], writes=[tbl.r])
                      k.op("dve", lambda tbl=tbl: nc.vector.tensor_copy(out=ui.t[:], in_=tbl.t[:]), reads=[tbl.r, ui.r], writes=[ui.r])
                      k.op("dve", lambda: nc.vector.tensor_copy(out=uf.t[:], in_=ui.t[:]), reads=[ui.r, uf.r], writes=[uf.r])
                      k.op("dve", lambda tbl=tbl: nc.vector.tensor_tensor(out=tbl.t[:], in0=tbl.t[:], in1=uf.t[:], op=ALU.subtract), reads=[tbl.r, uf.r], writes=[tbl.r])
                      k.op("dve", lambda tbl=tbl: nc.vector.tensor_single_scalar(out=uf.t[:], in_=tbl.t[:], scalar=0.0, op=ALU.is_lt), reads=[tbl.r, uf.r], writes=[uf.r])
                      k.op("dve", lambda tbl=tbl: nc.vector.tensor_tensor(out=tbl.t[:], in0=tbl.t[:], in1=uf.t[:], op=ALU.add), reads=[tbl.r, uf.r], writes=[tbl.r])
                      k.op("act", lambda tbl=tbl: nc.scalar.activation(out=tbl.t[:], in_=tbl.t[:], func=AF.Sin, bias=negpi.t[:, 0:1], scale=2 * PI),
                           reads=[tbl.r, negpi.r], writes=[tbl.r])
                  gn = sb("gn", [128, 4 * 64], F32)
                  k.dma("sp", gn.t[:], gains.partition_broadcast(128), gn.ds, writes=[gn.r])
                  w0 = sb("w0", [128, 8, 3328], BF16)
                  w0v = e_w_in.rearrange("(kc p) n -> p kc n", p=128)
                  for kc in range(8):
                      k.dma("pool", w0.t[:, kc, :], w0v[:, kc, :], w0.ds, writes=[w0.r])

                  hT = slots("hT", [128, 8, 128], BF16, 2)
                  st_qa = slots("st_qa", [128, 4, 512], BF16, 2)
                  st_ka = slots("st_ka", [128, 4, 512], BF16, 2)
                  st_qb = slots("st_qb", [128, 4, 512], BF16, 2)
                  st_kb = slots("st_kb", [128, 512], BF16, 2)
                  st_va = slots("st_va", [128, 4, 512], BF16, 2)
                  st_vb = slots("st_vb", [128, 4, 128], BF16, 2)
                  st_ga = slots("st_ga", [128, 4, 512], BF16, 2)
                  st_gb = slots("st_gb", [128, 4, 512], BF16, 2)
                  sq = sb("sq", [128, 512], F32)
                  ss8 = sb("ss8", [128, 8], F32)
                  rs8 = sb("rs8", [128, 8], F32)
                  tq = sb("tq", [128, 512], F32)
                  ra = sb("ra", [128, 8, 32], F32); rb = sb("rb", [128, 8, 32], F32)
                  qn = slots("qn", [128, 512], BF16, 2)
                  ge = sb("ge", [128, 512], F32)

                  def normrope(b, nh, gain_ap, blk, dst, dst_r):
                      W = nh * 64
                      k.op("act", lambda: nc.scalar.activation(out=sq.t[:, 0:W], in_=PSF[:, b, 0:W], func=AF.Square),
                           reads=[bank[b]], writes=[sq.r])
                      k.op("dve", lambda: nc.vector.tensor_reduce(out=ss8.t[:, 0:nh], in_=sq.t[:, 0:W].rearrange("p (h d) -> p h d", d=64), axis=AX.X, op=ALU.add),
                           reads=[sq.r], writes=[ss8.r])
                      k.op("act", lambda: nc.scalar.activation(out=rs8.t[:, 0:nh], in_=ss8.t[:, 0:nh], func=AF.Ln, scale=1.0 / 64, bias=EPS), reads=[ss8.r], writes=[rs8.r])
                      k.op("act", lambda: nc.scalar.activation(out=rs8.t[:, 0:nh], in_=rs8.t[:, 0:nh], func=AF.Exp, scale=-0.5), reads=[rs8.r], writes=[rs8.r])
                      t3 = tq.t[:, 0:W].rearrange("p (h d) -> p h d", d=64)
                      k.op("dve", lambda: nc.vector.tensor_tensor(out=t3, in0=PSF[:, b, 0:W].rearrange("p (h d) -> p h d", d=64),
                                                                 in1=rs8.t[:, 0:nh].unsqueeze(2).to_broadcast([128, nh, 64]), op=ALU.mult),
                           reads=[bank[b], rs8.r], writes=[tq.r])
                      k.op("dve", lambda: nc.vector.tensor_tensor(out=t3, in0=t3, in1=gain_ap.unsqueeze(1).to_broadcast([128, nh, 64]), op=ALU.mult),
                           reads=[tq.r, gn.r], writes=[tq.r])
                      cb = cosT.t[:, blk, :].unsqueeze(1).to_broadcast([128, nh, 32])
                      sbn = sinT.t[:, blk, :].unsqueeze(1).to_broadcast([128, nh, 32])
                      x1v, x2v = t3[:, :, 0:32], t3[:, :, 32:64]
                      d3 = dst.rearrange("p (h d) -> p h d", d=64)
                      k.op("dve", lambda: nc.vector.tensor_tensor(out=ra.t[:, 0:nh, :], in0=x1v, in1=cb, op=ALU.mult), reads=[tq.r, cosT.r], writes=[ra.r])
                      k.op("dve", lambda: nc.vector.tensor_tensor(out=rb.t[:, 0:nh, :], in0=x2v, in1=sbn, op=ALU.mult), reads=[tq.r, sinT.r], writes=[rb.r])
                      k.op("dve", lambda: nc.vector.tensor_tensor(out=d3[:, :, 0:32], in0=ra.t[:, 0:nh, :], in1=rb.t[:, 0:nh, :], op=ALU.subtract),
                           reads=[ra.r, rb.r], writes=[dst_r])
                      k.op("dve", lambda: nc.vector.tensor_tensor(out=ra.t[:, 0:nh, :], in0=x2v, in1=cb, op=ALU.mult), reads=[tq.r, cosT.r, ra.r], writes=[ra.r])
                      k.op("dve", lambda: nc.vector.tensor_tensor(out=rb.t[:, 0:nh, :], in0=x1v, in1=sbn, op=ALU.mult), reads=[tq.r, sinT.r, rb.r], writes=[rb.r])
                      k.op("dve", lambda: nc.vector.tensor_tensor(out=d3[:, :, 32:64], in0=ra.t[:, 0:nh, :], in1=rb.t[:, 0:nh, :], op=ALU.add),
                           reads=[ra.r, rb.r], writes=[dst_r])

                  def silu_to(b, dst_ap, dst_r):
                      k.op("act", lambda: nc.scalar.activation(out=ge.t[:], in_=PSF[:, b, :], func=AF.Exp, scale=-1.0), reads=[bank[b]], writes=[ge.r])
                      k.op("dve", lambda: nc.vector.tensor_scalar_add(out=ge.t[:], in0=ge.t[:], scalar1=1.0), reads=[ge.r], writes=[ge.r])
                      k.op("dve", lambda: nc.vector.reciprocal(out=ge.t[:], in_=ge.t[:]), reads=[ge.r], writes=[ge.r])
                      k.op("dve", lambda: nc.vector.tensor_tensor(out=dst_ap, in0=ge.t[:], in1=PSF[:, b, :], op=ALU.mult), reads=[ge.r, bank[b]], writes=[dst_r])

                  xv = x_in.rearrange("(n p) d -> n p d", p=128)
                  k.dma("sp", xt[0].t[:], xv[0], xt[0].ds, writes=[xt[0].r])
                  bsel = 0
                  for T in range(NT):
                      s_ = T % 2
                      for j in range(4):
                          blk = T * 4 + j
                          if blk + 1 < NB:
                              nx = xt[(blk + 1) % 3]
                              k.dma("sp", nx.t[:], xv[blk + 1], nx.ds, writes=[nx.r])
                          xb = xt[blk % 3]
                          hb = htm[blk % 2]
                          norm_mod(xb, hb, gs0, sh0)
                          hTb = hT[blk % 2]
                          transpose8(lambda kc, hb=hb: hb.t[:, kc * 128:(kc + 1) * 128], hb.r, hTb.t[:], hTb.r, eng="act")
                          for ch in range(7):
                              b = bsel
                              bsel = (bsel + 1) % 3
                              c0 = ch * 512
                              W = 512 if ch < 6 else 256
                              for kc in range(8):
                                  k.op("pe", lambda kc=kc, b=b, c0=c0, W=W, hTb=hTb: nc.tensor.matmul(PSF[:, b, 0:W], lhsT=hTb.t[:, kc, :], rhs=w0.t[:, kc, c0:c0 + W],
                                                                                                      start=(kc == 0), stop=(kc == 7)),
                                       reads=[hTb.r, w0.r], writes=[bank[b]], signal=(kc == 7), same_engine=False)
                              if ch in (0, 1, 4):
                                  q = qn[ch % 2]
                                  gain_ap = {0: gn.t[:, 0:64], 1: gn.t[:, 64:128], 4: gn.t[:, 128:192]}[ch]
                                  normrope(b, 8, gain_ap, blk, q.t[:], q.r)
                                  stg = {0: st_qa, 1: st_ka, 4: st_qb}[ch][s_]
                                  for cc in range(4):
                                      k.op("pe", lambda cc=cc, q=q: nc.tensor.transpose(out=PST[:, cc * 128:(cc + 1) * 128], in_=q.t[:, cc * 128:(cc + 1) * 128], identity=ident.t[:]),
                                           reads=[q.r, ident.r], writes=[bankT], signal=(cc == 3), same_engine=False)
                                  k.op("act", lambda stg=stg, j=j: nc.scalar.copy(out=stg.t[:, :, j * 128:(j + 1) * 128], in_=PST[:, 0:512].rearrange("p (c t) -> p c t", c=4)),
                                       reads=[bankT], writes=[stg.r])
                              elif ch == 2:
                                  stg = st_va[s_]
                                  k.op("act", lambda stg=stg, j=j, b=b: nc.scalar.copy(out=stg.t[:, j, :], in_=PSF[:, b, :]), reads=[bank[b]], writes=[stg.r])
                              elif ch in (3, 5):
                                  stg = (st_ga if ch == 3 else st_gb)[s_]
                                  silu_to(b, stg.t[:, j, :], stg.r)
                              else:
                                  q = qn[0]
                                  normrope(b, 2, gn.t[:, 192:256], blk, q.t[:, 0:128], q.r)
                                  k.op("pe", lambda q=q: nc.tensor.transpose(out=PST[:, 0:128], in_=q.t[:, 0:128], identity=ident.t[:]),
                                       reads=[q.r, ident.r], writes=[bankT], same_engine=False)
                                  stg = st_kb[s_]
                                  k.op("act", lambda stg=stg, j=j: nc.scalar.copy(out=stg.t[:, j * 128:(j + 1) * 128], in_=PST[:, 0:128]), reads=[bankT], writes=[stg.r])
                                  stg2 = st_vb[s_]
                                  k.op("act", lambda stg2=stg2, j=j, b=b: nc.scalar.copy(out=stg2.t[:, j, :], in_=PSF[:, b, 128:256]), reads=[bank[b]], writes=[stg2.r])
                      t0 = T * 512
                      for hh_ in range(4):
                          k.dma("pool", QAT[hh_, :, t0:t0 + 512], st_qa[s_].t[:, hh_, :], st_qa[s_].ds, reads=[st_qa[s_].r])
                          k.dma("pool", KAT[hh_, :, t0:t0 + 512], st_ka[s_].t[:, hh_, :], st_ka[s_].ds, reads=[st_ka[s_].r])
                          k.dma("pool", QBT[hh_, :, t0:t0 + 512], st_qb[s_].t[:, hh_, :], st_qb[s_].ds, reads=[st_qb[s_].r])
                      k.dma("pool", KBT[:, t0:t0 + 512], st_kb[s_].t[:], st_kb[s_].ds, reads=[st_kb[s_].r])
                      k.dma("pool", VA[t0:t0 + 512, :].rearrange("(j p) c -> p j c", p=128), st_va[s_].t[:], st_va[s_].ds, reads=[st_va[s_].r])
                      k.dma("pool", VB[t0:t0 + 512, :].rearrange("(j p) c -> p j c", p=128), st_vb[s_].t[:], st_vb[s_].ds, reads=[st_vb[s_].r])
                      k.dma("pool", SGA[t0:t0 + 512, :].rearrange("(j p) c -> p j c", p=128), st_ga[s_].t[:], st_ga[s_].ds, reads=[st_ga[s_].r])
                      k.dma("pool", SGB[t0:t0 + 512, :].rearrange("(j p) c -> p j c", p=128), st_gb[s_].t[:], st_gb[s_].ds, reads=[st_gb[s_].r])

        if dbg == "p0":
            k.barrier()
            return nc, k

        if L0:
            lvl = int(str(dbg)[3:]) if str(dbg).startswith("b0x") else 99
            with (Phase() if lvl == 99 else ExitStack()):
              if lvl == 99:
                  lv = sb("lv", [128, 256], F32)
                  lp = sb("lp", [128, 128], F32)
                  l2 = sb("l2", [128, 2], F32)
                  nlam = sb("nlam", [128, 1], F32)
                  k.dma("sp", lv.t[:], lamv.partition_broadcast(128), lv.ds, writes=[lv.r])
                  lv4 = lv.t[:].rearrange("p (a d) -> p a d", d=64)
                  k.op("dve", lambda: nc.vector.tensor_tensor(out=lp.t[:, 0:64], in0=lv4[:, 0, :], in1=lv4[:, 1, :], op=ALU.mult), reads=[lv.r], writes=[lp.r])
                  k.op("dve", lambda: nc.vector.tensor_tensor(out=lp.t[:, 64:128], in0=lv4[:, 2, :], in1=lv4[:, 3, :], op=ALU.mult), reads=[lv.r, lp.r], writes=[lp.r])
                  k.op("dve", lambda: nc.vector.tensor_reduce(out=l2.t[:], in_=lp.t[:].rearrange("p (a d) -> p a d", d=64), axis=AX.X, op=ALU.add), reads=[lp.r], writes=[l2.r])
                  k.op("act", lambda: nc.scalar.activation(out=l2.t[:], in_=l2.t[:], func=AF.Exp), reads=[l2.r], writes=[l2.r])
                  k.op("dve", lambda: nc.vector.tensor_tensor(out=nlam.t[:], in0=l2.t[:, 1:2], in1=l2.t[:, 0:1], op=ALU.subtract), reads=[l2.r], writes=[nlam.r])
                  k.op("dve", lambda: nc.vector.tensor_scalar_add(out=nlam.t[:], in0=nlam.t[:], scalar1=-0.2), reads=[nlam.r], writes=[nlam.r])
                  gsub = sb("gsub", [128, 128], F32)
                  k.dma("sp", gsub.t[:], subg.partition_broadcast(128), gsub.ds, writes=[gsub.r])
                  k.op("dve", lambda: nc.vector.tensor_scalar_mul(out=gsub.t[:], in0=gsub.t[:], scalar1=0.8), reads=[gsub.r], writes=[gsub.r])
                  KT = sb("KT", [128, S], BF16)
                  QT = sb("QT", [128, S], BF16)
                  V1 = sb("V1", [128, NB, 132], BF16)
                  SGh = sb("SGh", [128, NB, 128], BF16)
                  et = slots("et", [128, 2, 512], BF16, 2)
                  ya = sb("ya", [128, 128], F32)
                  yn = sb("yn", [128, 128], F32)
                  z2 = sb("z2", [128, 2], F32)
                  yst = slots("yst", [128, 4, 128], BF16, 2)
                  k.op("dve", lambda: nc.vector.memset(V1.t[:, :, 128:129], 1.0), writes=[V1.r])
                  accR = [bank[4 + a_ // 3] for a_ in range(8)]

                  def acc_ap(c_, j_, lo, hi):
                      a_ = c_ * 4 + j_
                      return PSF[:, 4 + a_ // 3, (a_ % 3) * 132 + lo:(a_ % 3) * 132 + hi]

                  it = 0
                  for h in range(4):
                      k.dma("sp", KT.t[:], KAT[h], KT.ds, writes=[KT.r])
                      k.dma("sp", QT.t[:], QAT[h], QT.ds, writes=[QT.r])
                      k.dma("sp", V1.t[:, :, 0:128], VA[:, h * 128:(h + 1) * 128].rearrange("(n p) c -> p n c", p=128), V1.ds, writes=[V1.r])
                      k.dma("sp", SGh.t[:], SGA[:, h * 128:(h + 1) * 128].rearrange("(n p) c -> p n c", p=128), SGh.ds, writes=[SGh.r])
                      steps = [(qt, kb) for qt in range(NT) for kb in range(4 * qt + 4)]

                      def stage_s(i):
                          qt, kb = steps[i]
                          pr = i % 2
                          e_ = et[pr]
                          for c_ in range(2):
                              k.op("pe", lambda c_=c_, pr=pr, kb=kb, qt=qt: nc.tensor.matmul(PSF[:, 2 * pr + c_, :], lhsT=KT.t[c_ * 64:(c_ + 1) * 64, kb * 128:(kb + 1) * 128],
                                                                                         rhs=QT.t[c_ * 64:(c_ + 1) * 64, qt * 512:(qt + 1) * 512], start=True, stop=True),
                                   reads=[KT.r, QT.r], writes=[bank[2 * pr + c_]], signal=(c_ == 1), same_engine=False)
                          k.op("act", lambda pr=pr, e_=e_: nc.scalar.activation(out=e_.t[:], in_=PSF[:, 2 * pr:2 * pr + 2, :], func=AF.Exp, scale=0.125),
                               reads=[bank[2 * pr], bank[2 * pr + 1]], writes=[e_.r])
                          r_ = kb - 4 * qt
                          if r_ >= 0:
                              k.op("dve", lambda e_=e_, r_=r_: nc.vector.tensor_tensor(out=e_.t[:, :, r_ * 128:(r_ + 1) * 128], in0=e_.t[:, :, r_ * 128:(r_ + 1) * 128],
                                                                                     in1=tri.t[:].unsqueeze(1).to_broadcast([128, 2, 128]), op=ALU.mult),
                                   reads=[e_.r, tri.r], writes=[e_.r])

                      def stage_pv(i):
                          qt, kb = steps[i]
                          e_ = et[i % 2]
                          ys = yst[qt % 2]
                          r_ = kb - 4 * qt
                          if kb == 0:
                              started.clear()
                          for j_ in range(4):
                              if r_ > j_:
                                  continue
                              last = (kb == 4 * qt + j_)
                              for c_ in range(2):
                                  bk_ = 4 + (c_ * 4 + j_) // 3
                                  st_ = (kb == 0 and bk_ not in started)
                                  started.add(bk_)
                                  k.op("pe", lambda c_=c_, j_=j_, e_=e_, kb=kb, last=last, st_=st_: nc.tensor.matmul(acc_ap(c_, j_, 0, 129), lhsT=e_.t[:, c_, j_ * 128:(j_ + 1) * 128],
                                                                                                       rhs=V1.t[:, kb, 0:129], start=st_, stop=last, skip_group_check=True),
                                       reads=[e_.r, V1.r], writes=[accR[c_ * 4 + j_]], signal=(last or (j_ == 3 and c_ == 1)), same_engine=False)
                              if last:
                                  blk = 4 * qt + j_
                                  a0, a1 = accR[j_], accR[4 + j_]
                                  k.op("dve", lambda j_=j_: nc.vector.tensor_copy(out=z2.t[:, 0:1], in_=acc_ap(0, j_, 128, 129)), reads=[a0], writes=[z2.r])
                                  k.op("dve", lambda j_=j_: nc.vector.tensor_copy(out=z2.t[:, 1:2], in_=acc_ap(1, j_, 128, 129)), reads=[a1, z2.r], writes=[z2.r])
                                  k.op("dve", lambda: nc.vector.reciprocal(out=z2.t[:], in_=z2.t[:]), reads=[z2.r], writes=[z2.r])
                                  k.op("dve", lambda: nc.vector.tensor_tensor(out=z2.t[:, 1:2], in0=z2.t[:, 1:2], in1=nlam.t[:], op=ALU.mult), reads=[z2.r, nlam.r], writes=[z2.r])
                                  k.op("dve", lambda j_=j_: nc.vector.tensor_scalar(out=ya.t[:], in0=acc_ap(0, j_, 0, 128), scalar1=z2.t[:, 0:1], scalar2=None, op0=ALU.mult),
                                       reads=[a0, z2.r], writes=[ya.r])
                                  k.op("dve", lambda j_=j_: nc.vector.scalar_tensor_tensor(out=ya.t[:], in0=acc_ap(1, j_, 0, 128), scalar=z2.t[:, 1:2], in1=ya.t[:], op0=ALU.mult, op1=ALU.add),
                                       reads=[a1, z2.r, ya.r], writes=[ya.r])
                                  k.op("act", lambda: nc.scalar.activation(out=yn.t[:], in_=ya.t[:], func=AF.Square, accum_out=ssq.t[:]), reads=[ya.r], writes=[yn.r, ssq.r])
                                  k.op("act", lambda: nc.scalar.activation(out=rstd.t[:], in_=ssq.t[:], func=AF.Ln, scale=1.0 / 128, bias=EPS), reads=[ssq.r], writes=[rstd.r])
                                  k.op("act", lambda: nc.scalar.activation(out=rstd.t[:], in_=rstd.t[:], func=AF.Exp, scale=-0.5), reads=[rstd.r], writes=[rstd.r])
                                  k.op("dve", lambda: nc.vector.scalar_tensor_tensor(out=yn.t[:], in0=ya.t[:], scalar=rstd.t[:, 0:1], in1=gsub.t[:], op0=ALU.mult, op1=ALU.mult),
                                       reads=[ya.r, rstd.r, gsub.r, yn.r], writes=[yn.r])
                                  k.op("dve", lambda j_=j_, blk=blk, ys=ys: nc.vector.tensor_tensor(out=ys.t[:, j_, :], in0=yn.t[:], in1=SGh.t[:, blk, :], op=ALU.mult),
                                       reads=[yn.r, SGh.r], writes=[ys.r])
                          if kb == 4 * qt + 3:
                              k.dma("pool", Y0[qt * 512:(qt + 1) * 512, h * 128:(h + 1) * 128].rearrange("(j p) c -> p j c", p=128), ys.t[:], ys.ds, reads=[ys.r])

                      started = set()
                      stage_s(0)
                      for i in range(len(steps)):
                          if i + 1 < len(steps):
                              stage_s(i + 1)
                          stage_pv(i)

            if dbg == "a0":
                k.barrier()
                return nc, k
            with Phase():
                esk = sb("esk", [128, 8], F32)
                k.dma("sp", esk.t[:], sinks.partition_broadcast(128), esk.ds, writes=[esk.r])
                k.op("act", lambda: nc.scalar.activation(out=esk.t[:], in_=esk.t[:], func=AF.Exp), reads=[esk.r], writes=[esk.r])
                qbt = slots("qbt", [128, 4, 512], BF16, 2)
                kbt = slots("kbt", [128, 640], BF16, 2)
                vb1 = slots("vb1", [128, 5, 2, 72], BF16, 2)
                sgb = slots("sgb", [128, 4, 512], BF16, 2)
                eb = slots("eb", [128, 16, 128], BF16, 2)
                zz = sb("zz", [128, 8], F32)
                ybt = sb("ybt", [128, 8, 64], F32)
                ysb = slots("ysb", [128, 4, 512], BF16, 2)
                for v_ in vb1:
                    k.op("dve", lambda v_=v_: nc.vector.memset(v_.t[:, :, :, 64:65], 1.0), writes=[v_.r])

                def b0_load(T):
                    s_ = T % 2
                    t0 = T * 512
                    for p_ in range(4):
                        k.dma("sp", qbt[s_].t[:, p_, :], QBT[p_, :, t0:t0 + 512], qbt[s_].ds, writes=[qbt[s_].r])
                    if T > 0:
                        k.dma("sp", kbt[s_].t[:, :], KBT[:, t0 - 128:t0 + 512], kbt[s_].ds, writes=[kbt[s_].r])
                        for g_ in range(2):
                            k.dma("sp", vb1[s_].t[:, :, g_, 0:64], VB[t0 - 128:t0 + 512, g_ * 64:(g_ + 1) * 64].rearrange("(n p) d -> p n d", p=128), vb1[s_].ds, writes=[vb1[s_].r])
                    else:
                        k.dma("sp", kbt[s_].t[:, 128:640], KBT[:, 0:512], kbt[s_].ds, writes=[kbt[s_].r])
                        for g_ in range(2):
                            k.dma("sp", vb1[s_].t[:, 1:5, g_, 0:64], VB[0:512, g_ * 64:(g_ + 1) * 64].rearrange("(n p) d -> p n d", p=128), vb1[s_].ds, writes=[vb1[s_].r])
                    k.dma("sp", sgb[s_].t[:], SGB[t0:t0 + 512, :].rearrange("(j p) c -> p j c", p=128), sgb[s_].ds, writes=[sgb[s_].r])

                PSflat = PSF[:, 0:4, :].rearrange("p a b -> p (a b)")
                b0_load(0)
                for T in range(NT):
                    s_ = T % 2
                    if T + 1 < NT:
                        b0_load(T + 1)
                    for j_ in range(4):
                        if lvl < 2:
                            break
                        blk = 4 * T + j_
                        e_ = eb[blk % 2]
                        kks = (0, 1) if blk > 0 else (1,)
                        for kk in kks:
                            for p_ in range(4):
                                for hf in range(2):
                                    off = (kk * 8 + hf * 4 + p_) * 128
                                    lastmm = (kk == 1 and p_ == 3 and hf == 1)
                                    k.op("pe", lambda off=off, hf=hf, kk=kk, p_=p_, j_=j_, s_=s_: nc.tensor.matmul(
                                        PSflat[:, off:off + 128], lhsT=kbt[s_].t[hf * 64:(hf + 1) * 64, (j_ + kk) * 128:(j_ + kk + 1) * 128],
                                        rhs=qbt[s_].t[hf * 64:(hf + 1) * 64, p_, j_ * 128:(j_ + 1) * 128], start=True, stop=True),
                                         reads=[kbt[s_].r, qbt[s_].r], writes=[bank[0], bank[1], bank[2], bank[3]], signal=lastmm, same_engine=False)
                        for kk in kks:
                            lo = kk * 8
                            k.op("act", lambda lo=lo, e_=e_: nc.scalar.activation(out=e_.t[:, lo:lo + 8, :], in_=PSflat[:, lo * 128:(lo + 8) * 128].rearrange("p (a t) -> p a t", t=128), func=AF.Exp, scale=0.125),
                                 reads=[bank[0], bank[1], bank[2], bank[3]], writes=[e_.r])
                        if lvl < 3:
                            continue
                        if blk > 0:
                            k.op("dve", lambda e_=e_: nc.vector.tensor_tensor(out=e_.t[:, 0:8, :], in0=e_.t[:, 0:8, :], in1=upp.t[:].unsqueeze(1).to_broadcast([128, 8, 128]), op=ALU.mult),
                                 reads=[e_.r, upp.r], writes=[e_.r])
                        k.op("dve", lambda e_=e_: nc.vector.tensor_tensor(out=e_.t[:, 8:16, :], in0=e_.t[:, 8:16, :], in1=tri.t[:].unsqueeze(1).to_broadcast([128, 8, 128]), op=ALU.mult),
                             reads=[e_.r, tri.r], writes=[e_.r])
                        if lvl < 4:
                            continue
                        for p_ in range(4):
                            for hf in range(2):
                                hd = hf * 4 + p_
                                for kk in kks:
                                    k.op("pe", lambda hd=hd, hf=hf, kk=kk, p_=p_, j_=j_, s_=s_, e_=e_, kks=kks: nc.tensor.matmul(
                                        PSF[:, 4 + hd // 4, (hd % 4) * 80:(hd % 4) * 80 + 65], lhsT=e_.t[:, kk * 8 + hf * 4 + p_, :],
                                        rhs=vb1[s_].t[:, j_ + kk, hf, 0:65], start=(kk == kks[0]), stop=(kk == 1)),
                                         reads=[e_.r, vb1[s_].r], writes=[bank[4], bank[5]], signal=(kk == 1 and p_ == 3 and hf == 1), same_engine=False)
                        if lvl < 5:
                            continue
                        for bb in range(2):
                            k.op("dve", lambda bb=bb: nc.vector.tensor_copy(out=zz.t[:, bb * 4:(bb + 1) * 4], in_=PSF[:, 4 + bb, 0:320].rearrange("p (h d) -> p h d", d=80)[:, :, 64]),
                                 reads=[bank[4], bank[5], zz.r], writes=[zz.r])
                        k.op("dve", lambda: nc.vector.tensor_tensor(out=zz.t[:], in0=zz.t[:], in1=esk.t[:], op=ALU.add), reads=[zz.r, esk.r], writes=[zz.r])
                        k.op("dve", lambda: nc.vector.reciprocal(out=zz.t[:], in_=zz.t[:]), reads=[zz.r], writes=[zz.r])
                        for bb in range(2):
                            k.op("dve", lambda bb=bb: nc.vector.tensor_tensor(out=ybt.t[:, bb * 4:(bb + 1) * 4, :], in0=PSF[:, 4 + bb, 0:320].rearrange("p (h d) -> p h d", d=80)[:, :, 0:64],
                                                                            in1=zz.t[:, bb * 4:(bb + 1) * 4].unsqueeze(2).to_broadcast([128, 4, 64]), op=ALU.mult),
                                 reads=[bank[4], bank[5], zz.r, ybt.r], writes=[ybt.r])
                        k.op("dve", lambda j_=j_, s_=s_: nc.vector.tensor_tensor(out=ysb[s_].t[:, j_, :], in0=ybt.t[:].rearrange("p h d -> p (h d)"), in1=sgb[s_].t[:, j_, :], op=ALU.mult),
                             reads=[ybt.r, sgb[s_].r], writes=[ysb[s_].r])
                    if lvl >= 6:
                        k.dma("pool", Y0[T * 512:(T + 1) * 512, 512:1024].rearrange("(j p) c -> p j c", p=128), ysb[s_].t[:], ysb[s_].ds, reads=[ysb[s_].r])

            if dbg == "b0" or lvl != 99:
                k.barrier()
                return nc, k
            with Phase():
                wo = sb("wo", [128, 8, D], BF16)
                k.dma("pool", wo.t[:], e_w_out.rearrange("(kc p) n -> p kc n", p=128), wo.ds, writes=[wo.r])
                yb_ = slots("yb_", [128, D], BF16, 2)
                yT = slots("yT", [128, 8, 128], BF16, 2)
                x1t = slots("x1t", [128, D], F32, 2)
                xv = x_in.rearrange("(n p) d -> n p d", p=128)
                x1v = x1.rearrange("(n p) d -> n p d", p=128)
                y0v = Y0.rearrange("(n p) d -> n p d", p=128)
                k.dma("sp", xt[0].t[:], xv[0], xt[0].ds, writes=[xt[0].r])
                k.dma("sp", yb_[0].t[:], y0v[0], yb_[0].ds, writes=[yb_[0].r])
                for blk in range(NB):
                    if blk + 1 < NB:
                        k.dma("sp", xt[(blk + 1) % 3].t[:], xv[blk + 1], xt[(blk + 1) % 3].ds, writes=[xt[(blk + 1) % 3].r])
                        k.dma("sp", yb_[(blk + 1) % 2].t[:], y0v[blk + 1], yb_[(blk + 1) % 2].ds, writes=[yb_[(blk + 1) % 2].r])
                    xb, yb, yTb, xo = xt[blk % 3], yb_[blk % 2], yT[blk % 2], x1t[blk % 2]
                    transpose8(lambda kc, yb=yb: yb.t[:, kc * 128:(kc + 1) * 128], yb.r, yTb.t[:], yTb.r, eng="act")
                    for cc in range(2):
                        b = cc
                        for kc in range(8):
                            k.op("pe", lambda kc=kc, b=b, cc=cc, yTb=yTb: nc.tensor.matmul(PSF[:, b, :], lhsT=yTb.t[:, kc, :], rhs=wo.t[:, kc, cc * 512:(cc + 1) * 512],
                                                                                       start=(kc == 0), stop=(kc == 7)),
                                 reads=[yTb.r, wo.r], writes=[bank[b]], signal=(kc == 7), same_engine=False)
                        k.op("dve", lambda b=b, cc=cc: nc.vector.tensor_tensor(out=junk.t[:, cc * 512:(cc + 1) * 512], in0=PSF[:, b, :], in1=gate0.t[:, cc * 512:(cc + 1) * 512], op=ALU.mult),
                             reads=[bank[b], gate0.r, junk.r], writes=[junk.r])
                        k.op("dve", lambda cc=cc, xo=xo, xb=xb: nc.vector.tensor_tensor(out=xo.t[:, cc * 512:(cc + 1) * 512], in0=junk.t[:, cc * 512:(cc + 1) * 512],
                                                                                   in1=xb.t[:, cc * 512:(cc + 1) * 512], op=ALU.add),
                             reads=[junk.r, xb.r, xo.r], writes=[xo.r])
                    k.dma("pool", x1v[blk], xo.t[:], xo.ds, reads=[xo.r])

        if dbg == "l0" or not L1:
            k.barrier()
            return nc, k

        sh1, gs1, gate1 = mod_vectors("1", o_w_mod, o_b_mod, o_norm_g)
        H1T = dram_scr("H1T", [8, 128, S], BF16)
        H1OT = dram_scr("H1OT", [8, 128, SO], BF16)
        X1O = dram_scr("X1O", [SO, D], F32)
        Y1T = dram_scr("Y1T", [16, 64, SO], BF16)
        om = sb("om", [128, 2], F32)
        k.dma("sp", om.t[:], ownm[:, :], om.ds, writes=[om.r])
        x1v = x1.rearrange("(n p) d -> n p d", p=128)
        x1ov = X1O.rearrange("(n p) d -> n p d", p=128)

        with Phase():
            st_h = slots("st_h", [128, 8, 512], BF16, 2)
            st_ho = slots("st_ho", [128, 8, 256], BF16, 2)
            hown = sb("hown", [128, D], BF16)
            hof = sb("hof", [128, D], F32)
            xo = slots("xo", [128, D], F32, 2)
            k.dma("sp", xt[0].t[:], x1v[0], xt[0].ds, writes=[xt[0].r])
            for T in range(NT):
                s_ = T % 2
                for j in range(4):
                    blk = T * 4 + j
                    if blk + 1 < NB:
                        nx = xt[(blk + 1) % 3]
                        k.dma("sp", nx.t[:], x1v[blk + 1], nx.ds, writes=[nx.r])
                    xb, hb = xt[blk % 3], htm[blk % 2]
                    norm_mod(xb, hb, gs1, sh1)
                    transpose8(lambda kc, hb=hb: hb.t[:, kc * 128:(kc + 1) * 128], hb.r, st_h[s_].t[:, :, j * 128:(j + 1) * 128], st_h[s_].r, eng="act")
                    if j % 2 == 1:
                        he, xe = htm[(blk - 1) % 2], xt[(blk - 1) % 3]
                        ob = blk // 2
                        k.op("dve", lambda he=he: nc.vector.tensor_scalar(out=hof.t[:], in0=he.t[:], scalar1=om.t[:, 0:1], scalar2=None, op0=ALU.mult),
                             reads=[he.r, om.r, hof.r], writes=[hof.r])
                        k.op("dve", lambda hb=hb: nc.vector.scalar_tensor_tensor(out=hown.t[:], in0=hb.t[:], scalar=om.t[:, 1:2], in1=hof.t[:], op0=ALU.mult, op1=ALU.add),
                             reads=[hb.r, om.r, hof.r, hown.r], writes=[hown.r])
                        transpose8(lambda kc: hown.t[:, kc * 128:(kc + 1) * 128], hown.r, st_ho[s_].t[:, :, (j // 2) * 128:(j // 2 + 1) * 128], st_ho[s_].r, eng="act")
                        xo_ = xo[ob % 2]
                        k.op("dve", lambda xe=xe: nc.vector.tensor_scalar(out=junk.t[:], in0=xe.t[:], scalar1=om.t[:, 0:1], scalar2=None, op0=ALU.mult),
                             reads=[xe.r, om.r, junk.r], writes=[junk.r])
                        k.op("dve", lambda xb=xb, xo_=xo_: nc.vector.scalar_tensor_tensor(out=xo_.t[:], in0=xb.t[:], scalar=om.t[:, 1:2], in1=junk.t[:], op0=ALU.mult, op1=ALU.add),
                             reads=[xb.r, om.r, junk.r, xo_.r], writes=[xo_.r])
                        k.dma("pool", x1ov[ob], xo_.t[:], xo_.ds, reads=[xo_.r])
                k.dma("pool", H1T[:, :, T * 512:(T + 1) * 512].rearrange("k p t -> p k t"), st_h[s_].t[:], st_h[s_].ds, reads=[st_h[s_].r])
                k.dma("pool", H1OT[:, :, T * 256:(T + 1) * 256].rearrange("k p t -> p k t"), st_ho[s_].t[:], st_ho[s_].ds, reads=[st_ho[s_].r])

        w1v = o_w_in.rearrange("(kc p) n -> p kc n", p=128)
        for g in range(4):
            with Phase():
                KTg = sb("KTg", [128, 2, S], BF16)
                Vg = sb("Vg", [128, NB, 256], BF16)
                QTg = sb("QTg", [128, 2, SO], BF16)
                SGT = sb("SGT", [64, 4, SO], BF16)
                with Phase():
                    wq = sb("wq", [128, 8, 256], BF16); wk = sb("wk", [128, 8, 256], BF16)
                    wv = sb("wv", [128, 8, 256], BF16); wg = sb("wg", [128, 8, 256], BF16)
                    for wt_, off in ((wq, 0), (wk, 1024), (wv, 2048), (wg, 3072)):
                        k.dma("pool", wt_.t[:], w1v[:, :, off + g * 256:off + (g + 1) * 256], wt_.ds, writes=[wt_.r])
                    h1t = slots("h1t", [128, 8, 512], BF16, 2)
                    ge1 = sb("ge1", [64, 512], F32)
                    k.dma("sp", h1t[0].t[:], H1T[:, :, 0:512].rearrange("k p t -> p k t"), h1t[0].ds, writes=[h1t[0].r])
                    bs = 0
                    for T in range(NT):
                        ht = h1t[T % 2]
                        if T + 1 < NT:
                            k.dma("sp", h1t[(T + 1) % 2].t[:], H1T[:, :, (T + 1) * 512:(T + 2) * 512].rearrange("k p t -> p k t"), h1t[(T + 1) % 2].ds, writes=[h1t[(T + 1) % 2].r])
                        for p_ in range(2):
                            b = bs; bs = (bs + 1) % 4
                            for kc in range(8):
                                k.op("pe", lambda kc=kc, b=b, p_=p_, ht=ht: nc.tensor.matmul(PSF[:, b, :], lhsT=wk.t[:, kc, p_ * 128:(p_ + 1) * 128], rhs=ht.t[:, kc, :], start=(kc == 0), stop=(kc == 7)),
                                     reads=[wk.r, ht.r], writes=[bank[b]], signal=(kc == 7), same_engine=False)
                            k.op("act", lambda b=b, p_=p_, T=T: nc.scalar.copy(out=KTg.t[:, p_, T * 512:(T + 1) * 512], in_=PSF[:, b, :]), reads=[bank[b]], writes=[KTg.r])
                        for j in range(4):
                            b = bs; bs = (bs + 1) % 4
                            for kc in range(8):
                                k.op("pe", lambda kc=kc, b=b, j=j, ht=ht: nc.tensor.matmul(PSF[:, b, 0:256], lhsT=ht.t[:, kc, j * 128:(j + 1) * 128], rhs=wv.t[:, kc, :], start=(kc == 0), stop=(kc == 7)),
                                     reads=[wv.r, ht.r], writes=[bank[b]], signal=(kc == 7), same_engine=False)
                            k.op("dve", lambda b=b, j=j, T=T: nc.vector.tensor_copy(out=Vg.t[:, T * 4 + j, :], in_=PSF[:, b, 0:256]), reads=[bank[b]], writes=[Vg.r])
                    k.dma("sp", h1t[0].t[:], H1OT[:, :, 0:512].rearrange("k p t -> p k t"), h1t[0].ds, writes=[h1t[0].r])
                    for TO in range(NTO):
                        ht = h1t[TO % 2]
                        if TO + 1 < NTO:
                            k.dma("sp", h1t[(TO + 1) % 2].t[:], H1OT[:, :, (TO + 1) * 512:(TO + 2) * 512].rearrange("k p t -> p k t"), h1t[(TO + 1) % 2].ds, writes=[h1t[(TO + 1) % 2].r])
                        for p_ in range(2):
                            b = bs; bs = (bs + 1) % 4
                            for kc in range(8):
                                k.op("pe", lambda kc=kc, b=b, p_=p_, ht=ht: nc.tensor.matmul(PSF[:, b, :], lhsT=wq.t[:, kc, p_ * 128:(p_ + 1) * 128], rhs=ht.t[:, kc, :], start=(kc == 0), stop=(kc == 7)),
                                     reads=[wq.r, ht.r], writes=[bank[b]], signal=(kc == 7), same_engine=False)
                            k.op("act", lambda b=b, p_=p_, TO=TO: nc.scalar.mul(out=QTg.t[:, p_, TO * 512:(TO + 1) * 512], in_=PSF[:, b, :], mul=0.125), reads=[bank[b]], writes=[QTg.r])
                        for hl in range(4):
                            b = bs; bs = (bs + 1) % 4
                            for kc in range(8):
                                k.op("pe", lambda kc=kc, b=b, hl=hl, ht=ht: nc.tensor.matmul(PSF[0:64, b, :], lhsT=wg.t[:, kc, hl * 64:(hl + 1) * 64], rhs=ht.t[:, kc, :], start=(kc == 0), stop=(kc == 7)),
                                     reads=[wg.r, ht.r], writes=[bank[b]], signal=(kc == 7), same_engine=False)
                            k.op("act", lambda b=b: nc.scalar.activation(out=ge1.t[:], in_=PSF[0:64, b, :], func=AF.Exp, scale=-1.0), reads=[bank[b]], writes=[ge1.r])
                            k.op("dve", lambda: nc.vector.tensor_scalar_add(out=ge1.t[:], in0=ge1.t[:], scalar1=1.0), reads=[ge1.r], writes=[ge1.r])
                            k.op("dve", lambda: nc.vector.reciprocal(out=ge1.t[:], in_=ge1.t[:]), reads=[ge1.r], writes=[ge1.r])
                            k.op("dve", lambda b=b, hl=hl, TO=TO: nc.vector.tensor_tensor(out=SGT.t[:, hl, TO * 512:(TO + 1) * 512], in0=ge1.t[:], in1=PSF[0:64, b, :], op=ALU.mult),
                                 reads=[ge1.r, bank[b]], writes=[SGT.r])
                with Phase():
                    dm = sb("dm", [128, 8, 512], BF16)
                    k.dma("pool", dm.t[:], dmask.rearrange("p (r q) -> p r q", r=8), dm.ds, writes=[dm.r])
                    e32 = slots("e32", [128, 2, 512], F32, 2)
                    sp_ = slots("sp_", [128, 2, 512], BF16, 2)
                    ex = [slots("ex%d" % i_, [128, 512], F32, 2) for i_ in range(2)]
                    ww = [slots("ww%d" % i_, [128, 512], BF16, 2) for i_ in range(2)]
                    yst1 = slots("yst1", [64, 2, 512], BF16, 2)
                    fin = 0
                    for m in range(NTO):
                        for p_ in range(2):
                            kbs = list(range(8 * m + 7, -1, -1))

                            def stage1(kb, i):
                                sl = i % 2
                                for hf in range(2):
                                    k.op("pe", lambda hf=hf, kb=kb: nc.tensor.matmul(PSF[:, hf, :], lhsT=KTg.t[hf * 64:(hf + 1) * 64, p_, kb * 128:(kb + 1) * 128],
                                                                                   rhs=QTg.t[hf * 64:(hf + 1) * 64, p_, m * 512:(m + 1) * 512], start=True, stop=True),
                                         reads=[KTg.r, QTg.r], writes=[bank[hf]], signal=(hf == 1), same_engine=False)
                                k.op("act", lambda sl=sl: nc.scalar.activation(out=e32[sl].t[:], in_=PSF[:, 0:2, :], func=AF.Exp), reads=[bank[0], bank[1]], writes=[e32[sl].r])
                                k.op("act", lambda sl=sl: nc.scalar.activation(out=sp_[sl].t[:], in_=e32[sl].t[:], func=AF.Ln, bias=1.0), reads=[e32[sl].r], writes=[sp_[sl].r])
                                r_ = kb - 8 * m
                                if r_ >= 0:
                                    mk = dm.t[:, r_, :].unsqueeze(1).to_broadcast([128, 2, 512])
                                    k.op("dve", lambda sl=sl, mk=mk: nc.vector.tensor_tensor(out=sp_[sl].t[:], in0=sp_[sl].t[:], in1=mk, op=ALU.mult), reads=[sp_[sl].r, dm.r], writes=[sp_[sl].r])
                                    k.op("dve", lambda sl=sl, mk=mk: nc.vector.tensor_tensor(out=e32[sl].t[:], in0=e32[sl].t[:], in1=mk, op=ALU.mult), reads=[e32[sl].r, dm.r], writes=[e32[sl].r])

                            def stage2(kb, i, first, last):
                                sl = i % 2
                                for hf in range(2):
                                    k.op("pe", lambda hf=hf, sl=sl: nc.tensor.matmul(PSF[:, 2 + hf, :], lhsT=negtri.t[:], rhs=sp_[sl].t[:, hf, :], start=first, stop=True, skip_group_check=True),
                                         reads=[negtri.r, sp_[sl].r], writes=[bank[2 + hf]], same_engine=False)
                                for hf in range(2):
                                    k.op("act", lambda sl=sl, hf=hf: nc.scalar.activation(out=ex[hf][sl].t[:], in_=PSF[:, 2 + hf, :], func=AF.Exp), reads=[bank[2 + hf]], writes=[ex[hf][sl].r])
                                    k.op("dve", lambda sl=sl, hf=hf: nc.vector.tensor_tensor(out=ww[hf][sl].t[:], in0=e32[sl].t[:, hf, :], in1=ex[hf][sl].t[:], op=ALU.mult),
                                         reads=[e32[sl].r, ex[hf][sl].r], writes=[ww[hf][sl].r])
                                if not last:
                                    for hf in range(2):
                                        k.op("pe", lambda hf=hf, sl=sl: nc.tensor.matmul(PSF[:, 2 + hf, :], lhsT=negrest.t[:], rhs=sp_[sl].t[:, hf, :], start=False, stop=True, skip_group_check=True),
                                             reads=[negrest.r, sp_[sl].r], writes=[bank[2 + hf]], same_engine=False)
                                for hf in range(2):
                                    k.op("pe", lambda hf=hf, sl=sl, kb=kb: nc.tensor.matmul(PSF[0:64, 4 + hf, :], lhsT=Vg.t[:, kb, (p_ * 2 + hf) * 64:(p_ * 2 + hf + 1) * 64], rhs=ww[hf][sl].t[:],
                                                                                          start=first, stop=last),
                                         reads=[Vg.r, ww[hf][sl].r], writes=[bank[4 + hf]], same_engine=False)

                            stage1(kbs[0], 0)
                            for i, kb in enumerate(kbs):
                                if i + 1 < len(kbs):
                                    stage1(kbs[i + 1], i + 1)
                                stage2(kb, i, i == 0, kb == 0)
                            ys = yst1[fin % 2]
                            fin += 1
                            for hf in range(2):
                                hl = p_ * 2 + hf
                                k.op("dve", lambda hf=hf, hl=hl, ys=ys: nc.vector.tensor_tensor(out=ys.t[:, hf, :], in0=PSF[0:64, 4 + hf, :], in1=SGT.t[:, hl, m * 512:(m + 1) * 512], op=ALU.mult),
                                     reads=[bank[4 + hf], SGT.r, ys.r], writes=[ys.r])
                            for hf in range(2):
                                k.dma("pool", Y1T[g * 4 + p_ * 2 + hf, :, m * 512:(m + 1) * 512], ys.t[:, hf, :], ys.ds, reads=[ys.r])

        with Phase():
            wo1 = sb("wo1", [64, 16, D], BF16)
            k.dma("pool", wo1.t[:], o_w_out.rearrange("(h p) n -> p h n", p=64), wo1.ds, writes=[wo1.r])
            yt1 = slots("yt1", [64, 16, 128], BF16, 2)
            outt = slots("outt", [128, D], F32, 2)
            outv = out_o.rearrange("(n p) d -> n p d", p=128)
            NOB = SO // 128
            k.dma("sp", xt[0].t[:], x1ov[0], xt[0].ds, writes=[xt[0].r])
            k.dma("sp", yt1[0].t[:], Y1T[:, :, 0:128].rearrange("h p t -> p h t"), yt1[0].ds, writes=[yt1[0].r])
            for ob in range(NOB):
                if ob + 1 < NOB:
                    k.dma("sp", xt[(ob + 1) % 3].t[:], x1ov[ob + 1], xt[(ob + 1) % 3].ds, writes=[xt[(ob + 1) % 3].r])
                    k.dma("sp", yt1[(ob + 1) % 2].t[:], Y1T[:, :, (ob + 1) * 128:(ob + 2) * 128].rearrange("h p t -> p h t"), yt1[(ob + 1) % 2].ds, writes=[yt1[(ob + 1) % 2].r])
                xb, yt, oo = xt[ob % 3], yt1[ob % 2], outt[ob % 2]
                for cc in range(2):
                    b = cc
                    for h in range(16):
                        k.op("pe", lambda h=h, b=b, cc=cc, yt=yt: nc.tensor.matmul(PSF[:, b, :], lhsT=yt.t[:, h, :], rhs=wo1.t[:, h, cc * 512:(cc + 1) * 512], start=(h == 0), stop=(h == 15)),
                             reads=[yt.r, wo1.r], writes=[bank[b]], signal=(h == 15), same_engine=False)
                    k.op("dve", lambda b=b, cc=cc: nc.vector.tensor_tensor(out=junk.t[:, cc * 512:(cc + 1) * 512], in0=PSF[:, b, :], in1=gate1.t[:, cc * 512:(cc + 1) * 512], op=ALU.mult),
                         reads=[bank[b], gate1.r, junk.r], writes=[junk.r])
                    k.op("dve", lambda cc=cc, oo=oo, xb=xb: nc.vector.tensor_tensor(out=oo.t[:, cc * 512:(cc + 1) * 512], in0=junk.t[:, cc * 512:(cc + 1) * 512], in1=xb.t[:, cc * 512:(cc + 1) * 512], op=ALU.add),
                         reads=[junk.r, xb.r, oo.r], writes=[oo.r])
                k.dma("pool", outv[ob], oo.t[:], oo.ds, reads=[oo.r])
        k.barrier()
    return nc, k


def _consts():
    s = np.arange(128)[:, None]
    t = np.arange(128)[None, :]
    ident = (s == t).astype(np.float32)
    tri = (s <= t).astype(np.float32)
    upp = (s > t).astype(np.float32)
    negtri = -(s >= t).astype(np.float32)
    inv = (10000.0 ** (-np.arange(32, dtype=np.float32) / np.float32(32))).astype(np.float32)
    return np.concatenate([ident, tri, upp, negtri, np.broadcast_to(inv[None, :], (128, 32))], axis=1).astype(np.float32)


def host_inputs(inp, b, hh, S, layers=(0, 1)):
    f = lambda a: np.ascontiguousarray(np.asarray(a), dtype=np.float32)
    m = {"cT": f(np.asarray(inp["c"])[b].reshape(8, 128).T), "consts": _consts()}
    if 0 in layers:
        m["x"] = f(np.asarray(inp["x"])[b, :S])
        m["posT"] = np.ascontiguousarray(np.asarray(inp["positions"])[b, :S].reshape(S // 128, 128).T.astype(np.int32))
        m["e_norm_g"] = f(inp["even_norm_g"]).reshape(1, D)
        m["e_w_mod"] = f(inp["even_w_mod"])[0]
        m["e_b_mod"] = f(inp["even_b_mod"]).reshape(1, 3 * D)
        w = f(inp["even_w_in"])[0]
        qa, ka, va, ga, qb, kb, vb, gb = np.split(w, np.cumsum([512, 512, 512, 512, 512, 128, 128])[:], axis=1)
        qbp = qb.reshape(D, 2, 4, 64).transpose(0, 2, 1, 3).reshape(D, 512)
        m["e_w_in"] = np.ascontiguousarray(np.concatenate([qa, ka, va, ga, qbp, gb, kb, vb], axis=1))
        m["e_w_out"] = f(inp["even_w_out"])[0]
        m["gains"] = np.concatenate([f(inp["a_q_gain"])[0], f(inp["a_k_gain"])[0], f(inp["b_q_gain"])[0], f(inp["b_k_gain"])[0]]).reshape(1, 256)
        m["lamv"] = np.concatenate([f(inp["a_lambda_q1"])[0], f(inp["a_lambda_k1"])[0], f(inp["a_lambda_q2"])[0], f(inp["a_lambda_k2"])[0]]).reshape(1, 256)
        m["subg"] = f(inp["a_subln_g"]).reshape(1, 128)
        m["sinks"] = f(inp["b_sinks"]).reshape(1, 8)
    if 1 in layers:
        m["o_norm_g"] = f(inp["odd_norm_g"]).reshape(1, D)
        m["o_w_mod"] = f(inp["odd_w_mod"])[0]
        m["o_b_mod"] = f(inp["odd_b_mod"]).reshape(1, 3 * D)
        m["o_w_in"] = f(inp["odd_w_in"])[0]
        m["o_w_out"] = f(inp["odd_w_out"])[0]
        m["ownm"] = np.ascontiguousarray(np.broadcast_to(np.array([[1.0 - hh, float(hh)]], np.float32), (128, 2)))
        s = np.arange(128)[:, None, None, None]
        r = np.arange(8)[None, :, None, None]
        jj = np.arange(4)[None, None, :, None]
        tq = np.arange(128)[None, None, None, :]
        g = 2 * jj + hh
        msk = ((r < g) | ((r == g) & (s < tq))).astype(np.float32)
        m["dmask"] = np.ascontiguousarray(msk.reshape(128, 8 * 512))
    return m


_CACHE = {}


def kernel(**inputs):
    S = 8192
    if "nc" not in _CACHE:
        _CACHE["nc"] = build(S=S, layers=(0, 1))[0]
    nc = _CACHE["nc"]
    in_maps = [host_inputs(inputs, c // 2, c % 2, S) for c in range(8)]
    res = run_bass_kernel_spmd(nc, in_maps, core_ids=list(range(8)))
    out = np.empty((4, S, D), np.float32)
    for c in range(8):
        b, hh = c // 2, c % 2
        o = np.asarray(res.results[c]["out"]).reshape(S // 256, 128, D)
        out[b].reshape(S // 256, 2, 128, D)[:, hh] = o
    return out
```

```python
import math
from contextlib import ExitStack
import numpy as np
import concourse.bass as bass
import concourse.mybir as mybir
from concourse.bass_utils import run_bass_kernel_spmd

F32 = mybir.dt.float32
BF16 = mybir.dt.bfloat16
I32 = mybir.dt.int32
AF = mybir.ActivationFunctionType
ALU = mybir.AluOpType
AX = mybir.AxisListType

D = 1024
EPS = 1e-6
PI = math.pi


class Res:
    __slots__ = ("name", "lw", "rd")

    def __init__(self, name=""):
        self.name = name
        self.lw = None
        self.rd = {}


class SemObj:
    __slots__ = ("sem", "count", "name")

    def __init__(self, sem, name):
        self.sem = sem
        self.count = 0
        self.name = name


class KB:
    def __init__(self, nc, stack):
        self.nc = nc
        self.stack = stack
        self.engs = {"pe": nc.tensor, "act": nc.scalar, "dve": nc.vector, "pool": nc.gpsimd, "sp": nc.sync}
        self.so = {}
        for k in self.engs:
            s = stack.enter_context(nc.semaphore("prog_" + k))
            self.so[k] = SemObj(s, k)
        self.waited = {k: {} for k in self.engs}
        self.n_inst = 0
        self.all_so = list(self.so.values())
        self.free_dma = {}

    def new_dma_sem(self, name, q="sp"):
        if self.free_dma.setdefault(q, []):
            return self.free_dma[q].pop()
        s = self.stack.enter_context(self.nc.semaphore("dma_" + name))
        so = SemObj(s, name)
        self.all_so.append(so)
        return so

    def barrier(self):
        for e in self.engs:
            self._wait(e, [(so, so.count) for so in self.all_so if so is not self.so[e]])

    def _wait(self, e, deps):
        w = self.waited[e]
        for so, val in deps:
            if val <= 0 or w.get(so, 0) >= val:
                continue
            self.engs[e].wait_ge(so.sem, val)
            w[so] = val

    def _deps(self, e, reads, writes, same_engine):
        me = self.so[e]
        deps = []
        for r in reads:
            if r.lw is not None and (same_engine or r.lw[0] is not me):
                deps.append(r.lw)
        for r in writes:
            if r.lw is not None and (same_engine or r.lw[0] is not me):
                deps.append(r.lw)
            for so, v in r.rd.items():
                if same_engine or so is not me:
                    deps.append((so, v))
        return deps

    def op(self, e, fn, reads=(), writes=(), signal=True, same_engine=True):
        me = self.so[e]
        self._wait(e, self._deps(e, reads, writes, same_engine))
        ins = fn()
        self.n_inst += 1
        if signal:
            me.count += 1
            ins.then_inc(me.sem, 1)
            ev = (me, me.count)
        else:
            ev = (me, me.count + 1)
        for r in reads:
            if r.rd.get(me, 0) < ev[1]:
                r.rd[me] = ev[1]
        for r in writes:
            r.lw = ev
            r.rd = {}
        return ins

    def dma(self, q, out_ap, in_ap, dsem, reads=(), writes=(), **kw):
        if isinstance(dsem, Buf):
            dsem = dsem.get_ds(q)
        self._wait(q, self._deps(q, reads, writes, True))
        ins = self.engs[q].dma_start(out=out_ap, in_=in_ap, **kw)
        self.n_inst += 1
        dsem.count += 16
        ins.then_inc(dsem.sem, 16)
        ev = (dsem, dsem.count)
        for r in reads:
            if r.rd.get(dsem, 0) < ev[1]:
                r.rd[dsem] = ev[1]
        for r in writes:
            r.lw = ev
            r.rd = {}
        return ins


class Buf:
    def __init__(self, t, name, k):
        self.t = t
        self.r = Res(name)
        self.name = name
        self._k = k
        self._ds = {}

    @property
    def ds(self):
        return self

    def get_ds(self, q):
        if q not in self._ds:
            self._ds[q] = self._k.new_dma_sem(self.name + q, q)
        return self._ds[q]


def build(S=8192, layers=(0, 1), dbg=False):
    NB = S // 128
    NT = S // 512
    SO = S // 2
    NTO = SO // 512
    nc = bass.Bass("TRN2", target_bir_lowering=False)
    dram_in = lambda n, sh, dt=F32: nc.dram_tensor(n, sh, dt, kind="ExternalInput").ap()
    dram_out = lambda n, sh, dt=F32: nc.dram_tensor(n, sh, dt, kind="ExternalOutput").ap()
    dram_scr = lambda n, sh, dt: nc.dram_tensor(n, sh, dt, kind=("ExternalOutput" if dbg else "Internal")).ap()

    L0 = 0 in layers
    L1 = 1 in layers
    cT = dram_in("cT", [128, 8])
    consts = dram_in("consts", [128, 128 * 4 + 32])
    if L0:
        x_in = dram_in("x", [S, D])
        posT = dram_in("posT", [128, NB], I32)
        e_norm_g = dram_in("e_norm_g", [1, D]); e_w_mod = dram_in("e_w_mod", [D, 3 * D]); e_b_mod = dram_in("e_b_mod", [1, 3 * D])
        e_w_in = dram_in("e_w_in", [D, 3328]); e_w_out = dram_in("e_w_out", [D, D])
        gains = dram_in("gains", [1, 4 * 64])
        lamv = dram_in("lamv", [1, 4 * 64])
        subg = dram_in("subg", [1, 128])
        sinks = dram_in("sinks", [1, 8])
    if L1:
        o_norm_g = dram_in("o_norm_g", [1, D]); o_w_mod = dram_in("o_w_mod", [D, 3 * D]); o_b_mod = dram_in("o_b_mod", [1, 3 * D])
        o_w_in = dram_in("o_w_in", [D, 4 * D]); o_w_out = dram_in("o_w_out", [D, D])
        ownm = dram_in("ownm", [128, 2])
        dmask = dram_in("dmask", [128, 8 * 512])
        out_o = dram_out("out", [SO, D])
    if L0 and L1:
        x1 = dram_scr("x1", [S, D], F32)
    elif L0:
        x1 = dram_out("x1", [S, D])
    else:
        x1 = dram_in("x1", [S, D])

    with ExitStack() as st:
        k = KB(nc, st)

        cur = [st]
        bufs_of = {id(st): []}

        uid = [0]

        def sb(name, shape, dt):
            uid[0] += 1
            name = f"{name}_{uid[0]}"
            bf_ = Buf(cur[0].enter_context(nc.sbuf_tensor(name, shape, dt)), name, k)
            bufs_of[id(cur[0])].append(bf_)
            return bf_

        def slots(name, shape, dt, n):
            return [sb(f"{name}{i}", shape, dt) for i in range(n)]

        class Phase:
            def __enter__(self_p):
                self_p.prev = cur[0]
                self_p.stk = ExitStack()
                cur[0] = self_p.stk
                bufs_of[id(self_p.stk)] = []
                return self_p

            def __exit__(self_p, *a):
                k.barrier()
                for bf_ in bufs_of.pop(id(self_p.stk)):
                    for q_, so_ in bf_._ds.items():
                        k.free_dma.setdefault(q_, []).append(so_)
                    bf_._ds = {}
                self_p.stk.close()
                cur[0] = self_p.prev
                return False

        PSF = st.enter_context(nc.psum_tensor("psf", [128, 7, 512], F32))
        PST = st.enter_context(nc.psum_tensor("pst", [128, 1024], BF16))
        bank = [Res(f"bank{i}") for i in range(7)]
        bankT = Res("bankT")

        cst32 = sb("cst32", [128, 128 * 4 + 32], F32)
        ident = sb("ident", [128, 128], BF16)
        tri = sb("tri", [128, 128], BF16)
        upp = sb("upp", [128, 128], BF16)
        negtri = sb("negtri", [128, 128], BF16)
        negrest = sb("negrest", [128, 128], BF16)
        k.dma("sp", cst32.t[:], consts[:, :], cst32.ds, writes=[cst32.r])
        for i, tdst in enumerate((ident, tri, upp, negtri)):
            k.op("dve", lambda tdst=tdst, i=i: nc.vector.tensor_copy(out=tdst.t[:], in_=cst32.t[:, i * 128:(i + 1) * 128]),
                 reads=[cst32.r], writes=[tdst.r])
        k.op("dve", lambda: nc.vector.tensor_scalar(out=negrest.t[:], in0=cst32.t[:, 384:512], scalar1=-1.0, scalar2=-1.0,
                                                   op0=ALU.mult, op1=ALU.add), reads=[cst32.r], writes=[negrest.r])
        invf = cst32.t[:, 512:544]

        sc = sb("sc", [128, 8], F32)
        sce = sb("sce", [128, 8], F32)
        k.dma("sp", sc.t[:], cT[:, :], sc.ds, writes=[sc.r])
        k.op("act", lambda: nc.scalar.activation(out=sce.t[:], in_=sc.t[:], func=AF.Exp, scale=-1.0), reads=[sc.r], writes=[sce.r])
        k.op("dve", lambda: nc.vector.tensor_scalar_add(out=sce.t[:], in0=sce.t[:], scalar1=1.0), reads=[sce.r], writes=[sce.r])
        k.op("dve", lambda: nc.vector.reciprocal(out=sce.t[:], in_=sce.t[:]), reads=[sce.r], writes=[sce.r])
        k.op("dve", lambda: nc.vector.tensor_tensor(out=sc.t[:], in0=sc.t[:], in1=sce.t[:], op=ALU.mult), reads=[sc.r, sce.r], writes=[sc.r])
        screp = sb("screp", [128, 8, 128], F32)
        k.op("dve", lambda: nc.vector.tensor_copy(out=screp.t[:], in_=sc.t[:].unsqueeze(2).to_broadcast([128, 8, 128])),
             reads=[sc.r], writes=[screp.r])

        def mod_vectors(tag, w_mod, b_mod, norm_g):
            shift = sb("shift" + tag, [128, D], F32)
            gs = sb("gs" + tag, [128, D], F32)
            gate = sb("gate" + tag, [128, D], F32)
            with Phase():
                _mod_vectors(w_mod, b_mod, norm_g, shift, gs, gate)
            return shift, gs, gate

        def _mod_vectors(w_mod, b_mod, norm_g, shift, gs, gate):
            wm = slots("wm", [128, 8, 512], F32, 2)
            bmod = sb("bmod", [128, 3 * D], F32)
            gbc = sb("gbc", [128, D], F32)
            k.dma("sp", bmod.t[:], b_mod.partition_broadcast(128), bmod.ds, writes=[bmod.r])
            k.dma("sp", gbc.t[:], norm_g.partition_broadcast(128), gbc.ds, writes=[gbc.r])
            wv = w_mod.rearrange("(kc p) n -> p kc n", p=128)
            for cc in range(6):
                w = wm[cc % 2]
                k.dma("sp", w.t[:], wv[:, :, cc * 512:(cc + 1) * 512], w.ds, writes=[w.r])
                b = cc % 2
                for kc in range(8):
                    k.op("pe", lambda kc=kc, b=b, w=w: nc.tensor.matmul(PSF[:, b, :], lhsT=screp.t[:, kc, :], rhs=w.t[:, kc, :],
                                                                        start=(kc == 0), stop=(kc == 7)),
                         reads=[screp.r, w.r], writes=[bank[b]], signal=(kc == 7), same_engine=False)
                seg, off = cc // 2, (cc % 2) * 512
                bsl = bmod.t[:, cc * 512:(cc + 1) * 512]
                if seg == 0:
                    k.op("dve", lambda b=b, off=off, bsl=bsl: nc.vector.tensor_tensor(out=shift.t[:, off:off + 512], in0=PSF[:, b, :], in1=bsl, op=ALU.add),
                         reads=[bank[b], bmod.r], writes=[shift.r])
                elif seg == 1:
                    k.op("dve", lambda b=b, off=off, bsl=bsl: nc.vector.scalar_tensor_tensor(out=gs.t[:, off:off + 512], in0=PSF[:, b, :], scalar=1.0, in1=bsl,
                                                                                            op0=ALU.add, op1=ALU.add),
                         reads=[bank[b], bmod.r], writes=[gs.r])
                    k.op("dve", lambda off=off: nc.vector.tensor_tensor(out=gs.t[:, off:off + 512], in0=gs.t[:, off:off + 512], in1=gbc.t[:, off:off + 512], op=ALU.mult),
                         reads=[gs.r, gbc.r], writes=[gs.r])
                else:
                    k.op("dve", lambda b=b, off=off, bsl=bsl: nc.vector.tensor_tensor(out=gate.t[:, off:off + 512], in0=PSF[:, b, :], in1=bsl, op=ALU.add),
                         reads=[bank[b], bmod.r], writes=[gate.r])
            return shift, gs, gate

        xt = slots("xt", [128, D], F32, 3)
        junk = sb("junk", [128, D], F32)
        ssq = sb("ssq", [128, 1], F32)
        rstd = sb("rstd", [128, 1], F32)
        htm = slots("htm", [128, D], BF16, 2)

        def norm_mod(xb, hb, gs, shift):
            k.op("act", lambda: nc.scalar.activation(out=junk.t[:], in_=xb.t[:], func=AF.Square, accum_out=ssq.t[:]),
                 reads=[xb.r], writes=[junk.r, ssq.r])
            k.op("act", lambda: nc.scalar.activation(out=rstd.t[:], in_=ssq.t[:], func=AF.Ln, scale=1.0 / D, bias=EPS), reads=[ssq.r], writes=[rstd.r])
            k.op("act", lambda: nc.scalar.activation(out=rstd.t[:], in_=rstd.t[:], func=AF.Exp, scale=-0.5), reads=[rstd.r], writes=[rstd.r])
            k.op("dve", lambda: nc.vector.scalar_tensor_tensor(out=junk.t[:], in0=xb.t[:], scalar=rstd.t[:, 0:1], in1=gs.t[:], op0=ALU.mult, op1=ALU.mult),
                 reads=[xb.r, rstd.r, gs.r, junk.r], writes=[junk.r])
            k.op("dve", lambda: nc.vector.tensor_tensor(out=hb.t[:], in0=junk.t[:], in1=shift.t[:], op=ALU.add),
                 reads=[junk.r, shift.r], writes=[hb.r])

        def transpose8(src_ap_fn, src_r, dst_ap, dst_r, eng="act"):
            for kc in range(8):
                k.op("pe", lambda kc=kc: nc.tensor.transpose(out=PST[:, kc * 128:(kc + 1) * 128], in_=src_ap_fn(kc), identity=ident.t[:]),
                     reads=[src_r, ident.r], writes=[bankT], signal=(kc == 7), same_engine=False)
            src3 = PST[:, :].rearrange("p (a b) -> p a b", a=8)
            if eng == "act":
                k.op("act", lambda: nc.scalar.copy(out=dst_ap, in_=src3), reads=[bankT], writes=[dst_r])
            else:
                k.op("dve", lambda: nc.vector.tensor_copy(out=dst_ap, in_=src3), reads=[bankT], writes=[dst_r])

        if L0:
            sh0, gs0, gate0 = mod_vectors("0", e_w_mod, e_b_mod, e_norm_g)
            QAT = dram_scr("QAT", [4, 128, S], BF16)
            KAT = dram_scr("KAT", [4, 128, S], BF16)
            VA = dram_scr("VA", [S, 512], BF16)
            SGA = dram_scr("SGA", [S, 512], BF16)
            QBT = dram_scr("QBT", [4, 128, S], BF16)
            KBT = dram_scr("KBT", [128, S], BF16)
            VB = dram_scr("VB", [S, 128], BF16)
            SGB = dram_scr("SGB", [S, 512], BF16)
            Y0 = dram_scr("Y0", [S, D], BF16)

            lvl = int(str(dbg)[3:]) if str(dbg).startswith("b0x") else 99
            with (Phase() if lvl == 99 else ExitStack()):
              if lvl == 99:
                  cosT = sb("cosT", [128, NB, 32], F32)
                  sinT = sb("sinT", [128, NB, 32], F32)
                  with Phase():
                      posi = sb("posi", [128, NB], I32)
                      posf = sb("posf", [128, NB], F32)
                      ang = sb("ang", [128, NB, 32], F32)
                      k.dma("sp", posi.t[:], posT[:, :], posi.ds, writes=[posi.r])
                      k.op("dve", lambda: nc.vector.tensor_copy(out=posf.t[:], in_=posi.t[:]), reads=[posi.r], writes=[posf.r])
                      k.op("dve", lambda: nc.vector.tensor_tensor(out=ang.t[:], in0=posf.t[:].unsqueeze(2).to_broadcast([128, NB, 32]),
                                                                 in1=invf.unsqueeze(1).to_broadcast([128, NB, 32]), op=ALU.mult),
                           reads=[posf.r, cst32.r], writes=[ang.r])
                      negpi = sb("negpi", [128, 1], F32)
                      k.op("dve", lambda: nc.vector.memset(negpi.t[:], -PI), writes=[negpi.r])
                      ui = sb("ui", [128, NB, 32], I32)
                      uf = sb("uf", [128, NB, 32], F32)
                      for tbl, c0 in ((sinT, 0.5), (cosT, 0.75)):
                          k.op("dve", lambda tbl=tbl, c0=c0: nc.vector.tensor_scalar(out=tbl.t[:], in0=ang.t[:], scalar1=1.0 / (2 * PI), scalar2=c0, op0=ALU.mult, op1=ALU.add),
                               reads=[ang.r], writes=[tbl.r])
                          k.op("dve", lambda tbl=tbl: nc.vector.tensor_copy(out=ui.t[:], in_=tbl.t[:]), reads=[tbl.r, ui.r], writes=[ui.r])
                          k.op("dve", lambda: nc.vector.tensor_copy(out=uf.t[:], in_=ui.t[:]), reads=[ui.r, uf.r], writes=[uf.r])
                          k.op("dve", lambda tbl=tbl: nc.vector.tensor_tensor(out=tbl.t[:], in0=tbl.t[:], in1=uf.t[:], op=ALU.subtract), reads=[tbl.r, uf.r], writes=[tbl.r])
                          k.op("dve", lambda tbl=tbl: nc.vector.tensor_single_scalar(out=uf.t[:], in_=tbl.t[:], scalar=0.0, op=ALU.is_lt), reads=[tbl.r, uf.r], writes=[uf.r])
                          k.op("dve", lambda tbl=tbl: nc.vector.tensor_tensor(out=tbl.t[:], in0=tbl.t[:], in1=uf.t[:], op=ALU.add), reads=[tbl.r, uf.r], writes=[tbl.r])
                          k.op("act", lambda tbl=tbl: nc.scalar.activation(out=tbl.t[:], in_=tbl.t[:], func=AF.Sin, bias=negpi.t[:, 0:1], scale=2 * PI),
                               reads=[tbl.r, negpi.r], writes=[tbl.r])
                  gn = sb("gn", [128, 4 * 64], F32)
                  k.dma("sp", gn.t[:], gains.partition_broadcast(128), gn.ds, writes=[gn.r])
                  w0 = sb("w0", [128, 8, 3328], BF16)
                  w0v = e_w_in.rearrange("(kc p) n -> p kc n", p=128)
                  for kc in range(8):
                      k.dma("pool", w0.t[:, kc, :], w0v[:, kc, :], w0.ds, writes=[w0.r])

                  hT = slots("hT", [128, 8, 128], BF16, 2)
                  st_qa = slots("st_qa", [128, 4, 512], BF16, 2)
                  st_ka = slots("st_ka", [128, 4, 512], BF16, 2)
                  st_qb = slots("st_qb", [128, 4, 512], BF16, 2)
                  st_kb = slots("st_kb", [128, 512], BF16, 2)
                  st_va = slots("st_va", [128, 4, 512], BF16, 2)
                  st_vb = slots("st_vb", [128, 4, 128], BF16, 2)
                  st_ga = slots("st_ga", [128, 4, 512], BF16, 2)
                  st_gb = slots("st_gb", [128, 4, 512], BF16, 2)
                  scr = []
                  for i_ in range(2):
                      scr.append(dict(sq=sb("sq", [128, 512], F32), ss8=sb("ss8", [128, 8], F32), rs8=sb("rs8", [128, 8], F32), tq=sb("tq", [128, 512], F32),
                                      ra=sb("ra", [128, 8, 32], F32), rb=sb("rb", [128, 8, 32], F32), rc=sb("rc", [128, 8, 32], F32), rd=sb("rd", [128, 8, 32], F32),
                                      ge=sb("ge", [128, 512], F32), qn=sb("qn", [128, 512], BF16)))

                  def interleave(*gens):
                      gens = list(gens)
                      while gens:
                          for g_ in list(gens):
                              try:
                                  next(g_)
                              except StopIteration:
                                  gens.remove(g_)

                  def normrope(si, b, nh, gain_ap, blk, stg_fn):
                      W = nh * 64
                      sc_ = scr[si]
                      sq, ss8, rs8, tq, ra, rb, rc, rd, q = (sc_[n_] for n_ in ("sq", "ss8", "rs8", "tq", "ra", "rb", "rc", "rd", "qn"))
                      k.op("act", lambda: nc.scalar.activation(out=sq.t[:, 0:W], in_=PSF[:, b, 0:W], func=AF.Square),
                           reads=[bank[b]], writes=[sq.r])
                      yield
                      k.op("dve", lambda: nc.vector.tensor_reduce(out=ss8.t[:, 0:nh], in_=sq.t[:, 0:W].rearrange("p (h d) -> p h d", d=64), axis=AX.X, op=ALU.add),
                           reads=[sq.r], writes=[ss8.r])
                      yield
                      k.op("act", lambda: nc.scalar.activation(out=rs8.t[:, 0:nh], in_=ss8.t[:, 0:nh], func=AF.Ln, scale=1.0 / 64, bias=EPS), reads=[ss8.r], writes=[rs8.r])
                      k.op("act", lambda: nc.scalar.activation(out=rs8.t[:, 0:nh], in_=rs8.t[:, 0:nh], func=AF.Exp, scale=-0.5), reads=[rs8.r], writes=[rs8.r])
                      yield
                      t3 = tq.t[:, 0:W].rearrange("p (h d) -> p h d", d=64)
                      k.op("dve", lambda: nc.vector.tensor_tensor(out=t3, in0=PSF[:, b, 0:W].rearrange("p (h d) -> p h d", d=64),
                                                                 in1=rs8.t[:, 0:nh].unsqueeze(2).to_broadcast([128, nh, 64]), op=ALU.mult),
                           reads=[bank[b], rs8.r], writes=[tq.r])
                      yield
                      k.op("dve", lambda: nc.vector.tensor_tensor(out=t3, in0=t3, in1=gain_ap.unsqueeze(1).to_broadcast([128, nh, 64]), op=ALU.mult),
                           reads=[tq.r, gn.r], writes=[tq.r])
                      yield
                      cb = cosT.t[:, blk, :].unsqueeze(1).to_broadcast([128, nh, 32])
                      sbn = sinT.t[:, blk, :].unsqueeze(1).to_broadcast([128, nh, 32])
                      x1v, x2v = t3[:, :, 0:32], t3[:, :, 32:64]
                      d3 = q.t[:, 0:W].rearrange("p (h d) -> p h d", d=64)
                      k.op("dve", lambda: nc.vector.tensor_tensor(out=ra.t[:, 0:nh, :], in0=x1v, in1=cb, op=ALU.mult), reads=[tq.r, cosT.r], writes=[ra.r])
                      yield
                      k.op("dve", lambda: nc.vector.tensor_tensor(out=rb.t[:, 0:nh, :], in0=x2v, in1=sbn, op=ALU.mult), reads=[tq.r, sinT.r], writes=[rb.r])
                      yield
                      k.op("dve", lambda: nc.vector.tensor_tensor(out=rc.t[:, 0:nh, :], in0=x2v, in1=cb, op=ALU.mult), reads=[tq.r, cosT.r], writes=[rc.r])
                      yield
                      k.op("dve", lambda: nc.vector.tensor_tensor(out=rd.t[:, 0:nh, :], in0=x1v, in1=sbn, op=ALU.mult), reads=[tq.r, sinT.r], writes=[rd.r])
                      yield
                      k.op("dve", lambda: nc.vector.tensor_tensor(out=d3[:, :, 0:32], in0=ra.t[:, 0:nh, :], in1=rb.t[:, 0:nh, :], op=ALU.subtract),
                           reads=[ra.r, rb.r], writes=[q.r])
                      yield
                      k.op("dve", lambda: nc.vector.tensor_tensor(out=d3[:, :, 32:64], in0=rc.t[:, 0:nh, :], in1=rd.t[:, 0:nh, :], op=ALU.add),
                           reads=[rc.r, rd.r], writes=[q.r])
                      yield
                      ncc = W // 128
                      for cc in range(ncc):
                          k.op("pe", lambda cc=cc: nc.tensor.transpose(out=PST[:, cc * 128:(cc + 1) * 128], in_=q.t[:, cc * 128:(cc + 1) * 128], identity=ident.t[:]),
                               reads=[q.r, ident.r], writes=[bankT], signal=(cc == ncc - 1), same_engine=False)
                      stg_fn()
                      yield

                  def silu_to(si, b, dst_ap, dst_r):
                      ge = scr[si]["ge"]
                      k.op("act", lambda: nc.scalar.activation(out=ge.t[:], in_=PSF[:, b, :], func=AF.Exp, scale=-1.0), reads=[bank[b]], writes=[ge.r])
                      yield
                      k.op("dve", lambda: nc.vector.tensor_scalar_add(out=ge.t[:], in0=ge.t[:], scalar1=1.0), reads=[ge.r], writes=[ge.r])
                      yield
                      k.op("dve", lambda: nc.vector.reciprocal(out=ge.t[:], in_=ge.t[:]), reads=[ge.r], writes=[ge.r])
                      yield
                      k.op("dve", lambda: nc.vector.tensor_tensor(out=dst_ap, in0=ge.t[:], in1=PSF[:, b, :], op=ALU.mult), reads=[ge.r, bank[b]], writes=[dst_r])
                      yield

                  xv = x_in.rearrange("(n p) d -> n p d", p=128)
                  k.dma("sp", xt[0].t[:], xv[0], xt[0].ds, writes=[xt[0].r])
                  bsel = 0
                  for T in range(NT):
                      s_ = T % 2
                      for j in range(4):
                          blk = T * 4 + j
                          if blk + 1 < NB:
                              nx = xt[(blk + 1) % 3]
                              k.dma("sp", nx.t[:], xv[blk + 1], nx.ds, writes=[nx.r])
                          xb = xt[blk % 3]
                          hb = htm[blk % 2]
                          norm_mod(xb, hb, gs0, sh0)
                          hTb = hT[blk % 2]
                          transpose8(lambda kc, hb=hb: hb.t[:, kc * 128:(kc + 1) * 128], hb.r, hTb.t[:], hTb.r, eng="act")
                          for ch in range(7):
                              b = ch
                              c0 = ch * 512
                              W = 512 if ch < 6 else 256
                              for kc in range(8):
                                  k.op("pe", lambda kc=kc, b=b, c0=c0, W=W, hTb=hTb: nc.tensor.matmul(PSF[:, b, 0:W], lhsT=hTb.t[:, kc, :], rhs=w0.t[:, kc, c0:c0 + W],
                                                                                                      start=(kc == 0), stop=(kc == 7)),
                                       reads=[hTb.r, w0.r], writes=[bank[b]], signal=(kc == 7), same_engine=False)

                          def stg4(stg, j=j):
                              return lambda: k.op("act", lambda: nc.scalar.copy(out=stg.t[:, :, j * 128:(j + 1) * 128], in_=PST[:, 0:512].rearrange("p (c t) -> p c t", c=4)),
                                                  reads=[bankT], writes=[stg.r])

                          def stg1(stg, j=j):
                              return lambda: k.op("act", lambda: nc.scalar.copy(out=stg.t[:, j * 128:(j + 1) * 128], in_=PST[:, 0:128]), reads=[bankT], writes=[stg.r])

                          interleave(normrope(0, 0, 8, gn.t[:, 0:64], blk, stg4(st_qa[s_])), normrope(1, 1, 8, gn.t[:, 64:128], blk, stg4(st_ka[s_])))
                          k.op("act", lambda j=j: nc.scalar.copy(out=st_va[s_].t[:, j, :], in_=PSF[:, 2, :]), reads=[bank[2]], writes=[st_va[s_].r])
                          k.op("act", lambda j=j: nc.scalar.copy(out=st_vb[s_].t[:, j, :], in_=PSF[:, 6, 128:256]), reads=[bank[6]], writes=[st_vb[s_].r])
                          interleave(silu_to(0, 3, st_ga[s_].t[:, j, :], st_ga[s_].r), silu_to(1, 5, st_gb[s_].t[:, j, :], st_gb[s_].r))
                          interleave(normrope(0, 4, 8, gn.t[:, 128:192], blk, stg4(st_qb[s_])), normrope(1, 6, 2, gn.t[:, 192:256], blk, stg1(st_kb[s_])))
                      t0 = T * 512
                      for hh_ in range(4):
                          k.dma("pool", QAT[hh_, :, t0:t0 + 512], st_qa[s_].t[:, hh_, :], st_qa[s_].ds, reads=[st_qa[s_].r])
                          k.dma("pool", KAT[hh_, :, t0:t0 + 512], st_ka[s_].t[:, hh_, :], st_ka[s_].ds, reads=[st_ka[s_].r])
                          k.dma("pool", QBT[hh_, :, t0:t0 + 512], st_qb[s_].t[:, hh_, :], st_qb[s_].ds, reads=[st_qb[s_].r])
                      k.dma("pool", KBT[:, t0:t0 + 512], st_kb[s_].t[:], st_kb[s_].ds, reads=[st_kb[s_].r])
                      k.dma("pool", VA[t0:t0 + 512, :].rearrange("(j p) c -> p j c", p=128), st_va[s_].t[:], st_va[s_].ds, reads=[st_va[s_].r])
                      k.dma("pool", VB[t0:t0 + 512, :].rearrange("(j p) c -> p j c", p=128), st_vb[s_].t[:], st_vb[s_].ds, reads=[st_vb[s_].r])
                      k.dma("pool", SGA[t0:t0 + 512, :].rearrange("(j p) c -> p j c", p=128), st_ga[s_].t[:], st_ga[s_].ds, reads=[st_ga[s_].r])
                      k.dma("pool", SGB[t0:t0 + 512, :].rearrange("(j p) c -> p j c", p=128), st_gb[s_].t[:], st_gb[s_].ds, reads=[st_gb[s_].r])

        if dbg == "p0":
            k.barrier()
            return nc, k

        if L0:
            lvl = int(str(dbg)[3:]) if str(dbg).startswith("b0x") else 99
            with (Phase() if lvl == 99 else ExitStack()):
              if lvl == 99:
                  lv = sb("lv", [128, 256], F32)
                  lp = sb("lp", [128, 128], F32)
                  l2 = sb("l2", [128, 2], F32)
                  nlam = sb("nlam", [128, 1], F32)
                  k.dma("sp", lv.t[:], lamv.partition_broadcast(128), lv.ds, writes=[lv.r])
                  lv4 = lv.t[:].rearrange("p (a d) -> p a d", d=64)
                  k.op("dve", lambda: nc.vector.tensor_tensor(out=lp.t[:, 0:64], in0=lv4[:, 0, :], in1=lv4[:, 1, :], op=ALU.mult), reads=[lv.r], writes=[lp.r])
                  k.op("dve", lambda: nc.vector.tensor_tensor(out=lp.t[:, 64:128], in0=lv4[:, 2, :], in1=lv4[:, 3, :], op=ALU.mult), reads=[lv.r, lp.r], writes=[lp.r])
                  k.op("dve", lambda: nc.vector.tensor_reduce(out=l2.t[:], in_=lp.t[:].rearrange("p (a d) -> p a d", d=64), axis=AX.X, op=ALU.add), reads=[lp.r], writes=[l2.r])
                  k.op("act", lambda: nc.scalar.activation(out=l2.t[:], in_=l2.t[:], func=AF.Exp), reads=[l2.r], writes=[l2.r])
                  k.op("dve", lambda: nc.vector.tensor_tensor(out=nlam.t[:], in0=l2.t[:, 1:2], in1=l2.t[:, 0:1], op=ALU.subtract), reads=[l2.r], writes=[nlam.r])
                  k.op("dve", lambda: nc.vector.tensor_scalar_add(out=nlam.t[:], in0=nlam.t[:], scalar1=-0.2), reads=[nlam.r], writes=[nlam.r])
                  gsub = sb("gsub", [128, 128], F32)
                  k.dma("sp", gsub.t[:], subg.partition_broadcast(128), gsub.ds, writes=[gsub.r])
                  k.op("dve", lambda: nc.vector.tensor_scalar_mul(out=gsub.t[:], in0=gsub.t[:], scalar1=0.8), reads=[gsub.r], writes=[gsub.r])
                  KT = sb("KT", [128, S], BF16)
                  QT = sb("QT", [128, S], BF16)
                  V1 = sb("V1", [128, NB, 132], BF16)
                  SGh = sb("SGh", [128, NB, 128], BF16)
                  et = slots("et", [128, 2, 512], BF16, 2)
                  ya = sb("ya", [128, 128], F32)
                  yn = sb("yn", [128, 128], F32)
                  z2 = sb("z2", [128, 2], F32)
                  yst = slots("yst", [128, 4, 128], BF16, 2)
                  k.op("dve", lambda: nc.vector.memset(V1.t[:, :, 128:129], 1.0), writes=[V1.r])
                  accR = [bank[4 + a_ // 3] for a_ in range(8)]

                  def acc_ap(c_, j_, lo, hi):
                      a_ = c_ * 4 + j_
                      return PSF[:, 4 + a_ // 3, (a_ % 3) * 132 + lo:(a_ % 3) * 132 + hi]

                  it = 0
                  for h in range(4):
                      k.dma("sp", KT.t[:], KAT[h], KT.ds, writes=[KT.r])
                      k.dma("sp", QT.t[:], QAT[h], QT.ds, writes=[QT.r])
                      k.dma("sp", V1.t[:, :, 0:128], VA[:, h * 128:(h + 1) * 128].rearrange("(n p) c -> p n c", p=128), V1.ds, writes=[V1.r])
                      k.dma("sp", SGh.t[:], SGA[:, h * 128:(h + 1) * 128].rearrange("(n p) c -> p n c", p=128), SGh.ds, writes=[SGh.r])
                      steps = [(qt, kb) for qt in range(NT) for kb in range(4 * qt + 4)]

                      def stage_s(i):
                          qt, kb = steps[i]
                          pr = i % 2
                          e_ = et[pr]
                          for c_ in range(2):
                              k.op("pe", lambda c_=c_, pr=pr, kb=kb, qt=qt: nc.tensor.matmul(PSF[:, 2 * pr + c_, :], lhsT=KT.t[c_ * 64:(c_ + 1) * 64, kb * 128:(kb + 1) * 128],
                                                                                         rhs=QT.t[c_ * 64:(c_ + 1) * 64, qt * 512:(qt + 1) * 512], start=True, stop=True),
                                   reads=[KT.r, QT.r], writes=[bank[2 * pr + c_]], signal=(c_ == 1), same_engine=False)
                          k.op("act", lambda pr=pr, e_=e_: nc.scalar.activation(out=e_.t[:], in_=PSF[:, 2 * pr:2 * pr + 2, :], func=AF.Exp, scale=0.125),
                               reads=[bank[2 * pr], bank[2 * pr + 1]], writes=[e_.r])
                          r_ = kb - 4 * qt
                          if r_ >= 0:
                              k.op("dve", lambda e_=e_, r_=r_: nc.vector.tensor_tensor(out=e_.t[:, :, r_ * 128:(r_ + 1) * 128], in0=e_.t[:, :, r_ * 128:(r_ + 1) * 128],
                                                                                     in1=tri.t[:].unsqueeze(1).to_broadcast([128, 2, 128]), op=ALU.mult),
                                   reads=[e_.r, tri.r], writes=[e_.r])

                      def stage_pv(i):
                          qt, kb = steps[i]
                          e_ = et[i % 2]
                          ys = yst[qt % 2]
                          r_ = kb - 4 * qt
                          if kb == 0:
                              started.clear()
                          for j_ in range(4):
                              if r_ > j_:
                                  continue
                              last = (kb == 4 * qt + j_)
                              for c_ in range(2):
                                  bk_ = 4 + (c_ * 4 + j_) // 3
                                  st_ = (kb == 0 and bk_ not in started)
                                  started.add(bk_)
                                  k.op("pe", lambda c_=c_, j_=j_, e_=e_, kb=kb, last=last, st_=st_: nc.tensor.matmul(acc_ap(c_, j_, 0, 129), lhsT=e_.t[:, c_, j_ * 128:(j_ + 1) * 128],
                                                                                                       rhs=V1.t[:, kb, 0:129], start=st_, stop=last, skip_group_check=True),
                                       reads=[e_.r, V1.r], writes=[accR[c_ * 4 + j_]], signal=(last or (j_ == 3 and c_ == 1)), same_engine=False)
                              if last:
                                  blk = 4 * qt + j_
                                  a0, a1 = accR[j_], accR[4 + j_]
                                  k.op("dve", lambda j_=j_: nc.vector.tensor_copy(out=z2.t[:, 0:1], in_=acc_ap(0, j_, 128, 129)), reads=[a0], writes=[z2.r])
                                  k.op("dve", lambda j_=j_: nc.vector.tensor_copy(out=z2.t[:, 1:2], in_=acc_ap(1, j_, 128, 129)), reads=[a1, z2.r], writes=[z2.r])
                                  k.op("dve", lambda: nc.vector.reciprocal(out=z2.t[:], in_=z2.t[:]), reads=[z2.r], writes=[z2.r])
                                  k.op("dve", lambda: nc.vector.tensor_tensor(out=z2.t[:, 1:2], in0=z2.t[:, 1:2], in1=nlam.t[:], op=ALU.mult), reads=[z2.r, nlam.r], writes=[z2.r])
                                  k.op("dve", lambda j_=j_: nc.vector.tensor_scalar(out=ya.t[:], in0=acc_ap(0, j_, 0, 128), scalar1=z2.t[:, 0:1], scalar2=None, op0=ALU.mult),
                                       reads=[a0, z2.r], writes=[ya.r])
                                  k.op("dve", lambda j_=j_: nc.vector.scalar_tensor_tensor(out=ya.t[:], in0=acc_ap(1, j_, 0, 128), scalar=z2.t[:, 1:2], in1=ya.t[:], op0=ALU.mult, op1=ALU.add),
                                       reads=[a1, z2.r, ya.r], writes=[ya.r])
                                  k.op("act", lambda: nc.scalar.activation(out=yn.t[:], in_=ya.t[:], func=AF.Square, accum_out=ssq.t[:]), reads=[ya.r], writes=[yn.r, ssq.r])
                                  k.op("act", lambda: nc.scalar.activation(out=rstd.t[:], in_=ssq.t[:], func=AF.Ln, scale=1.0 / 128, bias=EPS), reads=[ssq.r], writes=[rstd.r])
                                  k.op("act", lambda: nc.scalar.activation(out=rstd.t[:], in_=rstd.t[:], func=AF.Exp, scale=-0.5), reads=[rstd.r], writes=[rstd.r])
                                  k.op("dve", lambda: nc.vector.scalar_tensor_tensor(out=yn.t[:], in0=ya.t[:], scalar=rstd.t[:, 0:1], in1=gsub.t[:], op0=ALU.mult, op1=ALU.mult),
                                       reads=[ya.r, rstd.r, gsub.r, yn.r], writes=[yn.r])
                                  k.op("dve", lambda j_=j_, blk=blk, ys=ys: nc.vector.tensor_tensor(out=ys.t[:, j_, :], in0=yn.t[:], in1=SGh.t[:, blk, :], op=ALU.mult),
                                       reads=[yn.r, SGh.r], writes=[ys.r])
                          if kb == 4 * qt + 3:
                              k.dma("pool", Y0[qt * 512:(qt + 1) * 512, h * 128:(h + 1) * 128].rearrange("(j p) c -> p j c", p=128), ys.t[:], ys.ds, reads=[ys.r])

                      started = set()
                      stage_s(0)
                      for i in range(len(steps)):
                          if i + 1 < len(steps):
                              stage_s(i + 1)
                          stage_pv(i)

            if dbg == "a0":
                k.barrier()
                return nc, k
            with Phase():
                esk = sb("esk", [128, 8], F32)
                k.dma("sp", esk.t[:], sinks.partition_broadcast(128), esk.ds, writes=[esk.r])
                k.op("act", lambda: nc.scalar.activation(out=esk.t[:], in_=esk.t[:], func=AF.Exp), reads=[esk.r], writes=[esk.r])
                qbt = slots("qbt", [128, 4, 512], BF16, 2)
                kbt = slots("kbt", [128, 640], BF16, 2)
                vb1 = slots("vb1", [128, 5, 2, 72], BF16, 2)
                sgb = slots("sgb", [128, 4, 512], BF16, 2)
                eb = slots("eb", [128, 16, 128], BF16, 2)
                zz = sb("zz", [128, 8], F32)
                ybt = sb("ybt", [128, 8, 64], F32)
                ysb = slots("ysb", [128, 4, 512], BF16, 2)
                for v_ in vb1:
                    k.op("dve", lambda v_=v_: nc.vector.memset(v_.t[:, :, :, 64:65], 1.0), writes=[v_.r])

                def b0_load(T):
                    s_ = T % 2
                    t0 = T * 512
                    for p_ in range(4):
                        k.dma("sp", qbt[s_].t[:, p_, :], QBT[p_, :, t0:t0 + 512], qbt[s_].ds, writes=[qbt[s_].r])
                    if T > 0:
                        k.dma("sp", kbt[s_].t[:, :], KBT[:, t0 - 128:t0 + 512], kbt[s_].ds, writes=[kbt[s_].r])
                        for g_ in range(2):
                            k.dma("sp", vb1[s_].t[:, :, g_, 0:64], VB[t0 - 128:t0 + 512, g_ * 64:(g_ + 1) * 64].rearrange("(n p) d -> p n d", p=128), vb1[s_].ds, writes=[vb1[s_].r])
                    else:
                        k.dma("sp", kbt[s_].t[:, 128:640], KBT[:, 0:512], kbt[s_].ds, writes=[kbt[s_].r])
                        for g_ in range(2):
                            k.dma("sp", vb1[s_].t[:, 1:5, g_, 0:64], VB[0:512, g_ * 64:(g_ + 1) * 64].rearrange("(n p) d -> p n d", p=128), vb1[s_].ds, writes=[vb1[s_].r])
                    k.dma("sp", sgb[s_].t[:], SGB[t0:t0 + 512, :].rearrange("(j p) c -> p j c", p=128), sgb[s_].ds, writes=[sgb[s_].r])

                PSflat = PSF[:, 0:4, :].rearrange("p a b -> p (a b)")
                b0_load(0)
                for T in range(NT):
                    s_ = T % 2
                    if T + 1 < NT:
                        b0_load(T + 1)
                    for j_ in range(4):
                        if lvl < 2:
                            break
                        blk = 4 * T + j_
                        e_ = eb[blk % 2]
                        kks = (0, 1) if blk > 0 else (1,)
                        for kk in kks:
                            for p_ in range(4):
                                for hf in range(2):
                                    off = (kk * 8 + hf * 4 + p_) * 128
                                    lastmm = (kk == 1 and p_ == 3 and hf == 1)
                                    k.op("pe", lambda off=off, hf=hf, kk=kk, p_=p_, j_=j_, s_=s_: nc.tensor.matmul(
                                        PSflat[:, off:off + 128], lhsT=kbt[s_].t[hf * 64:(hf + 1) * 64, (j_ + kk) * 128:(j_ + kk + 1) * 128],
                                        rhs=qbt[s_].t[hf * 64:(hf + 1) * 64, p_, j_ * 128:(j_ + 1) * 128], start=True, stop=True),
                                         reads=[kbt[s_].r, qbt[s_].r], writes=[bank[0], bank[1], bank[2], bank[3]], signal=lastmm, same_engine=False)
                        for kk in kks:
                            lo = kk * 8
                            k.op("act", lambda lo=lo, e_=e_: nc.scalar.activation(out=e_.t[:, lo:lo + 8, :], in_=PSflat[:, lo * 128:(lo + 8) * 128].rearrange("p (a t) -> p a t", t=128), func=AF.Exp, scale=0.125),
                                 reads=[bank[0], bank[1], bank[2], bank[3]], writes=[e_.r])
                        if lvl < 3:
                            continue
                        if blk > 0:
                            k.op("dve", lambda e_=e_: nc.vector.tensor_tensor(out=e_.t[:, 0:8, :], in0=e_.t[:, 0:8, :], in1=upp.t[:].unsqueeze(1).to_broadcast([128, 8, 128]), op=ALU.mult),
                                 reads=[e_.r, upp.r], writes=[e_.r])
                        k.op("dve", lambda e_=e_: nc.vector.tensor_tensor(out=e_.t[:, 8:16, :], in0=e_.t[:, 8:16, :], in1=tri.t[:].unsqueeze(1).to_broadcast([128, 8, 128]), op=ALU.mult),
                             reads=[e_.r, tri.r], writes=[e_.r])
                        if lvl < 4:
                            continue
                        for p_ in range(4):
                            for hf in range(2):
                                hd = hf * 4 + p_
                                for kk in kks:
                                    k.op("pe", lambda hd=hd, hf=hf, kk=kk, p_=p_, j_=j_, s_=s_, e_=e_, kks=kks: nc.tensor.matmul(
                                        PSF[:, 4 + hd // 4, (hd % 4) * 80:(hd % 4) * 80 + 65], lhsT=e_.t[:, kk * 8 + hf * 4 + p_, :],
                                        rhs=vb1[s_].t[:, j_ + kk, hf, 0:65], start=(kk == kks[0]), stop=(kk == 1)),
                                         reads=[e_.r, vb1[s_].r], writes=[bank[4], bank[5]], signal=(kk == 1 and p_ == 3 and hf == 1), same_engine=False)
                        if lvl < 5:
                            continue
                        for bb in range(2):
                            k.op("dve", lambda bb=bb: nc.vector.tensor_copy(out=zz.t[:, bb * 4:(bb + 1) * 4], in_=PSF[:, 4 + bb, 0:320].rearrange("p (h d) -> p h d", d=80)[:, :, 64]),
                                 reads=[bank[4], bank[5], zz.r], writes=[zz.r])
                        k.op("dve", lambda: nc.vector.tensor_tensor(out=zz.t[:], in0=zz.t[:], in1=esk.t[:], op=ALU.add), reads=[zz.r, esk.r], writes=[zz.r])
                        k.op("dve", lambda: nc.vector.reciprocal(out=zz.t[:], in_=zz.t[:]), reads=[zz.r], writes=[zz.r])
                        for bb in range(2):
                            k.op("dve", lambda bb=bb: nc.vector.tensor_tensor(out=ybt.t[:, bb * 4:(bb + 1) * 4, :], in0=PSF[:, 4 + bb, 0:320].rearrange("p (h d) -> p h d", d=80)[:, :, 0:64],
                                                                            in1=zz.t[:, bb * 4:(bb + 1) * 4].unsqueeze(2).to_broadcast([128, 4, 64]), op=ALU.mult),
                                 reads=[bank[4], bank[5], zz.r, ybt.r], writes=[ybt.r])
                        k.op("dve", lambda j_=j_, s_=s_: nc.vector.tensor_tensor(out=ysb[s_].t[:, j_, :], in0=ybt.t[:].rearrange("p h d -> p (h d)"), in1=sgb[s_].t[:, j_, :], op=ALU.mult),
                             reads=[ybt.r, sgb[s_].r], writes=[ysb[s_].r])
                    if lvl >= 6:
                        k.dma("pool", Y0[T * 512:(T + 1) * 512, 512:1024].rearrange("(j p) c -> p j c", p=128), ysb[s_].t[:], ysb[s_].ds, reads=[ysb[s_].r])

            if dbg == "b0" or lvl != 99:
                k.barrier()
                return nc, k
            with Phase():
                wo = sb("wo", [128, 8, D], BF16)
                k.dma("pool", wo.t[:], e_w_out.rearrange("(kc p) n -> p kc n", p=128), wo.ds, writes=[wo.r])
                yb_ = slots("yb_", [128, D], BF16, 2)
                yT = slots("yT", [128, 8, 128], BF16, 2)
                x1t = slots("x1t", [128, D], F32, 2)
                xv = x_in.rearrange("(n p) d -> n p d", p=128)
                x1v = x1.rearrange("(n p) d -> n p d", p=128)
                y0v = Y0.rearrange("(n p) d -> n p d", p=128)
                k.dma("sp", xt[0].t[:], xv[0], xt[0].ds, writes=[xt[0].r])
                k.dma("sp", yb_[0].t[:], y0v[0], yb_[0].ds, writes=[yb_[0].r])
                for blk in range(NB):
                    if blk + 1 < NB:
                        k.dma("sp", xt[(blk + 1) % 3].t[:], xv[blk + 1], xt[(blk + 1) % 3].ds, writes=[xt[(blk + 1) % 3].r])
                        k.dma("sp", yb_[(blk + 1) % 2].t[:], y0v[blk + 1], yb_[(blk + 1) % 2].ds, writes=[yb_[(blk + 1) % 2].r])
                    xb, yb, yTb, xo = xt[blk % 3], yb_[blk % 2], yT[blk % 2], x1t[blk % 2]
                    transpose8(lambda kc, yb=yb: yb.t[:, kc * 128:(kc + 1) * 128], yb.r, yTb.t[:], yTb.r, eng="act")
                    for cc in range(2):
                        b = cc
                        for kc in range(8):
                            k.op("pe", lambda kc=kc, b=b, cc=cc, yTb=yTb: nc.tensor.matmul(PSF[:, b, :], lhsT=yTb.t[:, kc, :], rhs=wo.t[:, kc, cc * 512:(cc + 1) * 512],
                                                                                       start=(kc == 0), stop=(kc == 7)),
                                 reads=[yTb.r, wo.r], writes=[bank[b]], signal=(kc == 7), same_engine=False)
                        k.op("dve", lambda b=b, cc=cc: nc.vector.tensor_tensor(out=junk.t[:, cc * 512:(cc + 1) * 512], in0=PSF[:, b, :], in1=gate0.t[:, cc * 512:(cc + 1) * 512], op=ALU.mult),
                             reads=[bank[b], gate0.r, junk.r], writes=[junk.r])
                        k.op("dve", lambda cc=cc, xo=xo, xb=xb: nc.vector.tensor_tensor(out=xo.t[:, cc * 512:(cc + 1) * 512], in0=junk.t[:, cc * 512:(cc + 1) * 512],
                                                                                   in1=xb.t[:, cc * 512:(cc + 1) * 512], op=ALU.add),
                             reads=[junk.r, xb.r, xo.r], writes=[xo.r])
                    k.dma("pool", x1v[blk], xo.t[:], xo.ds, reads=[xo.r])

        if dbg == "l0" or not L1:
            k.barrier()
            return nc, k

        sh1, gs1, gate1 = mod_vectors("1", o_w_mod, o_b_mod, o_norm_g)
        H1T = dram_scr("H1T", [8, 128, S], BF16)
        H1OT = dram_scr("H1OT", [8, 128, SO], BF16)
        X1O = dram_scr("X1O", [SO, D], F32)
        Y1T = dram_scr("Y1T", [16, 64, SO], BF16)
        om = sb("om", [128, 2], F32)
        k.dma("sp", om.t[:], ownm[:, :], om.ds, writes=[om.r])
        x1v = x1.rearrange("(n p) d -> n p d", p=128)
        x1ov = X1O.rearrange("(n p) d -> n p d", p=128)

        with Phase():
            st_h = slots("st_h", [128, 8, 512], BF16, 2)
            st_ho = slots("st_ho", [128, 8, 256], BF16, 2)
            hown = sb("hown", [128, D], BF16)
            hof = sb("hof", [128, D], F32)
            xo = slots("xo", [128, D], F32, 2)
            k.dma("sp", xt[0].t[:], x1v[0], xt[0].ds, writes=[xt[0].r])
            for T in range(NT):
                s_ = T % 2
                for j in range(4):
                    blk = T * 4 + j
                    if blk + 1 < NB:
                        nx = xt[(blk + 1) % 3]
                        k.dma("sp", nx.t[:], x1v[blk + 1], nx.ds, writes=[nx.r])
                    xb, hb = xt[blk % 3], htm[blk % 2]
                    norm_mod(xb, hb, gs1, sh1)
                    transpose8(lambda kc, hb=hb: hb.t[:, kc * 128:(kc + 1) * 128], hb.r, st_h[s_].t[:, :, j * 128:(j + 1) * 128], st_h[s_].r, eng="act")
                    if j % 2 == 1:
                        he, xe = htm[(blk - 1) % 2], xt[(blk - 1) % 3]
                        ob = blk // 2
                        k.op("dve", lambda he=he: nc.vector.tensor_scalar(out=hof.t[:], in0=he.t[:], scalar1=om.t[:, 0:1], scalar2=None, op0=ALU.mult),
                             reads=[he.r, om.r, hof.r], writes=[hof.r])
                        k.op("dve", lambda hb=hb: nc.vector.scalar_tensor_tensor(out=hown.t[:], in0=hb.t[:], scalar=om.t[:, 1:2], in1=hof.t[:], op0=ALU.mult, op1=ALU.add),
                             reads=[hb.r, om.r, hof.r, hown.r], writes=[hown.r])
                        transpose8(lambda kc: hown.t[:, kc * 128:(kc + 1) * 128], hown.r, st_ho[s_].t[:, :, (j // 2) * 128:(j // 2 + 1) * 128], st_ho[s_].r, eng="act")
                        xo_ = xo[ob % 2]
                        k.op("dve", lambda xe=xe: nc.vector.tensor_scalar(out=junk.t[:], in0=xe.t[:], scalar1=om.t[:, 0:1], scalar2=None, op0=ALU.mult),
                             reads=[xe.r, om.r, junk.r], writes=[junk.r])
                        k.op("dve", lambda xb=xb, xo_=xo_: nc.vector.scalar_tensor_tensor(out=xo_.t[:], in0=xb.t[:], scalar=om.t[:, 1:2], in1=junk.t[:], op0=ALU.mult, op1=ALU.add),
                             reads=[xb.r, om.r, junk.r, xo_.r], writes=[xo_.r])
                        k.dma("pool", x1ov[ob], xo_.t[:], xo_.ds, reads=[xo_.r])
                k.dma("pool", H1T[:, :, T * 512:(T + 1) * 512].rearrange("k p t -> p k t"), st_h[s_].t[:], st_h[s_].ds, reads=[st_h[s_].r])
                k.dma("pool", H1OT[:, :, T * 256:(T + 1) * 256].rearrange("k p t -> p k t"), st_ho[s_].t[:], st_ho[s_].ds, reads=[st_ho[s_].r])

        w1v = o_w_in.rearrange("(kc p) n -> p kc n", p=128)
        for g in range(4):
            with Phase():
                KTg = sb("KTg", [128, 2, S], BF16)
                Vg = sb("Vg", [128, NB, 256], BF16)
                QTg = sb("QTg", [128, 2, SO], BF16)
                SGT = sb("SGT", [64, 4, SO], BF16)
                with Phase():
                    wq = sb("wq", [128, 8, 256], BF16); wk = sb("wk", [128, 8, 256], BF16)
                    wv = sb("wv", [128, 8, 256], BF16); wg = sb("wg", [128, 8, 256], BF16)
                    for wt_, off in ((wq, 0), (wk, 1024), (wv, 2048), (wg, 3072)):
                        k.dma("pool", wt_.t[:], w1v[:, :, off + g * 256:off + (g + 1) * 256], wt_.ds, writes=[wt_.r])
                    h1t = slots("h1t", [128, 8, 512], BF16, 2)
                    ge1 = sb("ge1", [64, 512], F32)
                    k.dma("sp", h1t[0].t[:], H1T[:, :, 0:512].rearrange("k p t -> p k t"), h1t[0].ds, writes=[h1t[0].r])
                    bs = 0
                    for T in range(NT):
                        ht = h1t[T % 2]
                        if T + 1 < NT:
                            k.dma("sp", h1t[(T + 1) % 2].t[:], H1T[:, :, (T + 1) * 512:(T + 2) * 512].rearrange("k p t -> p k t"), h1t[(T + 1) % 2].ds, writes=[h1t[(T + 1) % 2].r])
                        for p_ in range(2):
                            b = bs; bs = (bs + 1) % 4
                            for kc in range(8):
                                k.op("pe", lambda kc=kc, b=b, p_=p_, ht=ht: nc.tensor.matmul(PSF[:, b, :], lhsT=wk.t[:, kc, p_ * 128:(p_ + 1) * 128], rhs=ht.t[:, kc, :], start=(kc == 0), stop=(kc == 7)),
                                     reads=[wk.r, ht.r], writes=[bank[b]], signal=(kc == 7), same_engine=False)
                            k.op("act", lambda b=b, p_=p_, T=T: nc.scalar.copy(out=KTg.t[:, p_, T * 512:(T + 1) * 512], in_=PSF[:, b, :]), reads=[bank[b]], writes=[KTg.r])
                        for j in range(4):
                            b = bs; bs = (bs + 1) % 4
                            for kc in range(8):
                                k.op("pe", lambda kc=kc, b=b, j=j, ht=ht: nc.tensor.matmul(PSF[:, b, 0:256], lhsT=ht.t[:, kc, j * 128:(j + 1) * 128], rhs=wv.t[:, kc, :], start=(kc == 0), stop=(kc == 7)),
                                     reads=[wv.r, ht.r], writes=[bank[b]], signal=(kc == 7), same_engine=False)
                            k.op("dve", lambda b=b, j=j, T=T: nc.vector.tensor_copy(out=Vg.t[:, T * 4 + j, :], in_=PSF[:, b, 0:256]), reads=[bank[b]], writes=[Vg.r])
                    k.dma("sp", h1t[0].t[:], H1OT[:, :, 0:512].rearrange("k p t -> p k t"), h1t[0].ds, writes=[h1t[0].r])
                    for TO in range(NTO):
                        ht = h1t[TO % 2]
                        if TO + 1 < NTO:
                            k.dma("sp", h1t[(TO + 1) % 2].t[:], H1OT[:, :, (TO + 1) * 512:(TO + 2) * 512].rearrange("k p t -> p k t"), h1t[(TO + 1) % 2].ds, writes=[h1t[(TO + 1) % 2].r])
                        for p_ in range(2):
                            b = bs; bs = (bs + 1) % 4
                            for kc in range(8):
                                k.op("pe", lambda kc=kc, b=b, p_=p_, ht=ht: nc.tensor.matmul(PSF[:, b, :], lhsT=wq.t[:, kc, p_ * 128:(p_ + 1) * 128], rhs=ht.t[:, kc, :], start=(kc == 0), stop=(kc == 7)),
                                     reads=[wq.r, ht.r], writes=[bank[b]], signal=(kc == 7), same_engine=False)
                            k.op("act", lambda b=b, p_=p_, TO=TO: nc.scalar.mul(out=QTg.t[:, p_, TO * 512:(TO + 1) * 512], in_=PSF[:, b, :], mul=0.125), reads=[bank[b]], writes=[QTg.r])
                        for hl in range(4):
                            b = bs; bs = (bs + 1) % 4
                            for kc in range(8):
                                k.op("pe", lambda kc=kc, b=b, hl=hl, ht=ht: nc.tensor.matmul(PSF[0:64, b, :], lhsT=wg.t[:, kc, hl * 64:(hl + 1) * 64], rhs=ht.t[:, kc, :], start=(kc == 0), stop=(kc == 7)),
                                     reads=[wg.r, ht.r], writes=[bank[b]], signal=(kc == 7), same_engine=False)
                            k.op("act", lambda b=b: nc.scalar.activation(out=ge1.t[:], in_=PSF[0:64, b, :], func=AF.Exp, scale=-1.0), reads=[bank[b]], writes=[ge1.r])
                            k.op("dve", lambda: nc.vector.tensor_scalar_add(out=ge1.t[:], in0=ge1.t[:], scalar1=1.0), reads=[ge1.r], writes=[ge1.r])
                            k.op("dve", lambda: nc.vector.reciprocal(out=ge1.t[:], in_=ge1.t[:]), reads=[ge1.r], writes=[ge1.r])
                            k.op("dve", lambda b=b, hl=hl, TO=TO: nc.vector.tensor_tensor(out=SGT.t[:, hl, TO * 512:(TO + 1) * 512], in0=ge1.t[:], in1=PSF[0:64, b, :], op=ALU.mult),
                                 reads=[ge1.r, bank[b]], writes=[SGT.r])
                with Phase():
                    dm = sb("dm", [128, 8, 512], BF16)
                    k.dma("pool", dm.t[:], dmask.rearrange("p (r q) -> p r q", r=8), dm.ds, writes=[dm.r])
                    e32 = slots("e32", [128, 2, 512], F32, 3)
                    sp_ = slots("sp_", [128, 2, 512], BF16, 3)
                    ex = [slots("ex%d" % i_, [128, 512], F32, 2) for i_ in range(2)]
                    ww = [slots("ww%d" % i_, [128, 512], BF16, 2) for i_ in range(2)]
                    yst1 = slots("yst1", [64, 2, 512], BF16, 2)
                    fin = 0
                    for m in range(NTO):
                        for p_ in range(2):
                            kbs = list(range(8 * m + 7, -1, -1))

                            def stage1(kb, i):
                                sl = i % 3
                                for hf in range(2):
                                    k.op("pe", lambda hf=hf, kb=kb: nc.tensor.matmul(PSF[:, hf, :], lhsT=KTg.t[hf * 64:(hf + 1) * 64, p_, kb * 128:(kb + 1) * 128],
                                                                                   rhs=QTg.t[hf * 64:(hf + 1) * 64, p_, m * 512:(m + 1) * 512], start=True, stop=True),
                                         reads=[KTg.r, QTg.r], writes=[bank[hf]], signal=(hf == 1), same_engine=False)
                                k.op("act", lambda sl=sl: nc.scalar.activation(out=e32[sl].t[:], in_=PSF[:, 0:2, :], func=AF.Exp), reads=[bank[0], bank[1]], writes=[e32[sl].r])
                                k.op("act", lambda sl=sl: nc.scalar.activation(out=sp_[sl].t[:], in_=e32[sl].t[:], func=AF.Ln, bias=1.0), reads=[e32[sl].r], writes=[sp_[sl].r])
                                r_ = kb - 8 * m
                                if r_ >= 0:
                                    mk = dm.t[:, r_, :].unsqueeze(1).to_broadcast([128, 2, 512])
                                    k.op("dve", lambda sl=sl, mk=mk: nc.vector.tensor_tensor(out=sp_[sl].t[:], in0=sp_[sl].t[:], in1=mk, op=ALU.mult), reads=[sp_[sl].r, dm.r], writes=[sp_[sl].r])
                                    k.op("dve", lambda sl=sl, mk=mk: nc.vector.tensor_tensor(out=e32[sl].t[:], in0=e32[sl].t[:], in1=mk, op=ALU.mult), reads=[e32[sl].r, dm.r], writes=[e32[sl].r])

                            def stage2a(kb, i, first, last):
                                sl, s2 = i % 3, i % 2
                                for hf in range(2):
                                    k.op("pe", lambda hf=hf, sl=sl: nc.tensor.matmul(PSF[:, 2 + hf, :], lhsT=negtri.t[:], rhs=sp_[sl].t[:, hf, :], start=first, stop=True, skip_group_check=True),
                                         reads=[negtri.r, sp_[sl].r], writes=[bank[2 + hf]], same_engine=False)
                                for hf in range(2):
                                    k.op("act", lambda s2=s2, hf=hf: nc.scalar.activation(out=ex[hf][s2].t[:], in_=PSF[:, 2 + hf, :], func=AF.Exp), reads=[bank[2 + hf]], writes=[ex[hf][s2].r])
                                    k.op("dve", lambda sl=sl, s2=s2, hf=hf: nc.vector.tensor_tensor(out=ww[hf][s2].t[:], in0=e32[sl].t[:, hf, :], in1=ex[hf][s2].t[:], op=ALU.mult),
                                         reads=[e32[sl].r, ex[hf][s2].r], writes=[ww[hf][s2].r])

                            def stage2b(kb, i, first, last):
                                sl, s2 = i % 3, i % 2
                                if not last:
                                    for hf in range(2):
                                        k.op("pe", lambda hf=hf, sl=sl: nc.tensor.matmul(PSF[:, 2 + hf, :], lhsT=negrest.t[:], rhs=sp_[sl].t[:, hf, :], start=False, stop=True, skip_group_check=True),
                                             reads=[negrest.r, sp_[sl].r], writes=[bank[2 + hf]], same_engine=False)
                                for hf in range(2):
                                    k.op("pe", lambda hf=hf, s2=s2, kb=kb: nc.tensor.matmul(PSF[0:64, 4 + hf, :], lhsT=Vg.t[:, kb, (p_ * 2 + hf) * 64:(p_ * 2 + hf + 1) * 64], rhs=ww[hf][s2].t[:],
                                                                                          start=first, stop=last),
                                         reads=[Vg.r, ww[hf][s2].r], writes=[bank[4 + hf]], same_engine=False)

                            stage1(kbs[0], 0)
                            if len(kbs) > 1:
                                stage1(kbs[1], 1)
                            for i, kb in enumerate(kbs):
                                stage2a(kb, i, i == 0, kb == 0)
                                if i + 2 < len(kbs):
                                    stage1(kbs[i + 2], i + 2)
                                stage2b(kb, i, i == 0, kb == 0)
                            ys = yst1[fin % 2]
                            fin += 1
                            for hf in range(2):
                                hl = p_ * 2 + hf
                                k.op("dve", lambda hf=hf, hl=hl, ys=ys: nc.vector.tensor_tensor(out=ys.t[:, hf, :], in0=PSF[0:64, 4 + hf, :], in1=SGT.t[:, hl, m * 512:(m + 1) * 512], op=ALU.mult),
                                     reads=[bank[4 + hf], SGT.r, ys.r], writes=[ys.r])
                            for hf in range(2):
                                k.dma("pool", Y1T[g * 4 + p_ * 2 + hf, :, m * 512:(m + 1) * 512], ys.t[:, hf, :], ys.ds, reads=[ys.r])

        with Phase():
            wo1 = sb("wo1", [64, 16, D], BF16)
            k.dma("pool", wo1.t[:], o_w_out.rearrange("(h p) n -> p h n", p=64), wo1.ds, writes=[wo1.r])
            yt1 = slots("yt1", [64, 16, 128], BF16, 2)
            outt = slots("outt", [128, D], F32, 2)
            outv = out_o.rearrange("(n p) d -> n p d", p=128)
            NOB = SO // 128
            k.dma("sp", xt[0].t[:], x1ov[0], xt[0].ds, writes=[xt[0].r])
            k.dma("sp", yt1[0].t[:], Y1T[:, :, 0:128].rearrange("h p t -> p h t"), yt1[0].ds, writes=[yt1[0].r])
            for ob in range(NOB):
                if ob + 1 < NOB:
                    k.dma("sp", xt[(ob + 1) % 3].t[:], x1ov[ob + 1], xt[(ob + 1) % 3].ds, writes=[xt[(ob + 1) % 3].r])
                    k.dma("sp", yt1[(ob + 1) % 2].t[:], Y1T[:, :, (ob + 1) * 128:(ob + 2) * 128].rearrange("h p t -> p h t"), yt1[(ob + 1) % 2].ds, writes=[yt1[(ob + 1) % 2].r])
                xb, yt, oo = xt[ob % 3], yt1[ob % 2], outt[ob % 2]
                for cc in range(2):
                    b = cc
                    for h in range(16):
                        k.op("pe", lambda h=h, b=b, cc=cc, yt=yt: nc.tensor.matmul(PSF[:, b, :], lhsT=yt.t[:, h, :], rhs=wo1.t[:, h, cc * 512:(cc + 1) * 512], start=(h == 0), stop=(h == 15)),
                             reads=[yt.r, wo1.r], writes=[bank[b]], signal=(h == 15), same_engine=False)
                    k.op("dve", lambda b=b, cc=cc: nc.vector.tensor_tensor(out=junk.t[:, cc * 512:(cc + 1) * 512], in0=PSF[:, b, :], in1=gate1.t[:, cc * 512:(cc + 1) * 512], op=ALU.mult),
                         reads=[bank[b], gate1.r, junk.r], writes=[junk.r])
                    k.op("dve", lambda cc=cc, oo=oo, xb=xb: nc.vector.tensor_tensor(out=oo.t[:, cc * 512:(cc + 1) * 512], in0=junk.t[:, cc * 512:(cc + 1) * 512], in1=xb.t[:, cc * 512:(cc + 1) * 512], op=ALU.add),
                         reads=[junk.r, xb.r, oo.r], writes=[oo.r])
                k.dma("pool", outv[ob], oo.t[:], oo.ds, reads=[oo.r])
        k.barrier()
    return nc, k


def _consts():
    s = np.arange(128)[:, None]
    t = np.arange(128)[None, :]
    ident = (s == t).astype(np.float32)
    tri = (s <= t).astype(np.float32)
    upp = (s > t).astype(np.float32)
    negtri = -(s >= t).astype(np.float32)
    inv = (10000.0 ** (-np.arange(32, dtype=np.float32) / np.float32(32))).astype(np.float32)
    return np.concatenate([ident, tri, upp, negtri, np.broadcast_to(inv[None, :], (128, 32))], axis=1).astype(np.float32)


def host_inputs(inp, b, hh, S, layers=(0, 1)):
    f = lambda a: np.ascontiguousarray(np.asarray(a), dtype=np.float32)
    m = {"cT": f(np.asarray(inp["c"])[b].reshape(8, 128).T), "consts": _consts()}
    if 0 in layers:
        m["x"] = f(np.asarray(inp["x"])[b, :S])
        m["posT"] = np.ascontiguousarray(np.asarray(inp["positions"])[b, :S].reshape(S // 128, 128).T.astype(np.int32))
        m["e_norm_g"] = f(inp["even_norm_g"]).reshape(1, D)
        m["e_w_mod"] = f(inp["even_w_mod"])[0]
        m["e_b_mod"] = f(inp["even_b_mod"]).reshape(1, 3 * D)
        w = f(inp["even_w_in"])[0]
        qa, ka, va, ga, qb, kb, vb, gb = np.split(w, np.cumsum([512, 512, 512, 512, 512, 128, 128])[:], axis=1)
        qbp = qb.reshape(D, 2, 4, 64).transpose(0, 2, 1, 3).reshape(D, 512)
        m["e_w_in"] = np.ascontiguousarray(np.concatenate([qa, ka, va, ga, qbp, gb, kb, vb], axis=1))
        m["e_w_out"] = f(inp["even_w_out"])[0]
        m["gains"] = np.concatenate([f(inp["a_q_gain"])[0], f(inp["a_k_gain"])[0], f(inp["b_q_gain"])[0], f(inp["b_k_gain"])[0]]).reshape(1, 256)
        m["lamv"] = np.concatenate([f(inp["a_lambda_q1"])[0], f(inp["a_lambda_k1"])[0], f(inp["a_lambda_q2"])[0], f(inp["a_lambda_k2"])[0]]).reshape(1, 256)
        m["subg"] = f(inp["a_subln_g"]).reshape(1, 128)
        m["sinks"] = f(inp["b_sinks"]).reshape(1, 8)
    if 1 in layers:
        m["o_norm_g"] = f(inp["odd_norm_g"]).reshape(1, D)
        m["o_w_mod"] = f(inp["odd_w_mod"])[0]
        m["o_b_mod"] = f(inp["odd_b_mod"]).reshape(1, 3 * D)
        m["o_w_in"] = f(inp["odd_w_in"])[0]
        m["o_w_out"] = f(inp["odd_w_out"])[0]
        m["ownm"] = np.ascontiguousarray(np.broadcast_to(np.array([[1.0 - hh, float(hh)]], np.float32), (128, 2)))
        s = np.arange(128)[:, None, None, None]
        r = np.arange(8)[None, :, None, None]
        jj = np.arange(4)[None, None, :, None]
        tq = np.arange(128)[None, None, None, :]
        g = 2 * jj + hh
        msk = ((r < g) | ((r == g) & (s < tq))).astype(np.float32)
        m["dmask"] = np.ascontiguousarray(msk.reshape(128, 8 * 512))
    return m


_CACHE = {}


def kernel(**inputs):
    S = 8192
    if "nc" not in _CACHE:
        _CACHE["nc"] = build(S=S, layers=(0, 1))[0]
    nc = _CACHE["nc"]
    in_maps = [host_inputs(inputs, c // 2, c % 2, S) for c in range(8)]
    res = run_bass_kernel_spmd(nc, in_maps, core_ids=list(range(8)))
    out = np.empty((4, S, D), np.float32)
    for c in range(8):
        b, hh = c // 2, c % 2
        o = np.asarray(res.results[c]["out"]).reshape(S // 256, 128, D)
        out[b].reshape(S // 256, 2, 128, D)[:, hh] = o
    return out
```

```python
import math
from contextlib import ExitStack
import numpy as np
import concourse.bass as bass
import concourse.mybir as mybir
from concourse.bass_utils import run_bass_kernel_spmd

F32 = mybir.dt.float32
BF16 = mybir.dt.bfloat16
I32 = mybir.dt.int32
AF = mybir.ActivationFunctionType
ALU = mybir.AluOpType
AX = mybir.AxisListType

D = 1024
EPS = 1e-6
PI = math.pi


class Res:
    __slots__ = ("name", "lw", "rd")

    def __init__(self, name=""):
        self.name = name
        self.lw = None
        self.rd = {}


class SemObj:
    __slots__ = ("sem", "count", "name")

    def __init__(self, sem, name):
        self.sem = sem
        self.count = 0
        self.name = name


class KB:
    def __init__(self, nc, stack):
        self.nc = nc
        self.stack = stack
        self.engs = {"pe": nc.tensor, "act": nc.scalar, "dve": nc.vector, "pool": nc.gpsimd, "sp": nc.sync}
        self.so = {}
        for k in self.engs:
            s = stack.enter_context(nc.semaphore("prog_" + k))
            self.so[k] = SemObj(s, k)
        self.waited = {k: {} for k in self.engs}
        self.n_inst = 0
        self.all_so = list(self.so.values())
        self.free_dma = {}

    def new_dma_sem(self, name, q="sp"):
        if self.free_dma.setdefault(q, []):
            return self.free_dma[q].pop()
        s = self.stack.enter_context(self.nc.semaphore("dma_" + name))
        so = SemObj(s, name)
        self.all_so.append(so)
        return so

    def barrier(self):
        for e in self.engs:
            self._wait(e, [(so, so.count) for so in self.all_so if so is not self.so[e]])

    def _wait(self, e, deps):
        w = self.waited[e]
        for so, val in deps:
            if val <= 0 or w.get(so, 0) >= val:
                continue
            self.engs[e].wait_ge(so.sem, val)
            w[so] = val

    def _deps(self, e, reads, writes, same_engine):
        me = self.so[e]
        deps = []
        for r in reads:
            if r.lw is not None and (same_engine or r.lw[0] is not me):
                deps.append(r.lw)
        for r in writes:
            if r.lw is not None and (same_engine or r.lw[0] is not me):
                deps.append(r.lw)
            for so, v in r.rd.items():
                if same_engine or so is not me:
                    deps.append((so, v))
        return deps

    def op(self, e, fn, reads=(), writes=(), signal=True, same_engine=True):
        me = self.so[e]
        self._wait(e, self._deps(e, reads, writes, same_engine))
        ins = fn()
        self.n_inst += 1
        if signal:
            me.count += 1
            ins.then_inc(me.sem, 1)
            ev = (me, me.count)
        else:
            ev = (me, me.count + 1)
        for r in reads:
            if r.rd.get(me, 0) < ev[1]:
                r.rd[me] = ev[1]
        for r in writes:
            r.lw = ev
            r.rd = {}
        return ins

    def dma(self, q, out_ap, in_ap, dsem, reads=(), writes=(), **kw):
        if isinstance(dsem, Buf):
            dsem = dsem.get_ds(q)
        self._wait(q, self._deps(q, reads, writes, True))
        ins = self.engs[q].dma_start(out=out_ap, in_=in_ap, **kw)
        self.n_inst += 1
        dsem.count += 16
        ins.then_inc(dsem.sem, 16)
        ev = (dsem, dsem.count)
        for r in reads:
            if r.rd.get(dsem, 0) < ev[1]:
                r.rd[dsem] = ev[1]
        for r in writes:
            r.lw = ev
            r.rd = {}
        return ins


class Buf:
    def __init__(self, t, name, k):
        self.t = t
        self.r = Res(name)
        self.name = name
        self._k = k
        self._ds = {}

    @property
    def ds(self):
        return self

    def get_ds(self, q):
        if q not in self._ds:
            self._ds[q] = self._k.new_dma_sem(self.name + q, q)
        return self._ds[q]


def build(S=8192, layers=(0, 1), dbg=False):
    NB = S // 128
    NT = S // 512
    SO = S // 2
    NTO = SO // 512
    nc = bass.Bass("TRN2", target_bir_lowering=False)
    dram_in = lambda n, sh, dt=F32: nc.dram_tensor(n, sh, dt, kind="ExternalInput").ap()
    dram_out = lambda n, sh, dt=F32: nc.dram_tensor(n, sh, dt, kind="ExternalOutput").ap()
    dram_scr = lambda n, sh, dt: nc.dram_tensor(n, sh, dt, kind=("ExternalOutput" if dbg else "Internal")).ap()

    L0 = 0 in layers
    L1 = 1 in layers
    cT = dram_in("cT", [128, 8])
    consts = dram_in("consts", [128, 128 * 4 + 32])
    if L0:
        x_in = dram_in("x", [S, D])
        posT = dram_in("posT", [128, NB], I32)
        e_norm_g = dram_in("e_norm_g", [1, D]); e_w_mod = dram_in("e_w_mod", [D, 3 * D]); e_b_mod = dram_in("e_b_mod", [1, 3 * D])
        e_w_in = dram_in("e_w_in", [D, 3328]); e_w_out = dram_in("e_w_out", [D, D])
        gains = dram_in("gains", [1, 4 * 64])
        lamv = dram_in("lamv", [1, 4 * 64])
        subg = dram_in("subg", [1, 128])
        sinks = dram_in("sinks", [1, 8])
    if L1:
        o_norm_g = dram_in("o_norm_g", [1, D]); o_w_mod = dram_in("o_w_mod", [D, 3 * D]); o_b_mod = dram_in("o_b_mod", [1, 3 * D])
        o_w_in = dram_in("o_w_in", [D, 4 * D]); o_w_out = dram_in("o_w_out", [D, D])
        ownm = dram_in("ownm", [128, 2])
        dmask = dram_in("dmask", [128, 8 * 512])
        out_o = dram_out("out", [SO, D])
    if L0 and L1:
        x1 = dram_scr("x1", [S, D], F32)
    elif L0:
        x1 = dram_out("x1", [S, D])
    else:
        x1 = dram_in("x1", [S, D])

    with ExitStack() as st:
        k = KB(nc, st)

        cur = [st]
        bufs_of = {id(st): []}

        uid = [0]

        def sb(name, shape, dt):
            uid[0] += 1
            name = f"{name}_{uid[0]}"
            bf_ = Buf(cur[0].enter_context(nc.sbuf_tensor(name, shape, dt)), name, k)
            bufs_of[id(cur[0])].append(bf_)
            return bf_

        def slots(name, shape, dt, n):
            return [sb(f"{name}{i}", shape, dt) for i in range(n)]

        class Phase:
            def __enter__(self_p):
                self_p.prev = cur[0]
                self_p.stk = ExitStack()
                cur[0] = self_p.stk
                bufs_of[id(self_p.stk)] = []
                return self_p

            def __exit__(self_p, *a):
                k.barrier()
                for bf_ in bufs_of.pop(id(self_p.stk)):
                    for q_, so_ in bf_._ds.items():
                        k.free_dma.setdefault(q_, []).append(so_)
                    bf_._ds = {}
                self_p.stk.close()
                cur[0] = self_p.prev
                return False

        PSF = st.enter_context(nc.psum_tensor("psf", [128, 7, 512], F32))
        PST = st.enter_context(nc.psum_tensor("pst", [128, 1024], BF16))
        bank = [Res(f"bank{i}") for i in range(7)]
        bankT = Res("bankT")

        cst32 = sb("cst32", [128, 128 * 4 + 32], F32)
        ident = sb("ident", [128, 128], BF16)
        tri = sb("tri", [128, 128], BF16)
        upp = sb("upp", [128, 128], BF16)
        negtri = sb("negtri", [128, 128], BF16)
        negrest = sb("negrest", [128, 128], BF16)
        k.dma("sp", cst32.t[:], consts[:, :], cst32.ds, writes=[cst32.r])
        for i, tdst in enumerate((ident, tri, upp, negtri)):
            k.op("dve", lambda tdst=tdst, i=i: nc.vector.tensor_copy(out=tdst.t[:], in_=cst32.t[:, i * 128:(i + 1) * 128]),
                 reads=[cst32.r], writes=[tdst.r])
        k.op("dve", lambda: nc.vector.tensor_scalar(out=negrest.t[:], in0=cst32.t[:, 384:512], scalar1=-1.0, scalar2=-1.0,
                                                   op0=ALU.mult, op1=ALU.add), reads=[cst32.r], writes=[negrest.r])
        invf = cst32.t[:, 512:544]

        sc = sb("sc", [128, 8], F32)
        sce = sb("sce", [128, 8], F32)
        k.dma("sp", sc.t[:], cT[:, :], sc.ds, writes=[sc.r])
        k.op("act", lambda: nc.scalar.activation(out=sce.t[:], in_=sc.t[:], func=AF.Exp, scale=-1.0), reads=[sc.r], writes=[sce.r])
        k.op("dve", lambda: nc.vector.tensor_scalar_add(out=sce.t[:], in0=sce.t[:], scalar1=1.0), reads=[sce.r], writes=[sce.r])
        k.op("dve", lambda: nc.vector.reciprocal(out=sce.t[:], in_=sce.t[:]), reads=[sce.r], writes=[sce.r])
        k.op("dve", lambda: nc.vector.tensor_tensor(out=sc.t[:], in0=sc.t[:], in1=sce.t[:], op=ALU.mult), reads=[sc.r, sce.r], writes=[sc.r])
        screp = sb("screp", [128, 8, 128], F32)
        k.op("dve", lambda: nc.vector.tensor_copy(out=screp.t[:], in_=sc.t[:].unsqueeze(2).to_broadcast([128, 8, 128])),
             reads=[sc.r], writes=[screp.r])

        def mod_vectors(tag, w_mod, b_mod, norm_g):
            shift = sb("shift" + tag, [128, D], F32)
            gs = sb("gs" + tag, [128, D], F32)
            gate = sb("gate" + tag, [128, D], F32)
            with Phase():
                _mod_vectors(w_mod, b_mod, norm_g, shift, gs, gate)
            return shift, gs, gate

        def _mod_vectors(w_mod, b_mod, norm_g, shift, gs, gate):
            wm = slots("wm", [128, 8, 512], F32, 2)
            bmod = sb("bmod", [128, 3 * D], F32)
            gbc = sb("gbc", [128, D], F32)
            k.dma("sp", bmod.t[:], b_mod.partition_broadcast(128), bmod.ds, writes=[bmod.r])
            k.dma("sp", gbc.t[:], norm_g.partition_broadcast(128), gbc.ds, writes=[gbc.r])
            wv = w_mod.rearrange("(kc p) n -> p kc n", p=128)
            for cc in range(6):
                w = wm[cc % 2]
                k.dma("sp", w.t[:], wv[:, :, cc * 512:(cc + 1) * 512], w.ds, writes=[w.r])
                b = cc % 2
                for kc in range(8):
                    k.op("pe", lambda kc=kc, b=b, w=w: nc.tensor.matmul(PSF[:, b, :], lhsT=screp.t[:, kc, :], rhs=w.t[:, kc, :],
                                                                        start=(kc == 0), stop=(kc == 7)),
                         reads=[screp.r, w.r], writes=[bank[b]], signal=(kc == 7), same_engine=False)
                seg, off = cc // 2, (cc % 2) * 512
                bsl = bmod.t[:, cc * 512:(cc + 1) * 512]
                if seg == 0:
                    k.op("dve", lambda b=b, off=off, bsl=bsl: nc.vector.tensor_tensor(out=shift.t[:, off:off + 512], in0=PSF[:, b, :], in1=bsl, op=ALU.add),
                         reads=[bank[b], bmod.r], writes=[shift.r])
                elif seg == 1:
                    k.op("dve", lambda b=b, off=off, bsl=bsl: nc.vector.scalar_tensor_tensor(out=gs.t[:, off:off + 512], in0=PSF[:, b, :], scalar=1.0, in1=bsl,
                                                                                            op0=ALU.add, op1=ALU.add),
                         reads=[bank[b], bmod.r], writes=[gs.r])
                    k.op("dve", lambda off=off: nc.vector.tensor_tensor(out=gs.t[:, off:off + 512], in0=gs.t[:, off:off + 512], in1=gbc.t[:, off:off + 512], op=ALU.mult),
                         reads=[gs.r, gbc.r], writes=[gs.r])
                else:
                    k.op("dve", lambda b=b, off=off, bsl=bsl: nc.vector.tensor_tensor(out=gate.t[:, off:off + 512], in0=PSF[:, b, :], in1=bsl, op=ALU.add),
                         reads=[bank[b], bmod.r], writes=[gate.r])
            return shift, gs, gate

        xt = slots("xt", [128, D], F32, 3)
        junk = sb("junk", [128, D], F32)
        ssq = sb("ssq", [128, 1], F32)
        rstd = sb("rstd", [128, 1], F32)
        htm = slots("htm", [128, D], BF16, 2)

        def norm_mod(xb, hb, gs, shift):
            k.op("act", lambda: nc.scalar.activation(out=junk.t[:], in_=xb.t[:], func=AF.Square, accum_out=ssq.t[:]),
                 reads=[xb.r], writes=[junk.r, ssq.r])
            k.op("act", lambda: nc.scalar.activation(out=rstd.t[:], in_=ssq.t[:], func=AF.Ln, scale=1.0 / D, bias=EPS), reads=[ssq.r], writes=[rstd.r])
            k.op("act", lambda: nc.scalar.activation(out=rstd.t[:], in_=rstd.t[:], func=AF.Exp, scale=-0.5), reads=[rstd.r], writes=[rstd.r])
            k.op("dve", lambda: nc.vector.scalar_tensor_tensor(out=junk.t[:], in0=xb.t[:], scalar=rstd.t[:, 0:1], in1=gs.t[:], op0=ALU.mult, op1=ALU.mult),
                 reads=[xb.r, rstd.r, gs.r, junk.r], writes=[junk.r])
            k.op("dve", lambda: nc.vector.tensor_tensor(out=hb.t[:], in0=junk.t[:], in1=shift.t[:], op=ALU.add),
                 reads=[junk.r, shift.r], writes=[hb.r])

        def transpose8(src_ap_fn, src_r, dst_ap, dst_r, eng="act"):
            for kc in range(8):
                k.op("pe", lambda kc=kc: nc.tensor.transpose(out=PST[:, kc * 128:(kc + 1) * 128], in_=src_ap_fn(kc), identity=ident.t[:]),
                     reads=[src_r, ident.r], writes=[bankT], signal=(kc == 7), same_engine=False)
            src3 = PST[:, :].rearrange("p (a b) -> p a b", a=8)
            if eng == "act":
                k.op("act", lambda: nc.scalar.copy(out=dst_ap, in_=src3), reads=[bankT], writes=[dst_r])
            else:
                k.op("dve", lambda: nc.vector.tensor_copy(out=dst_ap, in_=src3), reads=[bankT], writes=[dst_r])

        if L0:
            sh0, gs0, gate0 = mod_vectors("0", e_w_mod, e_b_mod, e_norm_g)
            QAT = dram_scr("QAT", [4, 128, S], BF16)
            KAT = dram_scr("KAT", [4, 128, S], BF16)
            VA = dram_scr("VA", [S, 512], BF16)
            SGA = dram_scr("SGA", [S, 512], BF16)
            QBT = dram_scr("QBT", [4, 128, S], BF16)
            KBT = dram_scr("KBT", [128, S], BF16)
            VB = dram_scr("VB", [S, 128], BF16)
            SGB = dram_scr("SGB", [S, 512], BF16)
            Y0 = dram_scr("Y0", [S, D], BF16)

            lvl = int(str(dbg)[3:]) if str(dbg).startswith("b0x") else 99
            with (Phase() if lvl == 99 else ExitStack()):
              if lvl == 99:
                  cosT = sb("cosT", [128, NB, 32], F32)
                  sinT = sb("sinT", [128, NB, 32], F32)
                  with Phase():
                      posi = sb("posi", [128, NB], I32)
                      posf = sb("posf", [128, NB], F32)
                      ang = sb("ang", [128, NB, 32], F32)
                      k.dma("sp", posi.t[:], posT[:, :], posi.ds, writes=[posi.r])
                      k.op("dve", lambda: nc.vector.tensor_copy(out=posf.t[:], in_=posi.t[:]), reads=[posi.r], writes=[posf.r])
                      k.op("dve", lambda: nc.vector.tensor_tensor(out=ang.t[:], in0=posf.t[:].unsqueeze(2).to_broadcast([128, NB, 32]),
                                                                 in1=invf.unsqueeze(1).to_broadcast([128, NB, 32]), op=ALU.mult),
                           reads=[posf.r, cst32.r], writes=[ang.r])
                      negpi = sb("negpi", [128, 1], F32)
                      k.op("dve", lambda: nc.vector.memset(negpi.t[:], -PI), writes=[negpi.r])
                      ui = sb("ui", [128, NB, 32], I32)
                      uf = sb("uf", [128, NB, 32], F32)
                      for tbl, c0 in ((sinT, 0.5), (cosT, 0.75)):
                          k.op("dve", lambda tbl=tbl, c0=c0: nc.vector.tensor_scalar(out=tbl.t[:], in0=ang.t[:], scalar1=1.0 / (2 * PI), scalar2=c0, op0=ALU.mult, op1=ALU.add),
                               reads=[ang.r], writes=[tbl.r])
                          k.op("dve", lambda tbl=tbl: nc.vector.tensor_copy(out=ui.t[:], in_=tbl.t[:]), reads=[tbl.r, ui.r], writes=[ui.r])
                          k.op("dve", lambda: nc.vector.tensor_copy(out=uf.t[:], in_=ui.t[:]), reads=[ui.r, uf.r], writes=[uf.r])
                          k.op("dve", lambda tbl=tbl: nc.vector.tensor_tensor(out=tbl.t[:], in0=tbl.t[:], in1=uf.t[:], op=ALU.subtract), reads=[tbl.r, uf.r], writes=[tbl.r])
                          k.op("dve", lambda tbl=tbl: nc.vector.tensor_single_scalar(out=uf.t[:], in_=tbl.t[:], scalar=0.0, op=ALU.is_lt), reads=[tbl.r, uf.r], writes=[uf.r])
                          k.op("dve", lambda tbl=tbl: nc.vector.tensor_tensor(out=tbl.t[:], in0=tbl.t[:], in1=uf.t[:], op=ALU.add), reads=[tbl.r, uf.r], writes=[tbl.r])
                          k.op("act", lambda tbl=tbl: nc.scalar.activation(out=tbl.t[:], in_=tbl.t[:], func=AF.Sin, bias=negpi.t[:, 0:1], scale=2 * PI),
                               reads=[tbl.r, negpi.r], writes=[tbl.r])
                  gn = sb("gn", [128, 4 * 64], F32)
                  k.dma("sp", gn.t[:], gains.partition_broadcast(128), gn.ds, writes=[gn.r])
                  w0 = sb("w0", [128, 8, 3328], BF16)
                  w0v = e_w_in.rearrange("(kc p) n -> p kc n", p=128)
                  for kc in range(8):
                      k.dma("pool", w0.t[:, kc, :], w0v[:, kc, :], w0.ds, writes=[w0.r])

                  hT = slots("hT", [128, 8, 128], BF16, 2)
                  st_qa = slots("st_qa", [128, 4, 512], BF16, 2)
                  st_ka = slots("st_ka", [128, 4, 512], BF16, 2)
                  st_qb = slots("st_qb", [128, 4, 512], BF16, 2)
                  st_kb = slots("st_kb", [128, 512], BF16, 2)
                  st_va = slots("st_va", [128, 4, 512], BF16, 2)
                  st_vb = slots("st_vb", [128, 4, 128], BF16, 2)
                  st_ga = slots("st_ga", [128, 4, 512], BF16, 2)
                  st_gb = slots("st_gb", [128, 4, 512], BF16, 2)
                  scr = []
                  for i_ in range(2):
                      scr.append(dict(sq=sb("sq", [128, 512], F32), ss8=sb("ss8", [128, 8], F32), rs8=sb("rs8", [128, 8], F32), tq=sb("tq", [128, 512], F32),
                                      ra=sb("ra", [128, 8, 32], F32), rb=sb("rb", [128, 8, 32], F32), rc=sb("rc", [128, 8, 32], F32), rd=sb("rd", [128, 8, 32], F32),
                                      ge=sb("ge", [128, 512], F32), qn=sb("qn", [128, 512], BF16)))

                  def interleave(*gens):
                      gens = list(gens)
                      while gens:
                          for g_ in list(gens):
                              try:
                                  next(g_)
                              except StopIteration:
                                  gens.remove(g_)

                  def normrope(si, b, nh, gain_ap, blk, stg_fn):
                      W = nh * 64
                      sc_ = scr[si]
                      sq, ss8, rs8, tq, ra, rb, rc, rd, q = (sc_[n_] for n_ in ("sq", "ss8", "rs8", "tq", "ra", "rb", "rc", "rd", "qn"))
                      k.op("act", lambda: nc.scalar.activation(out=sq.t[:, 0:W], in_=PSF[:, b, 0:W], func=AF.Square),
                           reads=[bank[b]], writes=[sq.r])
                      yield
                      k.op("dve", lambda: nc.vector.tensor_reduce(out=ss8.t[:, 0:nh], in_=sq.t[:, 0:W].rearrange("p (h d) -> p h d", d=64), axis=AX.X, op=ALU.add),
                           reads=[sq.r], writes=[ss8.r])
                      yield
                      k.op("act", lambda: nc.scalar.activation(out=rs8.t[:, 0:nh], in_=ss8.t[:, 0:nh], func=AF.Ln, scale=1.0 / 64, bias=EPS), reads=[ss8.r], writes=[rs8.r])
                      k.op("act", lambda: nc.scalar.activation(out=rs8.t[:, 0:nh], in_=rs8.t[:, 0:nh], func=AF.Exp, scale=-0.5), reads=[rs8.r], writes=[rs8.r])
                      yield
                      t3 = tq.t[:, 0:W].rearrange("p (h d) -> p h d", d=64)
                      k.op("dve", lambda: nc.vector.tensor_tensor(out=t3, in0=PSF[:, b, 0:W].rearrange("p (h d) -> p h d", d=64),
                                                                 in1=rs8.t[:, 0:nh].unsqueeze(2).to_broadcast([128, nh, 64]), op=ALU.mult),
                           reads=[bank[b], rs8.r], writes=[tq.r])
                      yield
                      k.op("dve", lambda: nc.vector.tensor_tensor(out=t3, in0=t3, in1=gain_ap.unsqueeze(1).to_broadcast([128, nh, 64]), op=ALU.mult),
                           reads=[tq.r, gn.r], writes=[tq.r])
                      yield
                      cb = cosT.t[:, blk, :].unsqueeze(1).to_broadcast([128, nh, 32])
                      sbn = sinT.t[:, blk, :].unsqueeze(1).to_broadcast([128, nh, 32])
                      x1v, x2v = t3[:, :, 0:32], t3[:, :, 32:64]
                      d3 = q.t[:, 0:W].rearrange("p (h d) -> p h d", d=64)
                      k.op("dve", lambda: nc.vector.tensor_tensor(out=ra.t[:, 0:nh, :], in0=x1v, in1=cb, op=ALU.mult), reads=[tq.r, cosT.r], writes=[ra.r])
                      yield
                      k.op("dve", lambda: nc.vector.tensor_tensor(out=rb.t[:, 0:nh, :], in0=x2v, in1=sbn, op=ALU.mult), reads=[tq.r, sinT.r], writes=[rb.r])
                      yield
                      k.op("dve", lambda: nc.vector.tensor_tensor(out=rc.t[:, 0:nh, :], in0=x2v, in1=cb, op=ALU.mult), reads=[tq.r, cosT.r], writes=[rc.r])
                      yield
                      k.op("dve", lambda: nc.vector.tensor_tensor(out=rd.t[:, 0:nh, :], in0=x1v, in1=sbn, op=ALU.mult), reads=[tq.r, sinT.r], writes=[rd.r])
                      yield
                      k.op("dve", lambda: nc.vector.tensor_tensor(out=d3[:, :, 0:32], in0=ra.t[:, 0:nh, :], in1=rb.t[:, 0:nh, :], op=ALU.subtract),
                           reads=[ra.r, rb.r], writes=[q.r])
                      yield
                      k.op("dve", lambda: nc.vector.tensor_tensor(out=d3[:, :, 32:64], in0=rc.t[:, 0:nh, :], in1=rd.t[:, 0:nh, :], op=ALU.add),
                           reads=[rc.r, rd.r], writes=[q.r])
                      yield
                      ncc = W // 128
                      for cc in range(ncc):
                          k.op("pe", lambda cc=cc: nc.tensor.transpose(out=PST[:, cc * 128:(cc + 1) * 128], in_=q.t[:, cc * 128:(cc + 1) * 128], identity=ident.t[:]),
                               reads=[q.r, ident.r], writes=[bankT], signal=(cc == ncc - 1), same_engine=False)
                      stg_fn()
                      yield

                  def silu_to(si, b, dst_ap, dst_r):
                      ge = scr[si]["ge"]
                      k.op("act", lambda: nc.scalar.activation(out=ge.t[:], in_=PSF[:, b, :], func=AF.Exp, scale=-1.0), reads=[bank[b]], writes=[ge.r])
                      yield
                      k.op("dve", lambda: nc.vector.tensor_scalar_add(out=ge.t[:], in0=ge.t[:], scalar1=1.0), reads=[ge.r], writes=[ge.r])
                      yield
                      k.op("dve", lambda: nc.vector.reciprocal(out=ge.t[:], in_=ge.t[:]), reads=[ge.r], writes=[ge.r])
                      yield
                      k.op("dve", lambda: nc.vector.tensor_tensor(out=dst_ap, in0=ge.t[:], in1=PSF[:, b, :], op=ALU.mult), reads=[ge.r, bank[b]], writes=[dst_r])
                      yield

                  xv = x_in.rearrange("(n p) d -> n p d", p=128)
                  k.dma("sp", xt[0].t[:], xv[0], xt[0].ds, writes=[xt[0].r])
                  bsel = 0
                  for T in range(NT):
                      s_ = T % 2
                      for j in range(4):
                          blk = T * 4 + j
                          if blk + 1 < NB:
                              nx = xt[(blk + 1) % 3]
                              k.dma("sp", nx.t[:], xv[blk + 1], nx.ds, writes=[nx.r])
                          xb = xt[blk % 3]
                          hb = htm[blk % 2]
                          norm_mod(xb, hb, gs0, sh0)
                          hTb = hT[blk % 2]
                          transpose8(lambda kc, hb=hb: hb.t[:, kc * 128:(kc + 1) * 128], hb.r, hTb.t[:], hTb.r, eng="act")
                          for ch in range(7):
                              b = ch
                              c0 = ch * 512
                              W = 512 if ch < 6 else 256
                              for kc in range(8):
                                  k.op("pe", lambda kc=kc, b=b, c0=c0, W=W, hTb=hTb: nc.tensor.matmul(PSF[:, b, 0:W], lhsT=hTb.t[:, kc, :], rhs=w0.t[:, kc, c0:c0 + W],
                                                                                                      start=(kc == 0), stop=(kc == 7)),
                                       reads=[hTb.r, w0.r], writes=[bank[b]], signal=(kc == 7), same_engine=False)

                          def stg4(stg, j=j):
                              return lambda: k.op("act", lambda: nc.scalar.copy(out=stg.t[:, :, j * 128:(j + 1) * 128], in_=PST[:, 0:512].rearrange("p (c t) -> p c t", c=4)),
                                                  reads=[bankT], writes=[stg.r])

                          def stg1(stg, j=j):
                              return lambda: k.op("act", lambda: nc.scalar.copy(out=stg.t[:, j * 128:(j + 1) * 128], in_=PST[:, 0:128]), reads=[bankT], writes=[stg.r])

                          interleave(normrope(0, 0, 8, gn.t[:, 0:64], blk, stg4(st_qa[s_])), normrope(1, 1, 8, gn.t[:, 64:128], blk, stg4(st_ka[s_])))
                          k.op("act", lambda j=j: nc.scalar.copy(out=st_va[s_].t[:, j, :], in_=PSF[:, 2, :]), reads=[bank[2]], writes=[st_va[s_].r])
                          k.op("act", lambda j=j: nc.scalar.copy(out=st_vb[s_].t[:, j, :], in_=PSF[:, 6, 128:256]), reads=[bank[6]], writes=[st_vb[s_].r])
                          interleave(silu_to(0, 3, st_ga[s_].t[:, j, :], st_ga[s_].r), silu_to(1, 5, st_gb[s_].t[:, j, :], st_gb[s_].r))
                          interleave(normrope(0, 4, 8, gn.t[:, 128:192], blk, stg4(st_qb[s_])), normrope(1, 6, 2, gn.t[:, 192:256], blk, stg1(st_kb[s_])))
                      t0 = T * 512
                      for hh_ in range(4):
                          k.dma("pool", QAT[hh_, :, t0:t0 + 512], st_qa[s_].t[:, hh_, :], st_qa[s_].ds, reads=[st_qa[s_].r])
                          k.dma("pool", KAT[hh_, :, t0:t0 + 512], st_ka[s_].t[:, hh_, :], st_ka[s_].ds, reads=[st_ka[s_].r])
                          k.dma("pool", QBT[hh_, :, t0:t0 + 512], st_qb[s_].t[:, hh_, :], st_qb[s_].ds, reads=[st_qb[s_].r])
                      k.dma("pool", KBT[:, t0:t0 + 512], st_kb[s_].t[:], st_kb[s_].ds, reads=[st_kb[s_].r])
                      k.dma("pool", VA[t0:t0 + 512, :].rearrange("(j p) c -> p j c", p=128), st_va[s_].t[:], st_va[s_].ds, reads=[st_va[s_].r])
                      k.dma("pool", VB[t0:t0 + 512, :].rearrange("(j p) c -> p j c", p=128), st_vb[s_].t[:], st_vb[s_].ds, reads=[st_vb[s_].r])
                      k.dma("pool", SGA[t0:t0 + 512, :].rearrange("(j p) c -> p j c", p=128), st_ga[s_].t[:], st_ga[s_].ds, reads=[st_ga[s_].r])
                      k.dma("pool", SGB[t0:t0 + 512, :].rearrange("(j p) c -> p j c", p=128), st_gb[s_].t[:], st_gb[s_].ds, reads=[st_gb[s_].r])

        if dbg == "p0":
            k.barrier()
            return nc, k

        if L0:
            lvl = int(str(dbg)[3:]) if str(dbg).startswith("b0x") else 99
            with (Phase() if lvl == 99 else ExitStack()):
              if lvl == 99:
                  lv = sb("lv", [128, 256], F32)
                  lp = sb("lp", [128, 128], F32)
                  l2 = sb("l2", [128, 2], F32)
                  nlam = sb("nlam", [128, 1], F32)
                  k.dma("sp", lv.t[:], lamv.partition_broadcast(128), lv.ds, writes=[lv.r])
                  lv4 = lv.t[:].rearrange("p (a d) -> p a d", d=64)
                  k.op("dve", lambda: nc.vector.tensor_tensor(out=lp.t[:, 0:64], in0=lv4[:, 0, :], in1=lv4[:, 1, :], op=ALU.mult), reads=[lv.r], writes=[lp.r])
                  k.op("dve", lambda: nc.vector.tensor_tensor(out=lp.t[:, 64:128], in0=lv4[:, 2, :], in1=lv4[:, 3, :], op=ALU.mult), reads=[lv.r, lp.r], writes=[lp.r])
                  k.op("dve", lambda: nc.vector.tensor_reduce(out=l2.t[:], in_=lp.t[:].rearrange("p (a d) -> p a d", d=64), axis=AX.X, op=ALU.add), reads=[lp.r], writes=[l2.r])
                  k.op("act", lambda: nc.scalar.activation(out=l2.t[:], in_=l2.t[:], func=AF.Exp), reads=[l2.r], writes=[l2.r])
                  k.op("dve", lambda: nc.vector.tensor_tensor(out=nlam.t[:], in0=l2.t[:, 1:2], in1=l2.t[:, 0:1], op=ALU.subtract), reads=[l2.r], writes=[nlam.r])
                  k.op("dve", lambda: nc.vector.tensor_scalar_add(out=nlam.t[:], in0=nlam.t[:], scalar1=-0.2), reads=[nlam.r], writes=[nlam.r])
                  gsub = sb("gsub", [128, 128], F32)
                  k.dma("sp", gsub.t[:], subg.partition_broadcast(128), gsub.ds, writes=[gsub.r])
                  k.op("dve", lambda: nc.vector.tensor_scalar_mul(out=gsub.t[:], in0=gsub.t[:], scalar1=0.8), reads=[gsub.r], writes=[gsub.r])
                  KT = sb("KT", [128, S], BF16)
                  QT = sb("QT", [128, S], BF16)
                  V1 = sb("V1", [128, NB, 132], BF16)
                  SGh = sb("SGh", [128, NB, 128], BF16)
                  et = slots("et", [128, 2, 512], BF16, 2)
                  ya = sb("ya", [128, 128], F32)
                  yn = sb("yn", [128, 128], F32)
                  z2 = sb("z2", [128, 2], F32)
                  yst = slots("yst", [128, 4, 128], BF16, 2)
                  k.op("dve", lambda: nc.vector.memset(V1.t[:, :, 128:129], 1.0), writes=[V1.r])
                  accR = [bank[4 + a_ // 3] for a_ in range(8)]

                  def acc_ap(c_, j_, lo, hi):
                      a_ = c_ * 4 + j_
                      return PSF[:, 4 + a_ // 3, (a_ % 3) * 132 + lo:(a_ % 3) * 132 + hi]

                  it = 0
                  for h in range(4):
                      k.dma("sp", KT.t[:], KAT[h], KT.ds, writes=[KT.r])
                      k.dma("sp", QT.t[:], QAT[h], QT.ds, writes=[QT.r])
                      k.dma("sp", V1.t[:, :, 0:128], VA[:, h * 128:(h + 1) * 128].rearrange("(n p) c -> p n c", p=128), V1.ds, writes=[V1.r])
                      k.dma("sp", SGh.t[:], SGA[:, h * 128:(h + 1) * 128].rearrange("(n p) c -> p n c", p=128), SGh.ds, writes=[SGh.r])
                      steps = [(qt, kb) for qt in range(NT) for kb in range(4 * qt + 4)]

                      def stage_s(i):
                          qt, kb = steps[i]
                          pr = i % 2
                          e_ = et[pr]
                          for c_ in range(2):
                              k.op("pe", lambda c_=c_, pr=pr, kb=kb, qt=qt: nc.tensor.matmul(PSF[:, 2 * pr + c_, :], lhsT=KT.t[c_ * 64:(c_ + 1) * 64, kb * 128:(kb + 1) * 128],
                                                                                         rhs=QT.t[c_ * 64:(c_ + 1) * 64, qt * 512:(qt + 1) * 512], start=True, stop=True),
                                   reads=[KT.r, QT.r], writes=[bank[2 * pr + c_]], signal=(c_ == 1), same_engine=False)
                          k.op("act", lambda pr=pr, e_=e_: nc.scalar.activation(out=e_.t[:], in_=PSF[:, 2 * pr:2 * pr + 2, :], func=AF.Exp, scale=0.125),
                               reads=[bank[2 * pr], bank[2 * pr + 1]], writes=[e_.r])
                          r_ = kb - 4 * qt
                          if r_ >= 0:
                              k.op("dve", lambda e_=e_, r_=r_: nc.vector.tensor_tensor(out=e_.t[:, :, r_ * 128:(r_ + 1) * 128], in0=e_.t[:, :, r_ * 128:(r_ + 1) * 128],
                                                                                     in1=tri.t[:].unsqueeze(1).to_broadcast([128, 2, 128]), op=ALU.mult),
                                   reads=[e_.r, tri.r], writes=[e_.r])

                      def stage_pv(i):
                          qt, kb = steps[i]
                          e_ = et[i % 2]
                          ys = yst[qt % 2]
                          r_ = kb - 4 * qt
                          if kb == 0:
                              started.clear()
                          for j_ in range(4):
                              if r_ > j_:
                                  continue
                              last = (kb == 4 * qt + j_)
                              for c_ in range(2):
                                  bk_ = 4 + (c_ * 4 + j_) // 3
                                  st_ = (kb == 0 and bk_ not in started)
                                  started.add(bk_)
                                  k.op("pe", lambda c_=c_, j_=j_, e_=e_, kb=kb, last=last, st_=st_: nc.tensor.matmul(acc_ap(c_, j_, 0, 129), lhsT=e_.t[:, c_, j_ * 128:(j_ + 1) * 128],
                                                                                                       rhs=V1.t[:, kb, 0:129], start=st_, stop=last, skip_group_check=True),
                                       reads=[e_.r, V1.r], writes=[accR[c_ * 4 + j_]], signal=(last or (j_ == 3 and c_ == 1)), same_engine=False)
                              if last:
                                  blk = 4 * qt + j_
                                  a0, a1 = accR[j_], accR[4 + j_]
                                  k.op("dve", lambda j_=j_: nc.vector.tensor_copy(out=z2.t[:, 0:1], in_=acc_ap(0, j_, 128, 129)), reads=[a0], writes=[z2.r])
                                  k.op("dve", lambda j_=j_: nc.vector.tensor_copy(out=z2.t[:, 1:2], in_=acc_ap(1, j_, 128, 129)), reads=[a1, z2.r], writes=[z2.r])
                                  k.op("dve", lambda: nc.vector.reciprocal(out=z2.t[:], in_=z2.t[:]), reads=[z2.r], writes=[z2.r])
                                  k.op("dve", lambda: nc.vector.tensor_tensor(out=z2.t[:, 1:2], in0=z2.t[:, 1:2], in1=nlam.t[:], op=ALU.mult), reads=[z2.r, nlam.r], writes=[z2.r])
                                  k.op("dve", lambda j_=j_: nc.vector.tensor_scalar(out=ya.t[:], in0=acc_ap(0, j_, 0, 128), scalar1=z2.t[:, 0:1], scalar2=None, op0=ALU.mult),
                                       reads=[a0, z2.r], writes=[ya.r])
                                  k.op("dve", lambda j_=j_: nc.vector.scalar_tensor_tensor(out=ya.t[:], in0=acc_ap(1, j_, 0, 128), scalar=z2.t[:, 1:2], in1=ya.t[:], op0=ALU.mult, op1=ALU.add),
                                       reads=[a1, z2.r, ya.r], writes=[ya.r])
                                  k.op("act", lambda: nc.scalar.activation(out=yn.t[:], in_=ya.t[:], func=AF.Square, accum_out=ssq.t[:]), reads=[ya.r], writes=[yn.r, ssq.r])
                                  k.op("act", lambda: nc.scalar.activation(out=rstd.t[:], in_=ssq.t[:], func=AF.Ln, scale=1.0 / 128, bias=EPS), reads=[ssq.r], writes=[rstd.r])
                                  k.op("act", lambda: nc.scalar.activation(out=rstd.t[:], in_=rstd.t[:], func=AF.Exp, scale=-0.5), reads=[rstd.r], writes=[rstd.r])
                                  k.op("dve", lambda: nc.vector.scalar_tensor_tensor(out=yn.t[:], in0=ya.t[:], scalar=rstd.t[:, 0:1], in1=gsub.t[:], op0=ALU.mult, op1=ALU.mult),
                                       reads=[ya.r, rstd.r, gsub.r, yn.r], writes=[yn.r])
                                  k.op("dve", lambda j_=j_, blk=blk, ys=ys: nc.vector.tensor_tensor(out=ys.t[:, j_, :], in0=yn.t[:], in1=SGh.t[:, blk, :], op=ALU.mult),
                                       reads=[yn.r, SGh.r], writes=[ys.r])
                          if kb == 4 * qt + 3:
                              k.dma("pool", Y0[qt * 512:(qt + 1) * 512, h * 128:(h + 1) * 128].rearrange("(j p) c -> p j c", p=128), ys.t[:], ys.ds, reads=[ys.r])

                      started = set()
                      stage_s(0)
                      for i in range(len(steps)):
                          if i + 1 < len(steps):
                              stage_s(i + 1)
                          stage_pv(i)

            if dbg == "a0":
                k.barrier()
                return nc, k
            with Phase():
                esk = sb("esk", [128, 8], F32)
                k.dma("sp", esk.t[:], sinks.partition_broadcast(128), esk.ds, writes=[esk.r])
                k.op("act", lambda: nc.scalar.activation(out=esk.t[:], in_=esk.t[:], func=AF.Exp), reads=[esk.r], writes=[esk.r])
                qbt = slots("qbt", [128, 4, 512], BF16, 2)
                kbt = slots("kbt", [128, 640], BF16, 2)
                vb1 = slots("vb1", [128, 5, 2, 72], BF16, 2)
                sgb = slots("sgb", [128, 4, 512], BF16, 2)
                eb = slots("eb", [128, 16, 128], BF16, 2)
                zz = sb("zz", [128, 8], F32)
                ybt = sb("ybt", [128, 8, 64], F32)
                ysb = slots("ysb", [128, 4, 512], BF16, 2)
                for v_ in vb1:
                    k.op("dve", lambda v_=v_: nc.vector.memset(v_.t[:, :, :, 64:65], 1.0), writes=[v_.r])

                def b0_load(T):
                    s_ = T % 2
                    t0 = T * 512
                    for p_ in range(4):
                        k.dma("sp", qbt[s_].t[:, p_, :], QBT[p_, :, t0:t0 + 512], qbt[s_].ds, writes=[qbt[s_].r])
                    if T > 0:
                        k.dma("sp", kbt[s_].t[:, :], KBT[:, t0 - 128:t0 + 512], kbt[s_].ds, writes=[kbt[s_].r])
                        for g_ in range(2):
                            k.dma("sp", vb1[s_].t[:, :, g_, 0:64], VB[t0 - 128:t0 + 512, g_ * 64:(g_ + 1) * 64].rearrange("(n p) d -> p n d", p=128), vb1[s_].ds, writes=[vb1[s_].r])
                    else:
                        k.dma("sp", kbt[s_].t[:, 128:640], KBT[:, 0:512], kbt[s_].ds, writes=[kbt[s_].r])
                        for g_ in range(2):
                            k.dma("sp", vb1[s_].t[:, 1:5, g_, 0:64], VB[0:512, g_ * 64:(g_ + 1) * 64].rearrange("(n p) d -> p n d", p=128), vb1[s_].ds, writes=[vb1[s_].r])
                    k.dma("sp", sgb[s_].t[:], SGB[t0:t0 + 512, :].rearrange("(j p) c -> p j c", p=128), sgb[s_].ds, writes=[sgb[s_].r])

                PSflat = PSF[:, 0:4, :].rearrange("p a b -> p (a b)")
                b0_load(0)
                for T in range(NT):
                    s_ = T % 2
                    if T + 1 < NT:
                        b0_load(T + 1)
                    for j_ in range(4):
                        if lvl < 2:
                            break
                        blk = 4 * T + j_
                        e_ = eb[blk % 2]
                        kks = (0, 1) if blk > 0 else (1,)
                        for kk in kks:
                            for p_ in range(4):
                                for hf in range(2):
                                    off = (kk * 8 + hf * 4 + p_) * 128
                                    lastmm = (kk == 1 and p_ == 3 and hf == 1)
                                    k.op("pe", lambda off=off, hf=hf, kk=kk, p_=p_, j_=j_, s_=s_: nc.tensor.matmul(
                                        PSflat[:, off:off + 128], lhsT=kbt[s_].t[hf * 64:(hf + 1) * 64, (j_ + kk) * 128:(j_ + kk + 1) * 128],
                                        rhs=qbt[s_].t[hf * 64:(hf + 1) * 64, p_, j_ * 128:(j_ + 1) * 128], start=True, stop=True),
                                         reads=[kbt[s_].r, qbt[s_].r], writes=[bank[0], bank[1], bank[2], bank[3]], signal=lastmm, same_engine=False)
                        for kk in kks:
                            lo = kk * 8
                            k.op("act", lambda lo=lo, e_=e_: nc.scalar.activation(out=e_.t[:, lo:lo + 8, :], in_=PSflat[:, lo * 128:(lo + 8) * 128].rearrange("p (a t) -> p a t", t=128), func=AF.Exp, scale=0.125),
                                 reads=[bank[0], bank[1], bank[2], bank[3]], writes=[e_.r])
                        if lvl < 3:
                            continue
                        if blk > 0:
                            k.op("dve", lambda e_=e_: nc.vector.tensor_tensor(out=e_.t[:, 0:8, :], in0=e_.t[:, 0:8, :], in1=upp.t[:].unsqueeze(1).to_broadcast([128, 8, 128]), op=ALU.mult),
                                 reads=[e_.r, upp.r], writes=[e_.r])
                        k.op("dve", lambda e_=e_: nc.vector.tensor_tensor(out=e_.t[:, 8:16, :], in0=e_.t[:, 8:16, :], in1=tri.t[:].unsqueeze(1).to_broadcast([128, 8, 128]), op=ALU.mult),
                             reads=[e_.r, tri.r], writes=[e_.r])
                        if lvl < 4:
                            continue
                        for p_ in range(4):
                            for hf in range(2):
                                hd = hf * 4 + p_
                                for kk in kks:
                                    k.op("pe", lambda hd=hd, hf=hf, kk=kk, p_=p_, j_=j_, s_=s_, e_=e_, kks=kks: nc.tensor.matmul(
                                        PSF[:, 4 + hd // 4, (hd % 4) * 80:(hd % 4) * 80 + 65], lhsT=e_.t[:, kk * 8 + hf * 4 + p_, :],
                                        rhs=vb1[s_].t[:, j_ + kk, hf, 0:65], start=(kk == kks[0]), stop=(kk == 1)),
                                         reads=[e_.r, vb1[s_].r], writes=[bank[4], bank[5]], signal=(kk == 1 and p_ == 3 and hf == 1), same_engine=False)
                        if lvl < 5:
                            continue
                        for bb in range(2):
                            k.op("dve", lambda bb=bb: nc.vector.tensor_copy(out=zz.t[:, bb * 4:(bb + 1) * 4], in_=PSF[:, 4 + bb, 0:320].rearrange("p (h d) -> p h d", d=80)[:, :, 64]),
                                 reads=[bank[4], bank[5], zz.r], writes=[zz.r])
                        k.op("dve", lambda: nc.vector.tensor_tensor(out=zz.t[:], in0=zz.t[:], in1=esk.t[:], op=ALU.add), reads=[zz.r, esk.r], writes=[zz.r])
                        k.op("dve", lambda: nc.vector.reciprocal(out=zz.t[:], in_=zz.t[:]), reads=[zz.r], writes=[zz.r])
                        for bb in range(2):
                            k.op("dve", lambda bb=bb: nc.vector.tensor_tensor(out=ybt.t[:, bb * 4:(bb + 1) * 4, :], in0=PSF[:, 4 + bb, 0:320].rearrange("p (h d) -> p h d", d=80)[:, :, 0:64],
                                                                            in1=zz.t[:, bb * 4:(bb + 1) * 4].unsqueeze(2).to_broadcast([128, 4, 64]), op=ALU.mult),
                                 reads=[bank[4], bank[5], zz.r, ybt.r], writes=[ybt.r])
                        k.op("dve", lambda j_=j_, s_=s_: nc.vector.tensor_tensor(out=ysb[s_].t[:, j_, :], in0=ybt.t[:].rearrange("p h d -> p (h d)"), in1=sgb[s_].t[:, j_, :], op=ALU.mult),
                             reads=[ybt.r, sgb[s_].r], writes=[ysb[s_].r])
                    if lvl >= 6:
                        k.dma("pool", Y0[T * 512:(T + 1) * 512, 512:1024].rearrange("(j p) c -> p j c", p=128), ysb[s_].t[:], ysb[s_].ds, reads=[ysb[s_].r])

            if dbg == "b0" or lvl != 99:
                k.barrier()
                return nc, k
            with Phase():
                wo = sb("wo", [128, 8, D], BF16)
                k.dma("pool", wo.t[:], e_w_out.rearrange("(kc p) n -> p kc n", p=128), wo.ds, writes=[wo.r])
                yb_ = slots("yb_", [128, D], BF16, 2)
                yT = slots("yT", [128, 8, 128], BF16, 2)
                x1t = slots("x1t", [128, D], F32, 2)
                xv = x_in.rearrange("(n p) d -> n p d", p=128)
                x1v = x1.rearrange("(n p) d -> n p d", p=128)
                y0v = Y0.rearrange("(n p) d -> n p d", p=128)
                k.dma("sp", xt[0].t[:], xv[0], xt[0].ds, writes=[xt[0].r])
                k.dma("sp", yb_[0].t[:], y0v[0], yb_[0].ds, writes=[yb_[0].r])
                for blk in range(NB):
                    if blk + 1 < NB:
                        k.dma("sp", xt[(blk + 1) % 3].t[:], xv[blk + 1], xt[(blk + 1) % 3].ds, writes=[xt[(blk + 1) % 3].r])
                        k.dma("sp", yb_[(blk + 1) % 2].t[:], y0v[blk + 1], yb_[(blk + 1) % 2].ds, writes=[yb_[(blk + 1) % 2].r])
                    xb, yb, yTb, xo = xt[blk % 3], yb_[blk % 2], yT[blk % 2], x1t[blk % 2]
                    transpose8(lambda kc, yb=yb: yb.t[:, kc * 128:(kc + 1) * 128], yb.r, yTb.t[:], yTb.r, eng="act")
                    for cc in range(2):
                        b = cc
                        for kc in range(8):
                            k.op("pe", lambda kc=kc, b=b, cc=cc, yTb=yTb: nc.tensor.matmul(PSF[:, b, :], lhsT=yTb.t[:, kc, :], rhs=wo.t[:, kc, cc * 512:(cc + 1) * 512],
                                                                                       start=(kc == 0), stop=(kc == 7)),
                                 reads=[yTb.r, wo.r], writes=[bank[b]], signal=(kc == 7), same_engine=False)
                        k.op("dve", lambda b=b, cc=cc: nc.vector.tensor_tensor(out=junk.t[:, cc * 512:(cc + 1) * 512], in0=PSF[:, b, :], in1=gate0.t[:, cc * 512:(cc + 1) * 512], op=ALU.mult),
                             reads=[bank[b], gate0.r, junk.r], writes=[junk.r])
                        k.op("dve", lambda cc=cc, xo=xo, xb=xb: nc.vector.tensor_tensor(out=xo.t[:, cc * 512:(cc + 1) * 512], in0=junk.t[:, cc * 512:(cc + 1) * 512],
                                                                                   in1=xb.t[:, cc * 512:(cc + 1) * 512], op=ALU.add),
                             reads=[junk.r, xb.r, xo.r], writes=[xo.r])
                    k.dma("pool", x1v[blk], xo.t[:], xo.ds, reads=[xo.r])

        if dbg == "l0" or not L1:
            k.barrier()
            return nc, k

        sh1, gs1, gate1 = mod_vectors("1", o_w_mod, o_b_mod, o_norm_g)
        H1T = dram_scr("H1T", [8, 128, S], BF16)
        H1OT = dram_scr("H1OT", [8, 128, SO], BF16)
        X1O = dram_scr("X1O", [SO, D], F32)
        Y1T = dram_scr("Y1T", [16, 64, SO], BF16)
        om = sb("om", [128, 2], F32)
        k.dma("sp", om.t[:], ownm[:, :], om.ds, writes=[om.r])
        x1v = x1.rearrange("(n p) d -> n p d", p=128)
        x1ov = X1O.rearrange("(n p) d -> n p d", p=128)

        with Phase():
            st_h = slots("st_h", [128, 8, 512], BF16, 2)
            st_ho = slots("st_ho", [128, 8, 256], BF16, 2)
            hown = sb("hown", [128, D], BF16)
            hof = sb("hof", [128, D], F32)
            xo = slots("xo", [128, D], F32, 2)
            k.dma("sp", xt[0].t[:], x1v[0], xt[0].ds, writes=[xt[0].r])
            for T in range(NT):
                s_ = T % 2
                for j in range(4):
                    blk = T * 4 + j
                    if blk + 1 < NB:
                        nx = xt[(blk + 1) % 3]
                        k.dma("sp", nx.t[:], x1v[blk + 1], nx.ds, writes=[nx.r])
                    xb, hb = xt[blk % 3], htm[blk % 2]
                    norm_mod(xb, hb, gs1, sh1)
                    transpose8(lambda kc, hb=hb: hb.t[:, kc * 128:(kc + 1) * 128], hb.r, st_h[s_].t[:, :, j * 128:(j + 1) * 128], st_h[s_].r, eng="act")
                    if j % 2 == 1:
                        he, xe = htm[(blk - 1) % 2], xt[(blk - 1) % 3]
                        ob = blk // 2
                        k.op("dve", lambda he=he: nc.vector.tensor_scalar(out=hof.t[:], in0=he.t[:], scalar1=om.t[:, 0:1], scalar2=None, op0=ALU.mult),
                             reads=[he.r, om.r, hof.r], writes=[hof.r])
                        k.op("dve", lambda hb=hb: nc.vector.scalar_tensor_tensor(out=hown.t[:], in0=hb.t[:], scalar=om.t[:, 1:2], in1=hof.t[:], op0=ALU.mult, op1=ALU.add),
                             reads=[hb.r, om.r, hof.r, hown.r], writes=[hown.r])
                        transpose8(lambda kc: hown.t[:, kc * 128:(kc + 1) * 128], hown.r, st_ho[s_].t[:, :, (j // 2) * 128:(j // 2 + 1) * 128], st_ho[s_].r, eng="act")
                        xo_ = xo[ob % 2]
                        k.op("dve", lambda xe=xe: nc.vector.tensor_scalar(out=junk.t[:], in0=xe.t[:], scalar1=om.t[:, 0:1], scalar2=None, op0=ALU.mult),
                             reads=[xe.r, om.r, junk.r], writes=[junk.r])
                        k.op("dve", lambda xb=xb, xo_=xo_: nc.vector.scalar_tensor_tensor(out=xo_.t[:], in0=xb.t[:], scalar=om.t[:, 1:2], in1=junk.t[:], op0=ALU.mult, op1=ALU.add),
                             reads=[xb.r, om.r, junk.r, xo_.r], writes=[xo_.r])
                        k.dma("pool", x1ov[ob], xo_.t[:], xo_.ds, reads=[xo_.r])
                k.dma("pool", H1T[:, :, T * 512:(T + 1) * 512].rearrange("k p t -> p k t"), st_h[s_].t[:], st_h[s_].ds, reads=[st_h[s_].r])
                k.dma("pool", H1OT[:, :, T * 256:(T + 1) * 256].rearrange("k p t -> p k t"), st_ho[s_].t[:], st_ho[s_].ds, reads=[st_ho[s_].r])

        w1v = o_w_in.rearrange("(kc p) n -> p kc n", p=128)
        for g in range(4):
            with Phase():
                KTg = sb("KTg", [128, 2, S], BF16)
                Vg = sb("Vg", [128, NB, 256], BF16)
                QTg = sb("QTg", [128, 2, SO], BF16)
                SGT = sb("SGT", [64, 4, SO], BF16)
                with Phase():
                    wq = sb("wq", [128, 8, 256], BF16); wk = sb("wk", [128, 8, 256], BF16)
                    wv = sb("wv", [128, 8, 256], BF16); wg = sb("wg", [128, 8, 256], BF16)
                    for wt_, off in ((wq, 0), (wk, 1024), (wv, 2048), (wg, 3072)):
                        k.dma("pool", wt_.t[:], w1v[:, :, off + g * 256:off + (g + 1) * 256], wt_.ds, writes=[wt_.r])
                    h1t = slots("h1t", [128, 8, 512], BF16, 2)
                    ge1 = sb("ge1", [64, 512], F32)
                    k.dma("sp", h1t[0].t[:], H1T[:, :, 0:512].rearrange("k p t -> p k t"), h1t[0].ds, writes=[h1t[0].r])
                    bs = 0
                    for T in range(NT):
                        ht = h1t[T % 2]
                        if T + 1 < NT:
                            k.dma("sp", h1t[(T + 1) % 2].t[:], H1T[:, :, (T + 1) * 512:(T + 2) * 512].rearrange("k p t -> p k t"), h1t[(T + 1) % 2].ds, writes=[h1t[(T + 1) % 2].r])
                        for p_ in range(2):
                            b = bs; bs = (bs + 1) % 4
                            for kc in range(8):
                                k.op("pe", lambda kc=kc, b=b, p_=p_, ht=ht: nc.tensor.matmul(PSF[:, b, :], lhsT=wk.t[:, kc, p_ * 128:(p_ + 1) * 128], rhs=ht.t[:, kc, :], start=(kc == 0), stop=(kc == 7)),
                                     reads=[wk.r, ht.r], writes=[bank[b]], signal=(kc == 7), same_engine=False)
                            k.op("act", lambda b=b, p_=p_, T=T: nc.scalar.copy(out=KTg.t[:, p_, T * 512:(T + 1) * 512], in_=PSF[:, b, :]), reads=[bank[b]], writes=[KTg.r])
                        for j in range(4):
                            b = bs; bs = (bs + 1) % 4
                            for kc in range(8):
                                k.op("pe", lambda kc=kc, b=b, j=j, ht=ht: nc.tensor.matmul(PSF[:, b, 0:256], lhsT=ht.t[:, kc, j * 128:(j + 1) * 128], rhs=wv.t[:, kc, :], start=(kc == 0), stop=(kc == 7)),
                                     reads=[wv.r, ht.r], writes=[bank[b]], signal=(kc == 7), same_engine=False)
                            k.op("dve", lambda b=b, j=j, T=T: nc.vector.tensor_copy(out=Vg.t[:, T * 4 + j, :], in_=PSF[:, b, 0:256]), reads=[bank[b]], writes=[Vg.r])
                    k.dma("sp", h1t[0].t[:], H1OT[:, :, 0:512].rearrange("k p t -> p k t"), h1t[0].ds, writes=[h1t[0].r])
                    for TO in range(NTO):
                        ht = h1t[TO % 2]
                        if TO + 1 < NTO:
                            k.dma("sp", h1t[(TO + 1) % 2].t[:], H1OT[:, :, (TO + 1) * 512:(TO + 2) * 512].rearrange("k p t -> p k t"), h1t[(TO + 1) % 2].ds, writes=[h1t[(TO + 1) % 2].r])
                        for p_ in range(2):
                            b = bs; bs = (bs + 1) % 4
                            for kc in range(8):
                                k.op("pe", lambda kc=kc, b=b, p_=p_, ht=ht: nc.tensor.matmul(PSF[:, b, :], lhsT=wq.t[:, kc, p_ * 128:(p_ + 1) * 128], rhs=ht.t[:, kc, :], start=(kc == 0), stop=(kc == 7)),
                                     reads=[wq.r, ht.r], writes=[bank[b]], signal=(kc == 7), same_engine=False)
                            k.op("act", lambda b=b, p_=p_, TO=TO: nc.scalar.mul(out=QTg.t[:, p_, TO * 512:(TO + 1) * 512], in_=PSF[:, b, :], mul=0.125), reads=[bank[b]], writes=[QTg.r])
                        for hl in range(4):
                            b = bs; bs = (bs + 1) % 4
                            for kc in range(8):
                                k.op("pe", lambda kc=kc, b=b, hl=hl, ht=ht: nc.tensor.matmul(PSF[0:64, b, :], lhsT=wg.t[:, kc, hl * 64:(hl + 1) * 64], rhs=ht.t[:, kc, :], start=(kc == 0), stop=(kc == 7)),
                                     reads=[wg.r, ht.r], writes=[bank[b]], signal=(kc == 7), same_engine=False)
                            k.op("act", lambda b=b: nc.scalar.activation(out=ge1.t[:], in_=PSF[0:64, b, :], func=AF.Exp, scale=-1.0), reads=[bank[b]], writes=[ge1.r])
                            k.op("dve", lambda: nc.vector.tensor_scalar_add(out=ge1.t[:], in0=ge1.t[:], scalar1=1.0), reads=[ge1.r], writes=[ge1.r])
                            k.op("dve", lambda: nc.vector.reciprocal(out=ge1.t[:], in_=ge1.t[:]), reads=[ge1.r], writes=[ge1.r])
                            k.op("dve", lambda b=b, hl=hl, TO=TO: nc.vector.tensor_tensor(out=SGT.t[:, hl, TO * 512:(TO + 1) * 512], in0=ge1.t[:], in1=PSF[0:64, b, :], op=ALU.mult),
                                 reads=[ge1.r, bank[b]], writes=[SGT.r])
                with Phase():
                    dm = sb("dm", [128, 8, 512], BF16)
                    k.dma("pool", dm.t[:], dmask.rearrange("p (r q) -> p r q", r=8), dm.ds, writes=[dm.r])
                    e32 = slots("e32", [128, 2, 512], F32, 3)
                    sp_ = slots("sp_", [128, 2, 512], BF16, 3)
                    ex = [slots("ex%d" % i_, [128, 512], F32, 2) for i_ in range(2)]
                    ww = [slots("ww%d" % i_, [128, 512], BF16, 2) for i_ in range(2)]
                    yst1 = slots("yst1", [64, 2, 512], BF16, 2)
                    fin = 0
                    for m in range(NTO):
                        for p_ in range(2):
                            kbs = list(range(8 * m + 7, -1, -1))

                            def cst(kb):
                                r_ = kb - 8 * m
                                return 0 if r_ < 2 else ((r_ - 1 + 1) // 2) * 128

                            def stage1(kb, i):
                                sl = i % 3
                                c0 = cst(kb)
                                for hf in range(2):
                                    k.op("pe", lambda hf=hf, kb=kb, c0=c0: nc.tensor.matmul(PSF[:, hf, c0:512], lhsT=KTg.t[hf * 64:(hf + 1) * 64, p_, kb * 128:(kb + 1) * 128],
                                                                                   rhs=QTg.t[hf * 64:(hf + 1) * 64, p_, m * 512 + c0:(m + 1) * 512], start=True, stop=True),
                                         reads=[KTg.r, QTg.r], writes=[bank[hf]], signal=(hf == 1), same_engine=False)
                                k.op("act", lambda sl=sl, c0=c0: nc.scalar.activation(out=e32[sl].t[:, :, c0:512], in_=PSF[:, 0:2, c0:512], func=AF.Exp), reads=[bank[0], bank[1]], writes=[e32[sl].r])
                                k.op("act", lambda sl=sl, c0=c0: nc.scalar.activation(out=sp_[sl].t[:, :, c0:512], in_=e32[sl].t[:, :, c0:512], func=AF.Ln, bias=1.0), reads=[e32[sl].r], writes=[sp_[sl].r])
                                r_ = kb - 8 * m
                                if r_ >= 0:
                                    mk = dm.t[:, r_, c0:512].unsqueeze(1).to_broadcast([128, 2, 512 - c0])
                                    k.op("dve", lambda sl=sl, mk=mk, c0=c0: nc.vector.tensor_tensor(out=sp_[sl].t[:, :, c0:512], in0=sp_[sl].t[:, :, c0:512], in1=mk, op=ALU.mult), reads=[sp_[sl].r, dm.r], writes=[sp_[sl].r])
                                    k.op("dve", lambda sl=sl, mk=mk, c0=c0: nc.vector.tensor_tensor(out=e32[sl].t[:, :, c0:512], in0=e32[sl].t[:, :, c0:512], in1=mk, op=ALU.mult), reads=[e32[sl].r, dm.r], writes=[e32[sl].r])

                            def stage2a(kb, i, first, last):
                                sl, s2 = i % 3, i % 2
                                c0 = cst(kb)
                                for hf in range(2):
                                    k.op("pe", lambda hf=hf, sl=sl, c0=c0: nc.tensor.matmul(PSF[:, 2 + hf, c0:512], lhsT=negtri.t[:], rhs=sp_[sl].t[:, hf, c0:512], start=first, stop=True, skip_group_check=True),
                                         reads=[negtri.r, sp_[sl].r], writes=[bank[2 + hf]], same_engine=False)
                                for hf in range(2):
                                    k.op("act", lambda s2=s2, hf=hf, c0=c0: nc.scalar.activation(out=ex[hf][s2].t[:, c0:512], in_=PSF[:, 2 + hf, c0:512], func=AF.Exp), reads=[bank[2 + hf]], writes=[ex[hf][s2].r])
                                    k.op("dve", lambda sl=sl, s2=s2, hf=hf, c0=c0: nc.vector.tensor_tensor(out=ww[hf][s2].t[:, c0:512], in0=e32[sl].t[:, hf, c0:512], in1=ex[hf][s2].t[:, c0:512], op=ALU.mult),
                                         reads=[e32[sl].r, ex[hf][s2].r], writes=[ww[hf][s2].r])

                            def stage2b(kb, i, first, last):
                                sl, s2 = i % 3, i % 2
                                c0 = cst(kb)
                                if not last:
                                    for hf in range(2):
                                        k.op("pe", lambda hf=hf, sl=sl, c0=c0: nc.tensor.matmul(PSF[:, 2 + hf, c0:512], lhsT=negrest.t[:], rhs=sp_[sl].t[:, hf, c0:512], start=False, stop=True, skip_group_check=True),
                                             reads=[negrest.r, sp_[sl].r], writes=[bank[2 + hf]], same_engine=False)
                                for hf in range(2):
                                    k.op("pe", lambda hf=hf, s2=s2, kb=kb, c0=c0: nc.tensor.matmul(PSF[0:64, 4 + hf, c0:512], lhsT=Vg.t[:, kb, (p_ * 2 + hf) * 64:(p_ * 2 + hf + 1) * 64], rhs=ww[hf][s2].t[:, c0:512],
                                                                                          start=first, stop=last, skip_group_check=True),
                                         reads=[Vg.r, ww[hf][s2].r], writes=[bank[4 + hf]], same_engine=False)

                            stage1(kbs[0], 0)
                            if len(kbs) > 1:
                                stage1(kbs[1], 1)
                            for i, kb in enumerate(kbs):
                                stage2a(kb, i, i == 0, kb == 0)
                                if i + 2 < len(kbs):
                                    stage1(kbs[i + 2], i + 2)
                                stage2b(kb, i, i == 0, kb == 0)
                            ys = yst1[fin % 2]
                            fin += 1
                            for hf in range(2):
                                hl = p_ * 2 + hf
                                k.op("dve", lambda hf=hf, hl=hl, ys=ys: nc.vector.tensor_tensor(out=ys.t[:, hf, :], in0=PSF[0:64, 4 + hf, :], in1=SGT.t[:, hl, m * 512:(m + 1) * 512], op=ALU.mult),
                                     reads=[bank[4 + hf], SGT.r, ys.r], writes=[ys.r])
                            for hf in range(2):
                                k.dma("pool", Y1T[g * 4 + p_ * 2 + hf, :, m * 512:(m + 1) * 512], ys.t[:, hf, :], ys.ds, reads=[ys.r])

        with Phase():
            wo1 = sb("wo1", [64, 16, D], BF16)
            k.dma("pool", wo1.t[:], o_w_out.rearrange("(h p) n -> p h n", p=64), wo1.ds, writes=[wo1.r])
            yt1 = slots("yt1", [64, 16, 128], BF16, 2)
            outt = slots("outt", [128, D], F32, 2)
            outv = out_o.rearrange("(n p) d -> n p d", p=128)
            NOB = SO // 128
            k.dma("sp", xt[0].t[:], x1ov[0], xt[0].ds, writes=[xt[0].r])
            k.dma("sp", yt1[0].t[:], Y1T[:, :, 0:128].rearrange("h p t -> p h t"), yt1[0].ds, writes=[yt1[0].r])
            for ob in range(NOB):
                if ob + 1 < NOB:
                    k.dma("sp", xt[(ob + 1) % 3].t[:], x1ov[ob + 1], xt[(ob + 1) % 3].ds, writes=[xt[(ob + 1) % 3].r])
                    k.dma("sp", yt1[(ob + 1) % 2].t[:], Y1T[:, :, (ob + 1) * 128:(ob + 2) * 128].rearrange("h p t -> p h t"), yt1[(ob + 1) % 2].ds, writes=[yt1[(ob + 1) % 2].r])
                xb, yt, oo = xt[ob % 3], yt1[ob % 2], outt[ob % 2]
                for cc in range(2):
                    b = cc
                    for h in range(16):
                        k.op("pe", lambda h=h, b=b, cc=cc, yt=yt: nc.tensor.matmul(PSF[:, b, :], lhsT=yt.t[:, h, :], rhs=wo1.t[:, h, cc * 512:(cc + 1) * 512], start=(h == 0), stop=(h == 15)),
                             reads=[yt.r, wo1.r], writes=[bank[b]], signal=(h == 15), same_engine=False)
                    k.op("dve", lambda b=b, cc=cc: nc.vector.tensor_tensor(out=junk.t[:, cc * 512:(cc + 1) * 512], in0=PSF[:, b, :], in1=gate1.t[:, cc * 512:(cc + 1) * 512], op=ALU.mult),
                         reads=[bank[b], gate1.r, junk.r], writes=[junk.r])
                    k.op("dve", lambda cc=cc, oo=oo, xb=xb: nc.vector.tensor_tensor(out=oo.t[:, cc * 512:(cc + 1) * 512], in0=junk.t[:, cc * 512:(cc + 1) * 512], in1=xb.t[:, cc * 512:(cc + 1) * 512], op=ALU.add),
                         reads=[junk.r, xb.r, oo.r], writes=[oo.r])
                k.dma("pool", outv[ob], oo.t[:], oo.ds, reads=[oo.r])
        k.barrier()
    return nc, k


def _consts():
    s = np.arange(128)[:, None]
    t = np.arange(128)[None, :]
    ident = (s == t).astype(np.float32)
    tri = (s <= t).astype(np.float32)
    upp = (s > t).astype(np.float32)
    negtri = -(s >= t).astype(np.float32)
    inv = (10000.0 ** (-np.arange(32, dtype=np.float32) / np.float32(32))).astype(np.float32)
    return np.concatenate([ident, tri, upp, negtri, np.broadcast_to(inv[None, :], (128, 32))], axis=1).astype(np.float32)


def host_inputs(inp, b, hh, S, layers=(0, 1)):
    f = lambda a: np.ascontiguousarray(np.asarray(a), dtype=np.float32)
    m = {"cT": f(np.asarray(inp["c"])[b].reshape(8, 128).T), "consts": _consts()}
    if 0 in layers:
        m["x"] = f(np.asarray(inp["x"])[b, :S])
        m["posT"] = np.ascontiguousarray(np.asarray(inp["positions"])[b, :S].reshape(S // 128, 128).T.astype(np.int32))
        m["e_norm_g"] = f(inp["even_norm_g"]).reshape(1, D)
        m["e_w_mod"] = f(inp["even_w_mod"])[0]
        m["e_b_mod"] = f(inp["even_b_mod"]).reshape(1, 3 * D)
        w = f(inp["even_w_in"])[0]
        qa, ka, va, ga, qb, kb, vb, gb = np.split(w, np.cumsum([512, 512, 512, 512, 512, 128, 128])[:], axis=1)
        qbp = qb.reshape(D, 2, 4, 64).transpose(0, 2, 1, 3).reshape(D, 512)
        m["e_w_in"] = np.ascontiguousarray(np.concatenate([qa, ka, va, ga, qbp, gb, kb, vb], axis=1))
        m["e_w_out"] = f(inp["even_w_out"])[0]
        m["gains"] = np.concatenate([f(inp["a_q_gain"])[0], f(inp["a_k_gain"])[0], f(inp["b_q_gain"])[0], f(inp["b_k_gain"])[0]]).reshape(1, 256)
        m["lamv"] = np.concatenate([f(inp["a_lambda_q1"])[0], f(inp["a_lambda_k1"])[0], f(inp["a_lambda_q2"])[0], f(inp["a_lambda_k2"])[0]]).reshape(1, 256)
        m["subg"] = f(inp["a_subln_g"]).reshape(1, 128)
        m["sinks"] = f(inp["b_sinks"]).reshape(1, 8)
    if 1 in layers:
        m["o_norm_g"] = f(inp["odd_norm_g"]).reshape(1, D)
        m["o_w_mod"] = f(inp["odd_w_mod"])[0]
        m["o_b_mod"] = f(inp["odd_b_mod"]).reshape(1, 3 * D)
        m["o_w_in"] = f(inp["odd_w_in"])[0]
        m["o_w_out"] = f(inp["odd_w_out"])[0]
        m["ownm"] = np.ascontiguousarray(np.broadcast_to(np.array([[1.0 - hh, float(hh)]], np.float32), (128, 2)))
        s = np.arange(128)[:, None, None, None]
        r = np.arange(8)[None, :, None, None]
        jj = np.arange(4)[None, None, :, None]
        tq = np.arange(128)[None, None, None, :]
        g = 2 * jj + hh
        msk = ((r < g) | ((r == g) & (s < tq))).astype(np.float32)
        m["dmask"] = np.ascontiguousarray(msk.reshape(128, 8 * 512))
    return m


_CACHE = {}


def kernel(**inputs):
    S = 8192
    if "nc" not in _CACHE:
        _CACHE["nc"] = build(S=S, layers=(0, 1))[0]
    nc = _CACHE["nc"]
    in_maps = [host_inputs(inputs, c // 2, c % 2, S) for c in range(8)]
    res = run_bass_kernel_spmd(nc, in_maps, core_ids=list(range(8)))
    out = np.empty((4, S, D), np.float32)
    for c in range(8):
        b, hh = c // 2, c % 2
        o = np.asarray(res.results[c]["out"]).reshape(S // 256, 128, D)
        out[b].reshape(S // 256, 2, 128, D)[:, hh] = o
    return out
```

```python
import math
from contextlib import ExitStack
import numpy as np
import concourse.bass as bass
import concourse.mybir as mybir
from concourse.bass_utils import run_bass_kernel_spmd

F32 = mybir.dt.float32
BF16 = mybir.dt.bfloat16
I32 = mybir.dt.int32
AF = mybir.ActivationFunctionType
ALU = mybir.AluOpType
AX = mybir.AxisListType

D = 1024
EPS = 1e-6
PI = math.pi


class Res:
    __slots__ = ("name", "lw", "rd")

    def __init__(self, name=""):
        self.name = name
        self.lw = None
        self.rd = {}


class SemObj:
    __slots__ = ("sem", "count", "name")

    def __init__(self, sem, name):
        self.sem = sem
        self.count = 0
        self.name = name


class KB:
    def __init__(self, nc, stack):
        self.nc = nc
        self.stack = stack
        self.engs = {"pe": nc.tensor, "act": nc.scalar, "dve": nc.vector, "pool": nc.gpsimd, "sp": nc.sync}
        self.so = {}
        for k in self.engs:
            s = stack.enter_context(nc.semaphore("prog_" + k))
            self.so[k] = SemObj(s, k)
        self.waited = {k: {} for k in self.engs}
        self.n_inst = 0
        self.all_so = list(self.so.values())
        self.free_dma = {}

    def new_dma_sem(self, name, q="sp"):
        if self.free_dma.setdefault(q, []):
            return self.free_dma[q].pop()
        s = self.stack.enter_context(self.nc.semaphore("dma_" + name))
        so = SemObj(s, name)
        self.all_so.append(so)
        return so

    def barrier(self):
        for e in self.engs:
            self._wait(e, [(so, so.count) for so in self.all_so if so is not self.so[e]])

    def _wait(self, e, deps):
        w = self.waited[e]
        for so, val in deps:
            if val <= 0 or w.get(so, 0) >= val:
                continue
            self.engs[e].wait_ge(so.sem, val)
            w[so] = val

    def _deps(self, e, reads, writes, same_engine):
        me = self.so[e]
        deps = []
        for r in reads:
            if r.lw is not None and (same_engine or r.lw[0] is not me):
                deps.append(r.lw)
        for r in writes:
            if r.lw is not None and (same_engine or r.lw[0] is not me):
                deps.append(r.lw)
            for so, v in r.rd.items():
                if same_engine or so is not me:
                    deps.append((so, v))
        return deps

    def op(self, e, fn, reads=(), writes=(), signal=True, same_engine=True):
        me = self.so[e]
        self._wait(e, self._deps(e, reads, writes, same_engine))
        ins = fn()
        self.n_inst += 1
        if signal:
            me.count += 1
            ins.then_inc(me.sem, 1)
            ev = (me, me.count)
        else:
            ev = (me, me.count + 1)
        for r in reads:
            if r.rd.get(me, 0) < ev[1]:
                r.rd[me] = ev[1]
        for r in writes:
            r.lw = ev
            r.rd = {}
        return ins

    def dma(self, q, out_ap, in_ap, dsem, reads=(), writes=(), **kw):
        if isinstance(dsem, Buf):
            dsem = dsem.get_ds(q)
        self._wait(q, self._deps(q, reads, writes, True))
        ins = self.engs[q].dma_start(out=out_ap, in_=in_ap, **kw)
        self.n_inst += 1
        dsem.count += 16
        ins.then_inc(dsem.sem, 16)
        ev = (dsem, dsem.count)
        for r in reads:
            if r.rd.get(dsem, 0) < ev[1]:
                r.rd[dsem] = ev[1]
        for r in writes:
            r.lw = ev
            r.rd = {}
        return ins


class Buf:
    def __init__(self, t, name, k):
        self.t = t
        self.r = Res(name)
        self.name = name
        self._k = k
        self._ds = {}

    @property
    def ds(self):
        return self

    def get_ds(self, q):
        if q not in self._ds:
            self._ds[q] = self._k.new_dma_sem(self.name + q, q)
        return self._ds[q]


def build(S=8192, layers=(0, 1), dbg=False):
    NB = S // 128
    NT = S // 512
    SO = S // 2
    NTO = SO // 512
    nc = bass.Bass("TRN2", target_bir_lowering=False)
    dram_in = lambda n, sh, dt=F32: nc.dram_tensor(n, sh, dt, kind="ExternalInput").ap()
    dram_out = lambda n, sh, dt=F32: nc.dram_tensor(n, sh, dt, kind="ExternalOutput").ap()
    dram_scr = lambda n, sh, dt: nc.dram_tensor(n, sh, dt, kind=("ExternalOutput" if dbg else "Internal")).ap()

    L0 = 0 in layers
    L1 = 1 in layers
    cT = dram_in("cT", [128, 8])
    consts = dram_in("consts", [128, 128 * 4 + 32])
    if L0:
        x_in = dram_in("x", [S, D])
        posT = dram_in("posT", [128, NB], I32)
        e_norm_g = dram_in("e_norm_g", [1, D]); e_w_mod = dram_in("e_w_mod", [D, 3 * D]); e_b_mod = dram_in("e_b_mod", [1, 3 * D])
        e_w_in = dram_in("e_w_in", [D, 3328]); e_w_out = dram_in("e_w_out", [D, D])
        gains = dram_in("gains", [1, 4 * 64])
        lamv = dram_in("lamv", [1, 4 * 64])
        subg = dram_in("subg", [1, 128])
        sinks = dram_in("sinks", [1, 8])
    if L1:
        o_norm_g = dram_in("o_norm_g", [1, D]); o_w_mod = dram_in("o_w_mod", [D, 3 * D]); o_b_mod = dram_in("o_b_mod", [1, 3 * D])
        o_w_in = dram_in("o_w_in", [D, 4 * D]); o_w_out = dram_in("o_w_out", [D, D])
        ownm = dram_in("ownm", [128, 2])
        dmask = dram_in("dmask", [128, 8 * 512])
        out_o = dram_out("out", [SO, D])
    if L0 and L1:
        x1 = dram_scr("x1", [S, D], F32)
    elif L0:
        x1 = dram_out("x1", [S, D])
    else:
        x1 = dram_in("x1", [S, D])

    with ExitStack() as st:
        k = KB(nc, st)

        cur = [st]
        bufs_of = {id(st): []}

        uid = [0]

        def sb(name, shape, dt):
            uid[0] += 1
            name = f"{name}_{uid[0]}"
            bf_ = Buf(cur[0].enter_context(nc.sbuf_tensor(name, shape, dt)), name, k)
            bufs_of[id(cur[0])].append(bf_)
            return bf_

        def slots(name, shape, dt, n):
            return [sb(f"{name}{i}", shape, dt) for i in range(n)]

        class Phase:
            def __enter__(self_p):
                self_p.prev = cur[0]
                self_p.stk = ExitStack()
                cur[0] = self_p.stk
                bufs_of[id(self_p.stk)] = []
                return self_p

            def __exit__(self_p, *a):
                k.barrier()
                for bf_ in bufs_of.pop(id(self_p.stk)):
                    for q_, so_ in bf_._ds.items():
                        k.free_dma.setdefault(q_, []).append(so_)
                    bf_._ds = {}
                self_p.stk.close()
                cur[0] = self_p.prev
                return False

        PSF = st.enter_context(nc.psum_tensor("psf", [128, 7, 512], F32))
        PST = st.enter_context(nc.psum_tensor("pst", [128, 1024], BF16))
        bank = [Res(f"bank{i}") for i in range(7)]
        bankT = Res("bankT")

        cst32 = sb("cst32", [128, 128 * 4 + 32], F32)
        ident = sb("ident", [128, 128], BF16)
        tri = sb("tri", [128, 128], BF16)
        upp = sb("upp", [128, 128], BF16)
        negtri = sb("negtri", [128, 128], BF16)
        negrest = sb("negrest", [128, 128], BF16)
        k.dma("sp", cst32.t[:], consts[:, :], cst32.ds, writes=[cst32.r])
        for i, tdst in enumerate((ident, tri, upp, negtri)):
            k.op("dve", lambda tdst=tdst, i=i: nc.vector.tensor_copy(out=tdst.t[:], in_=cst32.t[:, i * 128:(i + 1) * 128]),
                 reads=[cst32.r], writes=[tdst.r])
        k.op("dve", lambda: nc.vector.tensor_scalar(out=negrest.t[:], in0=cst32.t[:, 384:512], scalar1=-1.0, scalar2=-1.0,
                                                   op0=ALU.mult, op1=ALU.add), reads=[cst32.r], writes=[negrest.r])
        invf = cst32.t[:, 512:544]

        sc = sb("sc", [128, 8], F32)
        sce = sb("sce", [128, 8], F32)
        k.dma("sp", sc.t[:], cT[:, :], sc.ds, writes=[sc.r])
        k.op("act", lambda: nc.scalar.activation(out=sce.t[:], in_=sc.t[:], func=AF.Exp, scale=-1.0), reads=[sc.r], writes=[sce.r])
        k.op("dve", lambda: nc.vector.tensor_scalar_add(out=sce.t[:], in0=sce.t[:], scalar1=1.0), reads=[sce.r], writes=[sce.r])
        k.op("dve", lambda: nc.vector.reciprocal(out=sce.t[:], in_=sce.t[:]), reads=[sce.r], writes=[sce.r])
        k.op("dve", lambda: nc.vector.tensor_tensor(out=sc.t[:], in0=sc.t[:], in1=sce.t[:], op=ALU.mult), reads=[sc.r, sce.r], writes=[sc.r])
        screp = sb("screp", [128, 8, 128], F32)
        k.op("dve", lambda: nc.vector.tensor_copy(out=screp.t[:], in_=sc.t[:].unsqueeze(2).to_broadcast([128, 8, 128])),
             reads=[sc.r], writes=[screp.r])

        def mod_vectors(tag, w_mod, b_mod, norm_g):
            shift = sb("shift" + tag, [128, D], F32)
            gs = sb("gs" + tag, [128, D], F32)
            gate = sb("gate" + tag, [128, D], F32)
            with Phase():
                _mod_vectors(w_mod, b_mod, norm_g, shift, gs, gate)
            return shift, gs, gate

        def _mod_vectors(w_mod, b_mod, norm_g, shift, gs, gate):
            wm = slots("wm", [128, 8, 512], F32, 2)
            bmod = sb("bmod", [128, 3 * D], F32)
            gbc = sb("gbc", [128, D], F32)
            k.dma("sp", bmod.t[:], b_mod.partition_broadcast(128), bmod.ds, writes=[bmod.r])
            k.dma("sp", gbc.t[:], norm_g.partition_broadcast(128), gbc.ds, writes=[gbc.r])
            wv = w_mod.rearrange("(kc p) n -> p kc n", p=128)
            for cc in range(6):
                w = wm[cc % 2]
                k.dma("sp", w.t[:], wv[:, :, cc * 512:(cc + 1) * 512], w.ds, writes=[w.r])
                b = cc % 2
                for kc in range(8):
                    k.op("pe", lambda kc=kc, b=b, w=w: nc.tensor.matmul(PSF[:, b, :], lhsT=screp.t[:, kc, :], rhs=w.t[:, kc, :],
                                                                        start=(kc == 0), stop=(kc == 7)),
                         reads=[screp.r, w.r], writes=[bank[b]], signal=(kc == 7), same_engine=False)
                seg, off = cc // 2, (cc % 2) * 512
                bsl = bmod.t[:, cc * 512:(cc + 1) * 512]
                if seg == 0:
                    k.op("dve", lambda b=b, off=off, bsl=bsl: nc.vector.tensor_tensor(out=shift.t[:, off:off + 512], in0=PSF[:, b, :], in1=bsl, op=ALU.add),
                         reads=[bank[b], bmod.r], writes=[shift.r])
                elif seg == 1:
                    k.op("dve", lambda b=b, off=off, bsl=bsl: nc.vector.scalar_tensor_tensor(out=gs.t[:, off:off + 512], in0=PSF[:, b, :], scalar=1.0, in1=bsl,
                                                                                            op0=ALU.add, op1=ALU.add),
                         reads=[bank[b], bmod.r], writes=[gs.r])
                    k.op("dve", lambda off=off: nc.vector.tensor_tensor(out=gs.t[:, off:off + 512], in0=gs.t[:, off:off + 512], in1=gbc.t[:, off:off + 512], op=ALU.mult),
                         reads=[gs.r, gbc.r], writes=[gs.r])
                else:
                    k.op("dve", lambda b=b, off=off, bsl=bsl: nc.vector.tensor_tensor(out=gate.t[:, off:off + 512], in0=PSF[:, b, :], in1=bsl, op=ALU.add),
                         reads=[bank[b], bmod.r], writes=[gate.r])
            return shift, gs, gate

        xt = slots("xt", [128, D], F32, 3)
        junk = sb("junk", [128, D], F32)
        ssq = sb("ssq", [128, 1], F32)
        rstd = sb("rstd", [128, 1], F32)
        htm = slots("htm", [128, D], BF16, 2)

        def norm_mod(xb, hb, gs, shift):
            k.op("act", lambda: nc.scalar.activation(out=junk.t[:], in_=xb.t[:], func=AF.Square, accum_out=ssq.t[:]),
                 reads=[xb.r], writes=[junk.r, ssq.r])
            k.op("act", lambda: nc.scalar.activation(out=rstd.t[:], in_=ssq.t[:], func=AF.Ln, scale=1.0 / D, bias=EPS), reads=[ssq.r], writes=[rstd.r])
            k.op("act", lambda: nc.scalar.activation(out=rstd.t[:], in_=rstd.t[:], func=AF.Exp, scale=-0.5), reads=[rstd.r], writes=[rstd.r])
            k.op("dve", lambda: nc.vector.scalar_tensor_tensor(out=junk.t[:], in0=xb.t[:], scalar=rstd.t[:, 0:1], in1=gs.t[:], op0=ALU.mult, op1=ALU.mult),
                 reads=[xb.r, rstd.r, gs.r, junk.r], writes=[junk.r])
            k.op("dve", lambda: nc.vector.tensor_tensor(out=hb.t[:], in0=junk.t[:], in1=shift.t[:], op=ALU.add),
                 reads=[junk.r, shift.r], writes=[hb.r])

        def transpose8(src_ap_fn, src_r, dst_ap, dst_r, eng="act"):
            for kc in range(8):
                k.op("pe", lambda kc=kc: nc.tensor.transpose(out=PST[:, kc * 128:(kc + 1) * 128], in_=src_ap_fn(kc), identity=ident.t[:]),
                     reads=[src_r, ident.r], writes=[bankT], signal=(kc == 7), same_engine=False)
            src3 = PST[:, :].rearrange("p (a b) -> p a b", a=8)
            if eng == "act":
                k.op("act", lambda: nc.scalar.copy(out=dst_ap, in_=src3), reads=[bankT], writes=[dst_r])
            else:
                k.op("dve", lambda: nc.vector.tensor_copy(out=dst_ap, in_=src3), reads=[bankT], writes=[dst_r])

        if L0:
            sh0, gs0, gate0 = mod_vectors("0", e_w_mod, e_b_mod, e_norm_g)
            QAT = dram_scr("QAT", [4, 128, S], BF16)
            KAT = dram_scr("KAT", [4, 128, S], BF16)
            VA = dram_scr("VA", [S, 512], BF16)
            SGA = dram_scr("SGA", [S, 512], BF16)
            QBT = dram_scr("QBT", [4, 128, S], BF16)
            KBT = dram_scr("KBT", [128, S], BF16)
            VB = dram_scr("VB", [S, 128], BF16)
            SGB = dram_scr("SGB", [S, 512], BF16)
            Y0 = dram_scr("Y0", [S, D], BF16)

            lvl = int(str(dbg)[3:]) if str(dbg).startswith("b0x") else 99
            with (Phase() if lvl == 99 else ExitStack()):
              if lvl == 99:
                  cosT = sb("cosT", [128, NB, 32], F32)
                  sinT = sb("sinT", [128, NB, 32], F32)
                  with Phase():
                      posi = sb("posi", [128, NB], I32)
                      posf = sb("posf", [128, NB], F32)
                      ang = sb("ang", [128, NB, 32], F32)
                      k.dma("sp", posi.t[:], posT[:, :], posi.ds, writes=[posi.r])
                      k.op("dve", lambda: nc.vector.tensor_copy(out=posf.t[:], in_=posi.t[:]), reads=[posi.r], writes=[posf.r])
                      k.op("dve", lambda: nc.vector.tensor_tensor(out=ang.t[:], in0=posf.t[:].unsqueeze(2).to_broadcast([128, NB, 32]),
                                                                 in1=invf.unsqueeze(1).to_broadcast([128, NB, 32]), op=ALU.mult),
                           reads=[posf.r, cst32.r], writes=[ang.r])
                      negpi = sb("negpi", [128, 1], F32)
                      k.op("dve", lambda: nc.vector.memset(negpi.t[:], -PI), writes=[negpi.r])
                      ui = sb("ui", [128, NB, 32], I32)
                      uf = sb("uf", [128, NB, 32], F32)
                      for tbl, c0 in ((sinT, 0.5), (cosT, 0.75)):
                          k.op("dve", lambda tbl=tbl, c0=c0: nc.vector.tensor_scalar(out=tbl.t[:], in0=ang.t[:], scalar1=1.0 / (2 * PI), scalar2=c0, op0=ALU.mult, op1=ALU.add),
                               reads=[ang.r], writes=[tbl.r])
                          k.op("dve", lambda tbl=tbl: nc.vector.tensor_copy(out=ui.t[:], in_=tbl.t[:]), reads=[tbl.r, ui.r], writes=[ui.r])
                          k.op("dve", lambda: nc.vector.tensor_copy(out=uf.t[:], in_=ui.t[:]), reads=[ui.r, uf.r], writes=[uf.r])
                          k.op("dve", lambda tbl=tbl: nc.vector.tensor_tensor(out=tbl.t[:], in0=tbl.t[:], in1=uf.t[:], op=ALU.subtract), reads=[tbl.r, uf.r], writes=[tbl.r])
                          k.op("dve", lambda tbl=tbl: nc.vector.tensor_single_scalar(out=uf.t[:], in_=tbl.t[:], scalar=0.0, op=ALU.is_lt), reads=[tbl.r, uf.r], writes=[uf.r])
                          k.op("dve", lambda tbl=tbl: nc.vector.tensor_tensor(out=tbl.t[:], in0=tbl.t[:], in1=uf.t[:], op=ALU.add), reads=[tbl.r, uf.r], writes=[tbl.r])
                          k.op("act", lambda tbl=tbl: nc.scalar.activation(out=tbl.t[:], in_=tbl.t[:], func=AF.Sin, bias=negpi.t[:, 0:1], scale=2 * PI),
                               reads=[tbl.r, negpi.r], writes=[tbl.r])
                  gn = sb("gn", [128, 4 * 64], F32)
                  k.dma("sp", gn.t[:], gains.partition_broadcast(128), gn.ds, writes=[gn.r])
                  w0 = sb("w0", [128, 8, 3328], BF16)
                  w0v = e_w_in.rearrange("(kc p) n -> p kc n", p=128)
                  for kc in range(8):
                      k.dma("pool", w0.t[:, kc, :], w0v[:, kc, :], w0.ds, writes=[w0.r])

                  hT = slots("hT", [128, 8, 128], BF16, 2)
                  st_qa = slots("st_qa", [128, 4, 512], BF16, 2)
                  st_ka = slots("st_ka", [128, 4, 512], BF16, 2)
                  st_qb = slots("st_qb", [128, 4, 512], BF16, 2)
                  st_kb = slots("st_kb", [128, 512], BF16, 2)
                  st_va = slots("st_va", [128, 4, 512], BF16, 2)
                  st_vb = slots("st_vb", [128, 4, 128], BF16, 2)
                  st_ga = slots("st_ga", [128, 4, 512], BF16, 2)
                  st_gb = slots("st_gb", [128, 4, 512], BF16, 2)
                  scr = []
                  for i_ in range(2):
                      scr.append(dict(sq=sb("sq", [128, 512], F32), ss8=sb("ss8", [128, 8], F32), rs8=sb("rs8", [128, 8], F32), tq=sb("tq", [128, 512], F32),
                                      ra=sb("ra", [128, 8, 32], F32), rb=sb("rb", [128, 8, 32], F32), rc=sb("rc", [128, 8, 32], F32), rd=sb("rd", [128, 8, 32], F32),
                                      ge=sb("ge", [128, 512], F32), qn=sb("qn", [128, 512], BF16)))

                  def interleave(*gens):
                      gens = list(gens)
                      while gens:
                          for g_ in list(gens):
                              try:
                                  next(g_)
                              except StopIteration:
                                  gens.remove(g_)

                  def normrope(si, b, nh, gain_ap, blk, stg_fn):
                      W = nh * 64
                      sc_ = scr[si]
                      sq, ss8, rs8, tq, ra, rb, rc, rd, q = (sc_[n_] for n_ in ("sq", "ss8", "rs8", "tq", "ra", "rb", "rc", "rd", "qn"))
                      k.op("act", lambda: nc.scalar.activation(out=sq.t[:, 0:W], in_=PSF[:, b, 0:W], func=AF.Square),
                           reads=[bank[b]], writes=[sq.r])
                      yield
                      k.op("dve", lambda: nc.vector.tensor_reduce(out=ss8.t[:, 0:nh], in_=sq.t[:, 0:W].rearrange("p (h d) -> p h d", d=64), axis=AX.X, op=ALU.add),
                           reads=[sq.r], writes=[ss8.r])
                      yield
                      k.op("act", lambda: nc.scalar.activation(out=rs8.t[:, 0:nh], in_=ss8.t[:, 0:nh], func=AF.Ln, scale=1.0 / 64, bias=EPS), reads=[ss8.r], writes=[rs8.r])
                      k.op("act", lambda: nc.scalar.activation(out=rs8.t[:, 0:nh], in_=rs8.t[:, 0:nh], func=AF.Exp, scale=-0.5), reads=[rs8.r], writes=[rs8.r])
                      yield
                      t3 = tq.t[:, 0:W].rearrange("p (h d) -> p h d", d=64)
                      k.op("dve", lambda: nc.vector.tensor_tensor(out=t3, in0=PSF[:, b, 0:W].rearrange("p (h d) -> p h d", d=64),
                                                                 in1=rs8.t[:, 0:nh].unsqueeze(2).to_broadcast([128, nh, 64]), op=ALU.mult),
                           reads=[bank[b], rs8.r], writes=[tq.r])
                      yield
                      k.op("dve", lambda: nc.vector.tensor_tensor(out=t3, in0=t3, in1=gain_ap.unsqueeze(1).to_broadcast([128, nh, 64]), op=ALU.mult),
                           reads=[tq.r, gn.r], writes=[tq.r])
                      yield
                      cb = cosT.t[:, blk, :].unsqueeze(1).to_broadcast([128, nh, 32])
                      sbn = sinT.t[:, blk, :].unsqueeze(1).to_broadcast([128, nh, 32])
                      x1v, x2v = t3[:, :, 0:32], t3[:, :, 32:64]
                      d3 = q.t[:, 0:W].rearrange("p (h d) -> p h d", d=64)
                      k.op("dve", lambda: nc.vector.tensor_tensor(out=ra.t[:, 0:nh, :], in0=x1v, in1=cb, op=ALU.mult), reads=[tq.r, cosT.r], writes=[ra.r])
                      yield
                      k.op("dve", lambda: nc.vector.tensor_tensor(out=rb.t[:, 0:nh, :], in0=x2v, in1=sbn, op=ALU.mult), reads=[tq.r, sinT.r], writes=[rb.r])
                      yield
                      k.op("dve", lambda: nc.vector.tensor_tensor(out=rc.t[:, 0:nh, :], in0=x2v, in1=cb, op=ALU.mult), reads=[tq.r, cosT.r], writes=[rc.r])
                      yield
                      k.op("dve", lambda: nc.vector.tensor_tensor(out=rd.t[:, 0:nh, :], in0=x1v, in1=sbn, op=ALU.mult), reads=[tq.r, sinT.r], writes=[rd.r])
                      yield
                      k.op("dve", lambda: nc.vector.tensor_tensor(out=d3[:, :, 0:32], in0=ra.t[:, 0:nh, :], in1=rb.t[:, 0:nh, :], op=ALU.subtract),
                           reads=[ra.r, rb.r], writes=[q.r])
                      yield
                      k.op("dve", lambda: nc.vector.tensor_tensor(out=d3[:, :, 32:64], in0=rc.t[:, 0:nh, :], in1=rd.t[:, 0:nh, :], op=ALU.add),
                           reads=[rc.r, rd.r], writes=[q.r])
                      yield
                      ncc = W // 128
                      for cc in range(ncc):
                          k.op("pe", lambda cc=cc: nc.tensor.transpose(out=PST[:, cc * 128:(cc + 1) * 128], in_=q.t[:, cc * 128:(cc + 1) * 128], identity=ident.t[:]),
                               reads=[q.r, ident.r], writes=[bankT], signal=(cc == ncc - 1), same_engine=False)
                      stg_fn()
                      yield

                  def silu_to(si, b, dst_ap, dst_r):
                      ge = scr[si]["ge"]
                      k.op("act", lambda: nc.scalar.activation(out=ge.t[:], in_=PSF[:, b, :], func=AF.Exp, scale=-1.0), reads=[bank[b]], writes=[ge.r])
                      yield
                      k.op("dve", lambda: nc.vector.tensor_scalar_add(out=ge.t[:], in0=ge.t[:], scalar1=1.0), reads=[ge.r], writes=[ge.r])
                      yield
                      k.op("dve", lambda: nc.vector.reciprocal(out=ge.t[:], in_=ge.t[:]), reads=[ge.r], writes=[ge.r])
                      yield
                      k.op("dve", lambda: nc.vector.tensor_tensor(out=dst_ap, in0=ge.t[:], in1=PSF[:, b, :], op=ALU.mult), reads=[ge.r, bank[b]], writes=[dst_r])
                      yield

                  xv = x_in.rearrange("(n p) d -> n p d", p=128)
                  k.dma("sp", xt[0].t[:], xv[0], xt[0].ds, writes=[xt[0].r])
                  bsel = 0
                  for T in range(NT):
                      s_ = T % 2
                      for j in range(4):
                          blk = T * 4 + j
                          if blk + 1 < NB:
                              nx = xt[(blk + 1) % 3]
                              k.dma("sp", nx.t[:], xv[blk + 1], nx.ds, writes=[nx.r])
                          xb = xt[blk % 3]
                          hb = htm[blk % 2]
                          norm_mod(xb, hb, gs0, sh0)
                          hTb = hT[blk % 2]
                          transpose8(lambda kc, hb=hb: hb.t[:, kc * 128:(kc + 1) * 128], hb.r, hTb.t[:], hTb.r, eng="act")
                          for ch in range(7):
                              b = ch
                              c0 = ch * 512
                              W = 512 if ch < 6 else 256
                              for kc in range(8):
                                  k.op("pe", lambda kc=kc, b=b, c0=c0, W=W, hTb=hTb: nc.tensor.matmul(PSF[:, b, 0:W], lhsT=hTb.t[:, kc, :], rhs=w0.t[:, kc, c0:c0 + W],
                                                                                                      start=(kc == 0), stop=(kc == 7)),
                                       reads=[hTb.r, w0.r], writes=[bank[b]], signal=(kc == 7), same_engine=False)

                          def stg4(stg, j=j):
                              return lambda: k.op("act", lambda: nc.scalar.copy(out=stg.t[:, :, j * 128:(j + 1) * 128], in_=PST[:, 0:512].rearrange("p (c t) -> p c t", c=4)),
                                                  reads=[bankT], writes=[stg.r])

                          def stg1(stg, j=j):
                              return lambda: k.op("act", lambda: nc.scalar.copy(out=stg.t[:, j * 128:(j + 1) * 128], in_=PST[:, 0:128]), reads=[bankT], writes=[stg.r])

                          interleave(normrope(0, 0, 8, gn.t[:, 0:64], blk, stg4(st_qa[s_])), normrope(1, 1, 8, gn.t[:, 64:128], blk, stg4(st_ka[s_])))
                          k.op("act", lambda j=j: nc.scalar.copy(out=st_va[s_].t[:, j, :], in_=PSF[:, 2, :]), reads=[bank[2]], writes=[st_va[s_].r])
                          k.op("act", lambda j=j: nc.scalar.copy(out=st_vb[s_].t[:, j, :], in_=PSF[:, 6, 128:256]), reads=[bank[6]], writes=[st_vb[s_].r])
                          interleave(silu_to(0, 3, st_ga[s_].t[:, j, :], st_ga[s_].r), silu_to(1, 5, st_gb[s_].t[:, j, :], st_gb[s_].r))
                          interleave(normrope(0, 4, 8, gn.t[:, 128:192], blk, stg4(st_qb[s_])), normrope(1, 6, 2, gn.t[:, 192:256], blk, stg1(st_kb[s_])))
                      t0 = T * 512
                      for hh_ in range(4):
                          k.dma("pool", QAT[hh_, :, t0:t0 + 512], st_qa[s_].t[:, hh_, :], st_qa[s_].ds, reads=[st_qa[s_].r])
                          k.dma("pool", KAT[hh_, :, t0:t0 + 512], st_ka[s_].t[:, hh_, :], st_ka[s_].ds, reads=[st_ka[s_].r])
                          k.dma("pool", QBT[hh_, :, t0:t0 + 512], st_qb[s_].t[:, hh_, :], st_qb[s_].ds, reads=[st_qb[s_].r])
                      k.dma("pool", KBT[:, t0:t0 + 512], st_kb[s_].t[:], st_kb[s_].ds, reads=[st_kb[s_].r])
                      k.dma("pool", VA[t0:t0 + 512, :].rearrange("(j p) c -> p j c", p=128), st_va[s_].t[:], st_va[s_].ds, reads=[st_va[s_].r])
                      k.dma("pool", VB[t0:t0 + 512, :].rearrange("(j p) c -> p j c", p=128), st_vb[s_].t[:], st_vb[s_].ds, reads=[st_vb[s_].r])
                      k.dma("pool", SGA[t0:t0 + 512, :].rearrange("(j p) c -> p j c", p=128), st_ga[s_].t[:], st_ga[s_].ds, reads=[st_ga[s_].r])
                      k.dma("pool", SGB[t0:t0 + 512, :].rearrange("(j p) c -> p j c", p=128), st_gb[s_].t[:], st_gb[s_].ds, reads=[st_gb[s_].r])

        if dbg == "p0":
            k.barrier()
            return nc, k

        if L0:
            lvl = int(str(dbg)[3:]) if str(dbg).startswith("b0x") else 99
            with (Phase() if lvl == 99 else ExitStack()):
              if lvl == 99:
                  lv = sb("lv", [128, 256], F32)
                  lp = sb("lp", [128, 128], F32)
                  l2 = sb("l2", [128, 2], F32)
                  nlam = sb("nlam", [128, 1], F32)
                  k.dma("sp", lv.t[:], lamv.partition_broadcast(128), lv.ds, writes=[lv.r])
                  lv4 = lv.t[:].rearrange("p (a d) -> p a d", d=64)
                  k.op("dve", lambda: nc.vector.tensor_tensor(out=lp.t[:, 0:64], in0=lv4[:, 0, :], in1=lv4[:, 1, :], op=ALU.mult), reads=[lv.r], writes=[lp.r])
                  k.op("dve", lambda: nc.vector.tensor_tensor(out=lp.t[:, 64:128], in0=lv4[:, 2, :], in1=lv4[:, 3, :], op=ALU.mult), reads=[lv.r, lp.r], writes=[lp.r])
                  k.op("dve", lambda: nc.vector.tensor_reduce(out=l2.t[:], in_=lp.t[:].rearrange("p (a d) -> p a d", d=64), axis=AX.X, op=ALU.add), reads=[lp.r], writes=[l2.r])
                  k.op("act", lambda: nc.scalar.activation(out=l2.t[:], in_=l2.t[:], func=AF.Exp), reads=[l2.r], writes=[l2.r])
                  k.op("dve", lambda: nc.vector.tensor_tensor(out=nlam.t[:], in0=l2.t[:, 1:2], in1=l2.t[:, 0:1], op=ALU.subtract), reads=[l2.r], writes=[nlam.r])
                  k.op("dve", lambda: nc.vector.tensor_scalar_add(out=nlam.t[:], in0=nlam.t[:], scalar1=-0.2), reads=[nlam.r], writes=[nlam.r])
                  gsub = sb("gsub", [128, 128], F32)
                  k.dma("sp", gsub.t[:], subg.partition_broadcast(128), gsub.ds, writes=[gsub.r])
                  k.op("dve", lambda: nc.vector.tensor_scalar_mul(out=gsub.t[:], in0=gsub.t[:], scalar1=0.8), reads=[gsub.r], writes=[gsub.r])
                  KT = sb("KT", [128, S], BF16)
                  QT = sb("QT", [128, S], BF16)
                  V1 = sb("V1", [128, NB, 132], BF16)
                  SGh = sb("SGh", [128, NB, 128], BF16)
                  et = slots("et", [128, 2, 512], BF16, 2)
                  ya = sb("ya", [128, 128], F32)
                  yn = sb("yn", [128, 128], F32)
                  z2 = sb("z2", [128, 2], F32)
                  yst = slots("yst", [128, 4, 128], BF16, 2)
                  k.op("dve", lambda: nc.vector.memset(V1.t[:, :, 128:129], 1.0), writes=[V1.r])
                  accR = [bank[4 + a_ // 3] for a_ in range(8)]

                  def acc_ap(c_, j_, lo, hi):
                      a_ = c_ * 4 + j_
                      return PSF[:, 4 + a_ // 3, (a_ % 3) * 132 + lo:(a_ % 3) * 132 + hi]

                  it = 0
                  for h in range(4):
                      k.dma("sp", KT.t[:], KAT[h], KT.ds, writes=[KT.r])
                      k.dma("sp", QT.t[:], QAT[h], QT.ds, writes=[QT.r])
                      k.dma("sp", V1.t[:, :, 0:128], VA[:, h * 128:(h + 1) * 128].rearrange("(n p) c -> p n c", p=128), V1.ds, writes=[V1.r])
                      k.dma("sp", SGh.t[:], SGA[:, h * 128:(h + 1) * 128].rearrange("(n p) c -> p n c", p=128), SGh.ds, writes=[SGh.r])
                      steps = [(qt, kb) for qt in range(NT) for kb in range(4 * qt + 4)]

                      def stage_s(i):
                          qt, kb = steps[i]
                          pr = i % 2
                          e_ = et[pr]
                          r_ = kb - 4 * qt
                          c0 = r_ * 128 if r_ > 0 else 0
                          for c_ in range(2):
                              k.op("pe", lambda c_=c_, pr=pr, kb=kb, qt=qt, c0=c0: nc.tensor.matmul(PSF[:, 2 * pr + c_, c0:512], lhsT=KT.t[c_ * 64:(c_ + 1) * 64, kb * 128:(kb + 1) * 128],
                                                                                         rhs=QT.t[c_ * 64:(c_ + 1) * 64, qt * 512 + c0:(qt + 1) * 512], start=True, stop=True),
                                   reads=[KT.r, QT.r], writes=[bank[2 * pr + c_]], signal=(c_ == 1), same_engine=False)
                          k.op("act", lambda pr=pr, e_=e_, c0=c0: nc.scalar.activation(out=e_.t[:, :, c0:512], in_=PSF[:, 2 * pr:2 * pr + 2, c0:512], func=AF.Exp, scale=0.125),
                               reads=[bank[2 * pr], bank[2 * pr + 1]], writes=[e_.r])
                          if r_ >= 0:
                              k.op("dve", lambda e_=e_, r_=r_: nc.vector.tensor_tensor(out=e_.t[:, :, r_ * 128:(r_ + 1) * 128], in0=e_.t[:, :, r_ * 128:(r_ + 1) * 128],
                                                                                     in1=tri.t[:].unsqueeze(1).to_broadcast([128, 2, 128]), op=ALU.mult),
                                   reads=[e_.r, tri.r], writes=[e_.r])

                      def stage_pv(i):
                          qt, kb = steps[i]
                          e_ = et[i % 2]
                          ys = yst[qt % 2]
                          r_ = kb - 4 * qt
                          if kb == 0:
                              started.clear()
                          for j_ in range(4):
                              if r_ > j_:
                                  continue
                              last = (kb == 4 * qt + j_)
                              for c_ in range(2):
                                  bk_ = 4 + (c_ * 4 + j_) // 3
                                  st_ = (kb == 0 and bk_ not in started)
                                  started.add(bk_)
                                  k.op("pe", lambda c_=c_, j_=j_, e_=e_, kb=kb, last=last, st_=st_: nc.tensor.matmul(acc_ap(c_, j_, 0, 129), lhsT=e_.t[:, c_, j_ * 128:(j_ + 1) * 128],
                                                                                                       rhs=V1.t[:, kb, 0:129], start=st_, stop=last, skip_group_check=True),
                                       reads=[e_.r, V1.r], writes=[accR[c_ * 4 + j_]], signal=(last or (j_ == 3 and c_ == 1)), same_engine=False)
                              if last:
                                  blk = 4 * qt + j_
                                  a0, a1 = accR[j_], accR[4 + j_]
                                  k.op("dve", lambda j_=j_: nc.vector.tensor_copy(out=z2.t[:, 0:1], in_=acc_ap(0, j_, 128, 129)), reads=[a0], writes=[z2.r])
                                  k.op("dve", lambda j_=j_: nc.vector.tensor_copy(out=z2.t[:, 1:2], in_=acc_ap(1, j_, 128, 129)), reads=[a1, z2.r], writes=[z2.r])
                                  k.op("dve", lambda: nc.vector.reciprocal(out=z2.t[:], in_=z2.t[:]), reads=[z2.r], writes=[z2.r])
                                  k.op("dve", lambda: nc.vector.tensor_tensor(out=z2.t[:, 1:2], in0=z2.t[:, 1:2], in1=nlam.t[:], op=ALU.mult), reads=[z2.r, nlam.r], writes=[z2.r])
                                  k.op("dve", lambda j_=j_: nc.vector.tensor_scalar(out=ya.t[:], in0=acc_ap(0, j_, 0, 128), scalar1=z2.t[:, 0:1], scalar2=None, op0=ALU.mult),
                                       reads=[a0, z2.r], writes=[ya.r])
                                  k.op("dve", lambda j_=j_: nc.vector.scalar_tensor_tensor(out=ya.t[:], in0=acc_ap(1, j_, 0, 128), scalar=z2.t[:, 1:2], in1=ya.t[:], op0=ALU.mult, op1=ALU.add),
                                       reads=[a1, z2.r, ya.r], writes=[ya.r])
                                  k.op("act", lambda: nc.scalar.activation(out=yn.t[:], in_=ya.t[:], func=AF.Square, accum_out=ssq.t[:]), reads=[ya.r], writes=[yn.r, ssq.r])
                                  k.op("act", lambda: nc.scalar.activation(out=rstd.t[:], in_=ssq.t[:], func=AF.Ln, scale=1.0 / 128, bias=EPS), reads=[ssq.r], writes=[rstd.r])
                                  k.op("act", lambda: nc.scalar.activation(out=rstd.t[:], in_=rstd.t[:], func=AF.Exp, scale=-0.5), reads=[rstd.r], writes=[rstd.r])
                                  k.op("dve", lambda: nc.vector.scalar_tensor_tensor(out=yn.t[:], in0=ya.t[:], scalar=rstd.t[:, 0:1], in1=gsub.t[:], op0=ALU.mult, op1=ALU.mult),
                                       reads=[ya.r, rstd.r, gsub.r, yn.r], writes=[yn.r])
                                  k.op("dve", lambda j_=j_, blk=blk, ys=ys: nc.vector.tensor_tensor(out=ys.t[:, j_, :], in0=yn.t[:], in1=SGh.t[:, blk, :], op=ALU.mult),
                                       reads=[yn.r, SGh.r], writes=[ys.r])
                          if kb == 4 * qt + 3:
                              k.dma("pool", Y0[qt * 512:(qt + 1) * 512, h * 128:(h + 1) * 128].rearrange("(j p) c -> p j c", p=128), ys.t[:], ys.ds, reads=[ys.r])

                      started = set()
                      stage_s(0)
                      for i in range(len(steps)):
                          if i + 1 < len(steps):
                              stage_s(i + 1)
                          stage_pv(i)

            if dbg == "a0":
                k.barrier()
                return nc, k
            with Phase():
                esk = sb("esk", [128, 8], F32)
                k.dma("sp", esk.t[:], sinks.partition_broadcast(128), esk.ds, writes=[esk.r])
                k.op("act", lambda: nc.scalar.activation(out=esk.t[:], in_=esk.t[:], func=AF.Exp), reads=[esk.r], writes=[esk.r])
                qbt = slots("qbt", [128, 4, 512], BF16, 2)
                kbt = slots("kbt", [128, 640], BF16, 2)
                vb1 = slots("vb1", [128, 5, 2, 72], BF16, 2)
                sgb = slots("sgb", [128, 4, 512], BF16, 2)
                eb = slots("eb", [128, 16, 128], BF16, 2)
                zz = sb("zz", [128, 8], F32)
                ybt = sb("ybt", [128, 8, 64], F32)
                ysb = slots("ysb", [128, 4, 512], BF16, 2)
                for v_ in vb1:
                    k.op("dve", lambda v_=v_: nc.vector.memset(v_.t[:, :, :, 64:65], 1.0), writes=[v_.r])

                def b0_load(T):
                    s_ = T % 2
                    t0 = T * 512
                    for p_ in range(4):
                        k.dma("sp", qbt[s_].t[:, p_, :], QBT[p_, :, t0:t0 + 512], qbt[s_].ds, writes=[qbt[s_].r])
                    if T > 0:
                        k.dma("sp", kbt[s_].t[:, :], KBT[:, t0 - 128:t0 + 512], kbt[s_].ds, writes=[kbt[s_].r])
                        for g_ in range(2):
                            k.dma("sp", vb1[s_].t[:, :, g_, 0:64], VB[t0 - 128:t0 + 512, g_ * 64:(g_ + 1) * 64].rearrange("(n p) d -> p n d", p=128), vb1[s_].ds, writes=[vb1[s_].r])
                    else:
                        k.dma("sp", kbt[s_].t[:, 128:640], KBT[:, 0:512], kbt[s_].ds, writes=[kbt[s_].r])
                        for g_ in range(2):
                            k.dma("sp", vb1[s_].t[:, 1:5, g_, 0:64], VB[0:512, g_ * 64:(g_ + 1) * 64].rearrange("(n p) d -> p n d", p=128), vb1[s_].ds, writes=[vb1[s_].r])
                    k.dma("sp", sgb[s_].t[:], SGB[t0:t0 + 512, :].rearrange("(j p) c -> p j c", p=128), sgb[s_].ds, writes=[sgb[s_].r])

                PSflat = PSF[:, 0:4, :].rearrange("p a b -> p (a b)")
                b0_load(0)
                for T in range(NT):
                    s_ = T % 2
                    if T + 1 < NT:
                        b0_load(T + 1)
                    for j_ in range(4):
                        if lvl < 2:
                            break
                        blk = 4 * T + j_
                        e_ = eb[blk % 2]
                        kks = (0, 1) if blk > 0 else (1,)
                        for kk in kks:
                            for p_ in range(4):
                                for hf in range(2):
                                    off = (kk * 8 + hf * 4 + p_) * 128
                                    lastmm = (kk == 1 and p_ == 3 and hf == 1)
                                    k.op("pe", lambda off=off, hf=hf, kk=kk, p_=p_, j_=j_, s_=s_: nc.tensor.matmul(
                                        PSflat[:, off:off + 128], lhsT=kbt[s_].t[hf * 64:(hf + 1) * 64, (j_ + kk) * 128:(j_ + kk + 1) * 128],
                                        rhs=qbt[s_].t[hf * 64:(hf + 1) * 64, p_, j_ * 128:(j_ + 1) * 128], start=True, stop=True),
                                         reads=[kbt[s_].r, qbt[s_].r], writes=[bank[0], bank[1], bank[2], bank[3]], signal=lastmm, same_engine=False)
                        for kk in kks:
                            lo = kk * 8
                            k.op("act", lambda lo=lo, e_=e_: nc.scalar.activation(out=e_.t[:, lo:lo + 8, :], in_=PSflat[:, lo * 128:(lo + 8) * 128].rearrange("p (a t) -> p a t", t=128), func=AF.Exp, scale=0.125),
                                 reads=[bank[0], bank[1], bank[2], bank[3]], writes=[e_.r])
                        if lvl < 3:
                            continue
                        if blk > 0:
                            k.op("dve", lambda e_=e_: nc.vector.tensor_tensor(out=e_.t[:, 0:8, :], in0=e_.t[:, 0:8, :], in1=upp.t[:].unsqueeze(1).to_broadcast([128, 8, 128]), op=ALU.mult),
                                 reads=[e_.r, upp.r], writes=[e_.r])
                        k.op("dve", lambda e_=e_: nc.vector.tensor_tensor(out=e_.t[:, 8:16, :], in0=e_.t[:, 8:16, :], in1=tri.t[:].unsqueeze(1).to_broadcast([128, 8, 128]), op=ALU.mult),
                             reads=[e_.r, tri.r], writes=[e_.r])
                        if lvl < 4:
                            continue
                        for p_ in range(4):
                            for hf in range(2):
                                hd = hf * 4 + p_
                                for kk in kks:
                                    k.op("pe", lambda hd=hd, hf=hf, kk=kk, p_=p_, j_=j_, s_=s_, e_=e_, kks=kks: nc.tensor.matmul(
                                        PSF[:, 4 + hd // 4, (hd % 4) * 80:(hd % 4) * 80 + 65], lhsT=e_.t[:, kk * 8 + hf * 4 + p_, :],
                                        rhs=vb1[s_].t[:, j_ + kk, hf, 0:65], start=(kk == kks[0]), stop=(kk == 1)),
                                         reads=[e_.r, vb1[s_].r], writes=[bank[4], bank[5]], signal=(kk == 1 and p_ == 3 and hf == 1), same_engine=False)
                        if lvl < 5:
                            continue
                        for bb in range(2):
                            k.op("dve", lambda bb=bb: nc.vector.tensor_copy(out=zz.t[:, bb * 4:(bb + 1) * 4], in_=PSF[:, 4 + bb, 0:320].rearrange("p (h d) -> p h d", d=80)[:, :, 64]),
                                 reads=[bank[4], bank[5], zz.r], writes=[zz.r])
                        k.op("dve", lambda: nc.vector.tensor_tensor(out=zz.t[:], in0=zz.t[:], in1=esk.t[:], op=ALU.add), reads=[zz.r, esk.r], writes=[zz.r])
                        k.op("dve", lambda: nc.vector.reciprocal(out=zz.t[:], in_=zz.t[:]), reads=[zz.r], writes=[zz.r])
                        for bb in range(2):
                            k.op("dve", lambda bb=bb: nc.vector.tensor_tensor(out=ybt.t[:, bb * 4:(bb + 1) * 4, :], in0=PSF[:, 4 + bb, 0:320].rearrange("p (h d) -> p h d", d=80)[:, :, 0:64],
                                                                            in1=zz.t[:, bb * 4:(bb + 1) * 4].unsqueeze(2).to_broadcast([128, 4, 64]), op=ALU.mult),
                                 reads=[bank[4], bank[5], zz.r, ybt.r], writes=[ybt.r])
                        k.op("dve", lambda j_=j_, s_=s_: nc.vector.tensor_tensor(out=ysb[s_].t[:, j_, :], in0=ybt.t[:].rearrange("p h d -> p (h d)"), in1=sgb[s_].t[:, j_, :], op=ALU.mult),
                             reads=[ybt.r, sgb[s_].r], writes=[ysb[s_].r])
                    if lvl >= 6:
                        k.dma("pool", Y0[T * 512:(T + 1) * 512, 512:1024].rearrange("(j p) c -> p j c", p=128), ysb[s_].t[:], ysb[s_].ds, reads=[ysb[s_].r])

            if dbg == "b0" or lvl != 99:
                k.barrier()
                return nc, k
            with Phase():
                wo = sb("wo", [128, 8, D], BF16)
                k.dma("pool", wo.t[:], e_w_out.rearrange("(kc p) n -> p kc n", p=128), wo.ds, writes=[wo.r])
                yb_ = slots("yb_", [128, D], BF16, 2)
                yT = slots("yT", [128, 8, 128], BF16, 2)
                x1t = slots("x1t", [128, D], F32, 2)
                xv = x_in.rearrange("(n p) d -> n p d", p=128)
                x1v = x1.rearrange("(n p) d -> n p d", p=128)
                y0v = Y0.rearrange("(n p) d -> n p d", p=128)
                k.dma("sp", xt[0].t[:], xv[0], xt[0].ds, writes=[xt[0].r])
                k.dma("sp", yb_[0].t[:], y0v[0], yb_[0].ds, writes=[yb_[0].r])
                for blk in range(NB):
                    if blk + 1 < NB:
                        k.dma("sp", xt[(blk + 1) % 3].t[:], xv[blk + 1], xt[(blk + 1) % 3].ds, writes=[xt[(blk + 1) % 3].r])
                        k.dma("sp", yb_[(blk + 1) % 2].t[:], y0v[blk + 1], yb_[(blk + 1) % 2].ds, writes=[yb_[(blk + 1) % 2].r])
                    xb, yb, yTb, xo = xt[blk % 3], yb_[blk % 2], yT[blk % 2], x1t[blk % 2]
                    transpose8(lambda kc, yb=yb: yb.t[:, kc * 128:(kc + 1) * 128], yb.r, yTb.t[:], yTb.r, eng="act")
                    for cc in range(2):
                        b = cc
                        for kc in range(8):
                            k.op("pe", lambda kc=kc, b=b, cc=cc, yTb=yTb: nc.tensor.matmul(PSF[:, b, :], lhsT=yTb.t[:, kc, :], rhs=wo.t[:, kc, cc * 512:(cc + 1) * 512],
                                                                                       start=(kc == 0), stop=(kc == 7)),
                                 reads=[yTb.r, wo.r], writes=[bank[b]], signal=(kc == 7), same_engine=False)
                        k.op("dve", lambda b=b, cc=cc: nc.vector.tensor_tensor(out=junk.t[:, cc * 512:(cc + 1) * 512], in0=PSF[:, b, :], in1=gate0.t[:, cc * 512:(cc + 1) * 512], op=ALU.mult),
                             reads=[bank[b], gate0.r, junk.r], writes=[junk.r])
                        k.op("dve", lambda cc=cc, xo=xo, xb=xb: nc.vector.tensor_tensor(out=xo.t[:, cc * 512:(cc + 1) * 512], in0=junk.t[:, cc * 512:(cc + 1) * 512],
                                                                                   in1=xb.t[:, cc * 512:(cc + 1) * 512], op=ALU.add),
                             reads=[junk.r, xb.r, xo.r], writes=[xo.r])
                    k.dma("pool", x1v[blk], xo.t[:], xo.ds, reads=[xo.r])

        if dbg == "l0" or not L1:
            k.barrier()
            return nc, k

        sh1, gs1, gate1 = mod_vectors("1", o_w_mod, o_b_mod, o_norm_g)
        H1T = dram_scr("H1T", [8, 128, S], BF16)
        H1OT = dram_scr("H1OT", [8, 128, SO], BF16)
        X1O = dram_scr("X1O", [SO, D], F32)
        Y1T = dram_scr("Y1T", [16, 64, SO], BF16)
        om = sb("om", [128, 2], F32)
        k.dma("sp", om.t[:], ownm[:, :], om.ds, writes=[om.r])
        x1v = x1.rearrange("(n p) d -> n p d", p=128)
        x1ov = X1O.rearrange("(n p) d -> n p d", p=128)

        with Phase():
            st_h = slots("st_h", [128, 8, 512], BF16, 2)
            st_ho = slots("st_ho", [128, 8, 256], BF16, 2)
            hown = sb("hown", [128, D], BF16)
            hof = sb("hof", [128, D], F32)
            xo = slots("xo", [128, D], F32, 2)
            k.dma("sp", xt[0].t[:], x1v[0], xt[0].ds, writes=[xt[0].r])
            for T in range(NT):
                s_ = T % 2
                for j in range(4):
                    blk = T * 4 + j
                    if blk + 1 < NB:
                        nx = xt[(blk + 1) % 3]
                        k.dma("sp", nx.t[:], x1v[blk + 1], nx.ds, writes=[nx.r])
                    xb, hb = xt[blk % 3], htm[blk % 2]
                    norm_mod(xb, hb, gs1, sh1)
                    transpose8(lambda kc, hb=hb: hb.t[:, kc * 128:(kc + 1) * 128], hb.r, st_h[s_].t[:, :, j * 128:(j + 1) * 128], st_h[s_].r, eng="act")
                    if j % 2 == 1:
                        he, xe = htm[(blk - 1) % 2], xt[(blk - 1) % 3]
                        ob = blk // 2
                        k.op("dve", lambda he=he: nc.vector.tensor_scalar(out=hof.t[:], in0=he.t[:], scalar1=om.t[:, 0:1], scalar2=None, op0=ALU.mult),
                             reads=[he.r, om.r, hof.r], writes=[hof.r])
                        k.op("dve", lambda hb=hb: nc.vector.scalar_tensor_tensor(out=hown.t[:], in0=hb.t[:], scalar=om.t[:, 1:2], in1=hof.t[:], op0=ALU.mult, op1=ALU.add),
                             reads=[hb.r, om.r, hof.r, hown.r], writes=[hown.r])
                        transpose8(lambda kc: hown.t[:, kc * 128:(kc + 1) * 128], hown.r, st_ho[s_].t[:, :, (j // 2) * 128:(j // 2 + 1) * 128], st_ho[s_].r, eng="act")
                        xo_ = xo[ob % 2]
                        k.op("dve", lambda xe=xe: nc.vector.tensor_scalar(out=junk.t[:], in0=xe.t[:], scalar1=om.t[:, 0:1], scalar2=None, op0=ALU.mult),
                             reads=[xe.r, om.r, junk.r], writes=[junk.r])
                        k.op("dve", lambda xb=xb, xo_=xo_: nc.vector.scalar_tensor_tensor(out=xo_.t[:], in0=xb.t[:], scalar=om.t[:, 1:2], in1=junk.t[:], op0=ALU.mult, op1=ALU.add),
                             reads=[xb.r, om.r, junk.r, xo_.r], writes=[xo_.r])
                        k.dma("pool", x1ov[ob], xo_.t[:], xo_.ds, reads=[xo_.r])
                k.dma("pool", H1T[:, :, T * 512:(T + 1) * 512].rearrange("k p t -> p k t"), st_h[s_].t[:], st_h[s_].ds, reads=[st_h[s_].r])
                k.dma("pool", H1OT[:, :, T * 256:(T + 1) * 256].rearrange("k p t -> p k t"), st_ho[s_].t[:], st_ho[s_].ds, reads=[st_ho[s_].r])

        w1v = o_w_in.rearrange("(kc p) n -> p kc n", p=128)
        for g in range(4):
            with Phase():
                KTg = sb("KTg", [128, 2, S], BF16)
                Vg = sb("Vg", [128, NB, 256], BF16)
                QTg = sb("QTg", [128, 2, SO], BF16)
                SGT = sb("SGT", [64, 4, SO], BF16)
                with Phase():
                    wq = sb("wq", [128, 8, 256], BF16); wk = sb("wk", [128, 8, 256], BF16)
                    wv = sb("wv", [128, 8, 256], BF16); wg = sb("wg", [128, 8, 256], BF16)
                    for wt_, off in ((wq, 0), (wk, 1024), (wv, 2048), (wg, 3072)):
                        k.dma("pool", wt_.t[:], w1v[:, :, off + g * 256:off + (g + 1) * 256], wt_.ds, writes=[wt_.r])
                    h1t = slots("h1t", [128, 8, 512], BF16, 2)
                    ge1 = sb("ge1", [64, 512], F32)
                    k.dma("sp", h1t[0].t[:], H1T[:, :, 0:512].rearrange("k p t -> p k t"), h1t[0].ds, writes=[h1t[0].r])
                    bs = 0
                    for T in range(NT):
                        ht = h1t[T % 2]
                        if T + 1 < NT:
                            k.dma("sp", h1t[(T + 1) % 2].t[:], H1T[:, :, (T + 1) * 512:(T + 2) * 512].rearrange("k p t -> p k t"), h1t[(T + 1) % 2].ds, writes=[h1t[(T + 1) % 2].r])
                        for p_ in range(2):
                            b = bs; bs = (bs + 1) % 4
                            for kc in range(8):
                                k.op("pe", lambda kc=kc, b=b, p_=p_, ht=ht: nc.tensor.matmul(PSF[:, b, :], lhsT=wk.t[:, kc, p_ * 128:(p_ + 1) * 128], rhs=ht.t[:, kc, :], start=(kc == 0), stop=(kc == 7)),
                                     reads=[wk.r, ht.r], writes=[bank[b]], signal=(kc == 7), same_engine=False)
                            k.op("act", lambda b=b, p_=p_, T=T: nc.scalar.copy(out=KTg.t[:, p_, T * 512:(T + 1) * 512], in_=PSF[:, b, :]), reads=[bank[b]], writes=[KTg.r])
                        for j in range(4):
                            b = bs; bs = (bs + 1) % 4
                            for kc in range(8):
                                k.op("pe", lambda kc=kc, b=b, j=j, ht=ht: nc.tensor.matmul(PSF[:, b, 0:256], lhsT=ht.t[:, kc, j * 128:(j + 1) * 128], rhs=wv.t[:, kc, :], start=(kc == 0), stop=(kc == 7)),
                                     reads=[wv.r, ht.r], writes=[bank[b]], signal=(kc == 7), same_engine=False)
                            k.op("dve", lambda b=b, j=j, T=T: nc.vector.tensor_copy(out=Vg.t[:, T * 4 + j, :], in_=PSF[:, b, 0:256]), reads=[bank[b]], writes=[Vg.r])
                    k.dma("sp", h1t[0].t[:], H1OT[:, :, 0:512].rearrange("k p t -> p k t"), h1t[0].ds, writes=[h1t[0].r])
                    for TO in range(NTO):
                        ht = h1t[TO % 2]
                        if TO + 1 < NTO:
                            k.dma("sp", h1t[(TO + 1) % 2].t[:], H1OT[:, :, (TO + 1) * 512:(TO + 2) * 512].rearrange("k p t -> p k t"), h1t[(TO + 1) % 2].ds, writes=[h1t[(TO + 1) % 2].r])
                        for p_ in range(2):
                            b = bs; bs = (bs + 1) % 4
                            for kc in range(8):
                                k.op("pe", lambda kc=kc, b=b, p_=p_, ht=ht: nc.tensor.matmul(PSF[:, b, :], lhsT=wq.t[:, kc, p_ * 128:(p_ + 1) * 128], rhs=ht.t[:, kc, :], start=(kc == 0), stop=(kc == 7)),
                                     reads=[wq.r, ht.r], writes=[bank[b]], signal=(kc == 7), same_engine=False)
                            k.op("act", lambda b=b, p_=p_, TO=TO: nc.scalar.mul(out=QTg.t[:, p_, TO * 512:(TO + 1) * 512], in_=PSF[:, b, :], mul=0.125), reads=[bank[b]], writes=[QTg.r])
                        for hl in range(4):
                            b = bs; bs = (bs + 1) % 4
                            for kc in range(8):
                                k.op("pe", lambda kc=kc, b=b, hl=hl, ht=ht: nc.tensor.matmul(PSF[0:64, b, :], lhsT=wg.t[:, kc, hl * 64:(hl + 1) * 64], rhs=ht.t[:, kc, :], start=(kc == 0), stop=(kc == 7)),
                                     reads=[wg.r, ht.r], writes=[bank[b]], signal=(kc == 7), same_engine=False)
                            k.op("act", lambda b=b: nc.scalar.activation(out=ge1.t[:], in_=PSF[0:64, b, :], func=AF.Exp, scale=-1.0), reads=[bank[b]], writes=[ge1.r])
                            k.op("dve", lambda: nc.vector.tensor_scalar_add(out=ge1.t[:], in0=ge1.t[:], scalar1=1.0), reads=[ge1.r], writes=[ge1.r])
                            k.op("dve", lambda: nc.vector.reciprocal(out=ge1.t[:], in_=ge1.t[:]), reads=[ge1.r], writes=[ge1.r])
                            k.op("dve", lambda b=b, hl=hl, TO=TO: nc.vector.tensor_tensor(out=SGT.t[:, hl, TO * 512:(TO + 1) * 512], in0=ge1.t[:], in1=PSF[0:64, b, :], op=ALU.mult),
                                 reads=[ge1.r, bank[b]], writes=[SGT.r])
                with Phase():
                    dm = sb("dm", [128, 8, 512], BF16)
                    k.dma("pool", dm.t[:], dmask.rearrange("p (r q) -> p r q", r=8), dm.ds, writes=[dm.r])
                    e32 = slots("e32", [128, 2, 512], F32, 3)
                    sp_ = slots("sp_", [128, 2, 512], BF16, 3)
                    ex = [slots("ex%d" % i_, [128, 512], F32, 2) for i_ in range(2)]
                    ww = [slots("ww%d" % i_, [128, 512], BF16, 2) for i_ in range(2)]
                    yst1 = slots("yst1", [64, 2, 512], BF16, 2)
                    fin = 0
                    for m in range(NTO):
                        for p_ in range(2):
                            kbs = list(range(8 * m + 7, -1, -1))

                            def cst(kb):
                                r_ = kb - 8 * m
                                return 0 if r_ < 2 else ((r_ - 1 + 1) // 2) * 128

                            def stage1(kb, i):
                                sl = i % 3
                                c0 = cst(kb)
                                for hf in range(2):
                                    k.op("pe", lambda hf=hf, kb=kb, c0=c0: nc.tensor.matmul(PSF[:, hf, c0:512], lhsT=KTg.t[hf * 64:(hf + 1) * 64, p_, kb * 128:(kb + 1) * 128],
                                                                                   rhs=QTg.t[hf * 64:(hf + 1) * 64, p_, m * 512 + c0:(m + 1) * 512], start=True, stop=True),
                                         reads=[KTg.r, QTg.r], writes=[bank[hf]], signal=(hf == 1), same_engine=False)
                                k.op("act", lambda sl=sl, c0=c0: nc.scalar.activation(out=e32[sl].t[:, :, c0:512], in_=PSF[:, 0:2, c0:512], func=AF.Exp), reads=[bank[0], bank[1]], writes=[e32[sl].r])
                                k.op("act", lambda sl=sl, c0=c0: nc.scalar.activation(out=sp_[sl].t[:, :, c0:512], in_=e32[sl].t[:, :, c0:512], func=AF.Ln, bias=1.0), reads=[e32[sl].r], writes=[sp_[sl].r])
                                r_ = kb - 8 * m
                                if r_ >= 0:
                                    mk = dm.t[:, r_, c0:512].unsqueeze(1).to_broadcast([128, 2, 512 - c0])
                                    k.op("dve", lambda sl=sl, mk=mk, c0=c0: nc.vector.tensor_tensor(out=sp_[sl].t[:, :, c0:512], in0=sp_[sl].t[:, :, c0:512], in1=mk, op=ALU.mult), reads=[sp_[sl].r, dm.r], writes=[sp_[sl].r])
                                    k.op("dve", lambda sl=sl, mk=mk, c0=c0: nc.vector.tensor_tensor(out=e32[sl].t[:, :, c0:512], in0=e32[sl].t[:, :, c0:512], in1=mk, op=ALU.mult), reads=[e32[sl].r, dm.r], writes=[e32[sl].r])

                            def stage2a(kb, i, first, last):
                                sl, s2 = i % 3, i % 2
                                c0 = cst(kb)
                                for hf in range(2):
                                    k.op("pe", lambda hf=hf, sl=sl, c0=c0: nc.tensor.matmul(PSF[:, 2 + hf, c0:512], lhsT=negtri.t[:], rhs=sp_[sl].t[:, hf, c0:512], start=first, stop=True, skip_group_check=True),
                                         reads=[negtri.r, sp_[sl].r], writes=[bank[2 + hf]], same_engine=False)
                                for hf in range(2):
                                    k.op("act", lambda s2=s2, hf=hf, c0=c0: nc.scalar.activation(out=ex[hf][s2].t[:, c0:512], in_=PSF[:, 2 + hf, c0:512], func=AF.Exp), reads=[bank[2 + hf]], writes=[ex[hf][s2].r])
                                    k.op("dve", lambda sl=sl, s2=s2, hf=hf, c0=c0: nc.vector.tensor_tensor(out=ww[hf][s2].t[:, c0:512], in0=e32[sl].t[:, hf, c0:512], in1=ex[hf][s2].t[:, c0:512], op=ALU.mult),
                                         reads=[e32[sl].r, ex[hf][s2].r], writes=[ww[hf][s2].r])

                            def stage2b(kb, i, first, last):
                                sl, s2 = i % 3, i % 2
                                c0 = cst(kb)
                                if not last:
                                    for hf in range(2):
                                        k.op("pe", lambda hf=hf, sl=sl, c0=c0: nc.tensor.matmul(PSF[:, 2 + hf, c0:512], lhsT=negrest.t[:], rhs=sp_[sl].t[:, hf, c0:512], start=False, stop=True, skip_group_check=True),
                                             reads=[negrest.r, sp_[sl].r], writes=[bank[2 + hf]], same_engine=False)
                                for hf in range(2):
                                    k.op("pe", lambda hf=hf, s2=s2, kb=kb, c0=c0: nc.tensor.matmul(PSF[0:64, 4 + hf, c0:512], lhsT=Vg.t[:, kb, (p_ * 2 + hf) * 64:(p_ * 2 + hf + 1) * 64], rhs=ww[hf][s2].t[:, c0:512],
                                                                                          start=first, stop=last, skip_group_check=True),
                                         reads=[Vg.r, ww[hf][s2].r], writes=[bank[4 + hf]], same_engine=False)

                            stage1(kbs[0], 0)
                            if len(kbs) > 1:
                                stage1(kbs[1], 1)
                            for i, kb in enumerate(kbs):
                                stage2a(kb, i, i == 0, kb == 0)
                                if i + 2 < len(kbs):
                                    stage1(kbs[i + 2], i + 2)
                                stage2b(kb, i, i == 0, kb == 0)
                            ys = yst1[fin % 2]
                            fin += 1
                            for hf in range(2):
                                hl = p_ * 2 + hf
                                k.op("dve", lambda hf=hf, hl=hl, ys=ys: nc.vector.tensor_tensor(out=ys.t[:, hf, :], in0=PSF[0:64, 4 + hf, :], in1=SGT.t[:, hl, m * 512:(m + 1) * 512], op=ALU.mult),
                                     reads=[bank[4 + hf], SGT.r, ys.r], writes=[ys.r])
                            for hf in range(2):
                                k.dma("pool", Y1T[g * 4 + p_ * 2 + hf, :, m * 512:(m + 1) * 512], ys.t[:, hf, :], ys.ds, reads=[ys.r])

        with Phase():
            wo1 = sb("wo1", [64, 16, D], BF16)
            k.dma("pool", wo1.t[:], o_w_out.rearrange("(h p) n -> p h n", p=64), wo1.ds, writes=[wo1.r])
            yt1 = slots("yt1", [64, 16, 128], BF16, 2)
            outt = slots("outt", [128, D], F32, 2)
            outv = out_o.rearrange("(n p) d -> n p d", p=128)
            NOB = SO // 128
            k.dma("sp", xt[0].t[:], x1ov[0], xt[0].ds, writes=[xt[0].r])
            k.dma("sp", yt1[0].t[:], Y1T[:, :, 0:128].rearrange("h p t -> p h t"), yt1[0].ds, writes=[yt1[0].r])
            for ob in range(NOB):
                if ob + 1 < NOB:
                    k.dma("sp", xt[(ob + 1) % 3].t[:], x1ov[ob + 1], xt[(ob + 1) % 3].ds, writes=[xt[(ob + 1) % 3].r])
                    k.dma("sp", yt1[(ob + 1) % 2].t[:], Y1T[:, :, (ob + 1) * 128:(ob + 2) * 128].rearrange("h p t -> p h t"), yt1[(ob + 1) % 2].ds, writes=[yt1[(ob + 1) % 2].r])
                xb, yt, oo = xt[ob % 3], yt1[ob % 2], outt[ob % 2]
                for cc in range(2):
                    b = cc
                    for h in range(16):
                        k.op("pe", lambda h=h, b=b, cc=cc, yt=yt: nc.tensor.matmul(PSF[:, b, :], lhsT=yt.t[:, h, :], rhs=wo1.t[:, h, cc * 512:(cc + 1) * 512], start=(h == 0), stop=(h == 15)),
                             reads=[yt.r, wo1.r], writes=[bank[b]], signal=(h == 15), same_engine=False)
                    k.op("dve", lambda b=b, cc=cc: nc.vector.tensor_tensor(out=junk.t[:, cc * 512:(cc + 1) * 512], in0=PSF[:, b, :], in1=gate1.t[:, cc * 512:(cc + 1) * 512], op=ALU.mult),
                         reads=[bank[b], gate1.r, junk.r], writes=[junk.r])
                    k.op("dve", lambda cc=cc, oo=oo, xb=xb: nc.vector.tensor_tensor(out=oo.t[:, cc * 512:(cc + 1) * 512], in0=junk.t[:, cc * 512:(cc + 1) * 512], in1=xb.t[:, cc * 512:(cc + 1) * 512], op=ALU.add),
                         reads=[junk.r, xb.r, oo.r], writes=[oo.r])
                k.dma("pool", outv[ob], oo.t[:], oo.ds, reads=[oo.r])
        k.barrier()
    return nc, k


def _consts():
    s = np.arange(128)[:, None]
    t = np.arange(128)[None, :]
    ident = (s == t).astype(np.float32)
    tri = (s <= t).astype(np.float32)
    upp = (s > t).astype(np.float32)
    negtri = -(s >= t).astype(np.float32)
    inv = (10000.0 ** (-np.arange(32, dtype=np.float32) / np.float32(32))).astype(np.float32)
    return np.concatenate([ident, tri, upp, negtri, np.broadcast_to(inv[None, :], (128, 32))], axis=1).astype(np.float32)


def host_inputs(inp, b, hh, S, layers=(0, 1)):
    f = lambda a: np.ascontiguousarray(np.asarray(a), dtype=np.float32)
    m = {"cT": f(np.asarray(inp["c"])[b].reshape(8, 128).T), "consts": _consts()}
    if 0 in layers:
        m["x"] = f(np.asarray(inp["x"])[b, :S])
        m["posT"] = np.ascontiguousarray(np.asarray(inp["positions"])[b, :S].reshape(S // 128, 128).T.astype(np.int32))
        m["e_norm_g"] = f(inp["even_norm_g"]).reshape(1, D)
        m["e_w_mod"] = f(inp["even_w_mod"])[0]
        m["e_b_mod"] = f(inp["even_b_mod"]).reshape(1, 3 * D)
        w = f(inp["even_w_in"])[0]
        qa, ka, va, ga, qb, kb, vb, gb = np.split(w, np.cumsum([512, 512, 512, 512, 512, 128, 128])[:], axis=1)
        qbp = qb.reshape(D, 2, 4, 64).transpose(0, 2, 1, 3).reshape(D, 512)
        m["e_w_in"] = np.ascontiguousarray(np.concatenate([qa, ka, va, ga, qbp, gb, kb, vb], axis=1))
        m["e_w_out"] = f(inp["even_w_out"])[0]
        m["gains"] = np.concatenate([f(inp["a_q_gain"])[0], f(inp["a_k_gain"])[0], f(inp["b_q_gain"])[0], f(inp["b_k_gain"])[0]]).reshape(1, 256)
        m["lamv"] = np.concatenate([f(inp["a_lambda_q1"])[0], f(inp["a_lambda_k1"])[0], f(inp["a_lambda_q2"])[0], f(inp["a_lambda_k2"])[0]]).reshape(1, 256)
        m["subg"] = f(inp["a_subln_g"]).reshape(1, 128)
        m["sinks"] = f(inp["b_sinks"]).reshape(1, 8)
    if 1 in layers:
        m["o_norm_g"] = f(inp["odd_norm_g"]).reshape(1, D)
        m["o_w_mod"] = f(inp["odd_w_mod"])[0]
        m["o_b_mod"] = f(inp["odd_b_mod"]).reshape(1, 3 * D)
        m["o_w_in"] = f(inp["odd_w_in"])[0]
        m["o_w_out"] = f(inp["odd_w_out"])[0]
        m["ownm"] = np.ascontiguousarray(np.broadcast_to(np.array([[1.0 - hh, float(hh)]], np.float32), (128, 2)))
        s = np.arange(128)[:, None, None, None]
        r = np.arange(8)[None, :, None, None]
        jj = np.arange(4)[None, None, :, None]
        tq = np.arange(128)[None, None, None, :]
        g = 2 * jj + hh
        msk = ((r < g) | ((r == g) & (s < tq))).astype(np.float32)
        m["dmask"] = np.ascontiguousarray(msk.reshape(128, 8 * 512))
    return m


_CACHE = {}


def kernel(**inputs):
    S = 8192
    if "nc" not in _CACHE:
        _CACHE["nc"] = build(S=S, layers=(0, 1))[0]
    nc = _CACHE["nc"]
    in_maps = [host_inputs(inputs, c // 2, c % 2, S) for c in range(8)]
    res = run_bass_kernel_spmd(nc, in_maps, core_ids=list(range(8)))
    out = np.empty((4, S, D), np.float32)
    for c in range(8):
        b, hh = c // 2, c % 2
        o = np.asarray(res.results[c]["out"]).reshape(S // 256, 128, D)
        out[b].reshape(S // 256, 2, 128, D)[:, hh] = o
    return out
```
